# Optimizing a Trainium2 kernel written in Bass

```python
import math
import jax, jax.numpy as jnp
from jax import lax
import numpy as np

D_MODEL = 2048
BATCH = 4
SEQ = 8192
DEPTH = 1
DEC_BATCH = 8
DEC_SEQ = 4096
PAST_LEN = 128

D_POOL = D_MODEL // 2
POOL_WINDOWS = (2, 4, 8, 16)
N_POOL_GROUPS = len(POOL_WINDOWS)
POOL_GROUP = D_POOL // N_POOL_GROUPS
D_HYENA = D_MODEL // 2
HYENA_ORDER = 2
SHORT_CONV = 3
FILTER_EMB = 33
FILTER_BANDS = (FILTER_EMB - 1) // 2
FILTER_HIDDEN = 64
DECAY_TARGET = 1e-2
FAST_DECAY_PCT = 0.3
SLOW_DECAY_PCT = 1.5
D_FF = int(math.ceil(8 * D_MODEL / 3 / 256)) * 256
D_IN = D_POOL + (HYENA_ORDER + 1) * D_HYENA + 2 * D_MODEL
EPS = 1e-6

kernel_name = "gated_pool_hyena_encoder"


def _rmsnorm(x, g):
    xf = x.astype(jnp.float32)
    y = xf * lax.rsqrt(jnp.mean(xf * xf, axis=-1, keepdims=True) + EPS)
    return (y * g.astype(jnp.float32)).astype(x.dtype)


def _centred_mean_minus_self(u, w):
    L = u.shape[1]
    cs = jnp.concatenate([jnp.zeros_like(u[:, :1]), jnp.cumsum(u, axis=1)], axis=1)
    t = jnp.arange(L)
    lo = jnp.clip(t - w // 2, 0, L)
    hi = jnp.clip(t + (w - w // 2), 0, L)
    cnt = (hi - lo).astype(jnp.float32)
    return (cs[:, hi] - cs[:, lo]) / cnt[None, :, None] - u


def _pool_mixer(u, pool_w, pool_scale):
    B, L, _ = u.shape
    ug = u.astype(jnp.float32).reshape(B, L, N_POOL_GROUPS, POOL_GROUP)
    pooled = jnp.stack(
        [_centred_mean_minus_self(ug[:, :, g], w) for g, w in enumerate(POOL_WINDOWS)], axis=2
    )
    mixed = jnp.einsum("blgc,gcd->blgd", pooled.astype(u.dtype), pool_w)
    return mixed.reshape(B, L, D_POOL) * pool_scale


def _short_conv3(u, w, b):
    up = jnp.pad(u, ((0, 0), (1, 1), (0, 0)))
    return up[:, :-2] * w[0] + up[:, 1:-1] * w[1] + up[:, 2:] * w[2] + b


def _hyena_filters(L, w1, b1, f1, w2, b2, f2, w3):
    f32 = jnp.float32
    t = jnp.linspace(0.0, 1.0, L, dtype=f32)[:, None]
    wt = 2.0 * math.pi * jnp.arange(L, dtype=f32)[:, None] / L
    bands = jnp.linspace(1e-4, FILTER_BANDS - 1, FILTER_BANDS, dtype=f32)[None, :]
    z = jnp.concatenate([t, jnp.cos(bands * wt), -jnp.sin(bands * wt)], axis=-1)
    h = jnp.sin(f1.astype(f32) * (z @ w1.astype(f32) + b1.astype(f32)))
    h = jnp.sin(f2.astype(f32) * (h @ w2.astype(f32) + b2.astype(f32)))
    h = (h @ w3.astype(f32)).reshape(L, HYENA_ORDER, 2, D_HYENA)
    max_decay = math.log(DECAY_TARGET) / FAST_DECAY_PCT
    min_decay = math.log(DECAY_TARGET) / SLOW_DECAY_PCT
    deltas = jnp.abs(jnp.linspace(min_decay, max_decay, D_HYENA, dtype=f32))
    h = h * jnp.exp(-t * deltas)[:, None, None, :]
    fwd, bwd = h[:, :, 0], h[:, :, 1]
    two = jnp.concatenate(
        [fwd[:1] + bwd[:1], fwd[1:], jnp.zeros_like(fwd[:1]), bwd[:0:-1]], axis=0
    )
    two = two * lax.rsqrt(jnp.sum(two * two, axis=0, keepdims=True) + EPS)
    return jnp.moveaxis(two, 1, 0)


def _long_conv(z, filt_f):
    L = z.shape[1]
    zf = jnp.fft.rfft(z, n=2 * L, axis=1)
    return jnp.fft.irfft(zf * filt_f[None], n=2 * L, axis=1)[:, :L]


def _hyena_mixer(u, conv_w, conv_b, w1, b1, f1, w2, b2, f2, w3, hyena_bias):
    dt = u.dtype
    L = u.shape[1]
    uc = _short_conv3(u, conv_w, conv_b).astype(jnp.float32)
    v, x1, x2 = jnp.split(uc, HYENA_ORDER + 1, axis=-1)
    filt_f = jnp.fft.rfft(_hyena_filters(L, w1, b1, f1, w2, b2, f2, w3), axis=1)
    bias = hyena_bias.astype(jnp.float32)
    z = v
    for o, gate in enumerate((x1, x2)):
        z = gate * (_long_conv(z, filt_f[o]) + bias[o] * z)
    return z.astype(dt)


def _trunk(x, g_mix, w_in, pool_w, pool_scale, conv_w, conv_b, filt_w1, filt_b1, filt_freq1,
           filt_w2, filt_b2, filt_freq2, filt_w3, hyena_bias, w_branch_a, w_branch_b, w_out,
           g_ffn, w_gate, w_up, w_down, g_final):
    s1 = D_POOL
    s2 = s1 + (HYENA_ORDER + 1) * D_HYENA
    s3 = s2 + D_MODEL
    for i in range(DEPTH):
        h = _rmsnorm(x, g_mix[i])
        p = h @ w_in[i]
        u_pool, u_hy, gate_a, gate_b = p[..., :s1], p[..., s1:s2], p[..., s2:s3], p[..., s3:]
        a = _pool_mixer(u_pool, pool_w[i], pool_scale[i])
        b = _hyena_mixer(u_hy, conv_w[i], conv_b[i], filt_w1[i], filt_b1[i], filt_freq1[i],
                         filt_w2[i], filt_b2[i], filt_freq2[i], filt_w3[i], hyena_bias[i])
        merged = (jax.nn.sigmoid(gate_a) * (a @ w_branch_a[i])
                  + jax.nn.sigmoid(gate_b) * (b @ w_branch_b[i]))
        x = x + merged @ w_out[i]
        h = _rmsnorm(x, g_ffn[i])
        x = x + (jax.nn.silu(h @ w_gate[i]) * (h @ w_up[i])) @ w_down[i]
    return _rmsnorm(x, g_final)


def setup_inputs(seed: int = 0) -> dict:
    key = jax.random.key(seed)
    ks = jax.random.split(key, 24)
    f32 = jnp.float32

    def nrm(k, shape, scale):
        return jax.random.normal(k, shape, f32) * scale

    return {
        "x_prompt": nrm(ks[0], (BATCH, SEQ, D_MODEL), 1.0),
        "x_sample": nrm(ks[1], (DEC_BATCH, DEC_SEQ, D_MODEL), 1.0),
        "g_mix": 1.0 + nrm(ks[2], (DEPTH, D_MODEL), 0.02),
        "w_in": nrm(ks[3], (DEPTH, D_MODEL, D_IN), D_MODEL ** -0.5),
        "pool_w": nrm(ks[4], (DEPTH, N_POOL_GROUPS, POOL_GROUP, POOL_GROUP), POOL_GROUP ** -0.5),
        "pool_scale": 1.0 + nrm(ks[5], (DEPTH, D_POOL), 0.02),
        "conv_w": nrm(ks[6], (DEPTH, SHORT_CONV, (HYENA_ORDER + 1) * D_HYENA), SHORT_CONV ** -0.5),
        "conv_b": nrm(ks[7], (DEPTH, (HYENA_ORDER + 1) * D_HYENA), 0.02),
        "filt_w1": nrm(ks[8], (DEPTH, FILTER_EMB, FILTER_HIDDEN), FILTER_EMB ** -0.5),
        "filt_b1": nrm(ks[9], (DEPTH, FILTER_HIDDEN), 0.02),
        "filt_freq1": 1.0 + nrm(ks[10], (DEPTH, FILTER_HIDDEN), 0.02),
        "filt_w2": nrm(ks[11], (DEPTH, FILTER_HIDDEN, FILTER_HIDDEN), FILTER_HIDDEN ** -0.5),
        "filt_b2": nrm(ks[12], (DEPTH, FILTER_HIDDEN), 0.02),
        "filt_freq2": 1.0 + nrm(ks[13], (DEPTH, FILTER_HIDDEN), 0.02),
        "filt_w3": nrm(ks[14], (DEPTH, FILTER_HIDDEN, HYENA_ORDER * 2 * D_HYENA), FILTER_HIDDEN ** -0.5),
        "hyena_bias": nrm(ks[15], (DEPTH, HYENA_ORDER, D_HYENA), 1.0),
        "w_branch_a": nrm(ks[16], (DEPTH, D_POOL, D_MODEL), D_POOL ** -0.5),
        "w_branch_b": nrm(ks[17], (DEPTH, D_HYENA, D_MODEL), D_HYENA ** -0.5),
        "w_out": nrm(ks[18], (DEPTH, D_MODEL, D_MODEL), D_MODEL ** -0.5),
        "g_ffn": 1.0 + nrm(ks[19], (DEPTH, D_MODEL), 0.02),
        "w_gate": nrm(ks[20], (DEPTH, D_MODEL, D_FF), D_MODEL ** -0.5),
        "w_up": nrm(ks[21], (DEPTH, D_MODEL, D_FF), D_MODEL ** -0.5),
        "w_down": nrm(ks[22], (DEPTH, D_FF, D_MODEL), D_FF ** -0.5),
        "g_final": 1.0 + nrm(ks[23], (D_MODEL,), 0.02),
    }


def reference(x_prompt, x_sample, g_mix, w_in, pool_w, pool_scale, conv_w, conv_b, filt_w1, filt_b1,
              filt_freq1, filt_w2, filt_b2, filt_freq2, filt_w3, hyena_bias, w_branch_a, w_branch_b,
              w_out, g_ffn, w_gate, w_up, w_down, g_final):
    y_prompt = _trunk(x_prompt, g_mix, w_in, pool_w, pool_scale, conv_w, conv_b, filt_w1, filt_b1,
                      filt_freq1, filt_w2, filt_b2, filt_freq2, filt_w3, hyena_bias, w_branch_a,
                      w_branch_b, w_out, g_ffn, w_gate, w_up, w_down, g_final)
    y_sample = _trunk(x_sample, g_mix, w_in, pool_w, pool_scale, conv_w, conv_b, filt_w1, filt_b1,
                      filt_freq1, filt_w2, filt_b2, filt_freq2, filt_w3, hyena_bias, w_branch_a,
                      w_branch_b, w_out, g_ffn, w_gate, w_up, w_down, g_final)
    return (y_prompt, y_sample)
```

```python
import math
from contextlib import ExitStack

import numpy as np
import ml_dtypes

import concourse.bass as bass
import concourse.mybir as mybir
from concourse.bass_utils import run_bass_kernel_spmd

F32 = mybir.dt.float32
BF16 = mybir.dt.bfloat16
AF = mybir.ActivationFunctionType
ALU = mybir.AluOpType
AX = mybir.AxisListType

D = 2048
DP = 1024
DH = 1024
DIN = 8192
DFF = 5632
NTOK = 8192
T = 512
NT = NTOK // T
EPS = 1e-6
NCORES = 8

ENGS = ("pe", "act", "dve", "pool", "sp")
NS = 12


class Op:
    __slots__ = ("eng", "fn", "deps", "signal", "dma", "sem", "val", "qidx", "name")

    def __init__(self, eng, fn, dma, name=None):
        self.eng = eng
        self.fn = fn
        self.dma = dma
        self.deps = []
        self.signal = False
        self.sem = None
        self.val = None
        self.qidx = None
        self.name = name


class Sched:
    def __init__(self):
        self.ops = {e: [] for e in ENGS}
        self.lastw = {}
        self.readers = {}
        self.qcount = {e: 0 for e in ENGS}
        self.qops = {e: [] for e in ENGS}

    def op(self, eng, fn, reads=(), writes=(), dma=False, name=None):
        o = Op(eng, fn, dma, name)
        deps = {}

        def add(d):
            if d is None or d is o:
                return
            if (not d.dma) and (not dma) and d.eng == eng and eng == "pe":
                return
            deps[id(d)] = d

        for b in reads:
            add(self.lastw.get(b))
        for b in writes:
            add(self.lastw.get(b))
            rd = self.readers.get(b)
            if rd:
                for r in rd.values():
                    add(r)
        for b in writes:
            self.lastw[b] = o
            self.readers[b] = {}
        for b in reads:
            rd = self.readers.setdefault(b, {})
            if dma:
                rd[("dma", id(o))] = o
            else:
                rd[eng] = o
        if dma:
            i = self.qcount[eng]
            o.qidx = i
            self.qcount[eng] = i + 1
            if i >= NS:
                prev = self.qops[eng][i - NS]
                deps[id(prev)] = prev
            self.qops[eng].append(o)
        o.deps = list(deps.values())
        for d in o.deps:
            d.signal = True
        self.ops[eng].append(o)
        return o

    def barrier(self):
        tails = []
        for e in ENGS:
            comp = [o for o in self.ops[e] if not o.dma and o.fn is not None]
            if comp:
                tails.append(comp[-1])
            tails.extend(self.qops[e][-NS:])
        for t in tails:
            t.signal = True
        for e in ENGS:
            o = Op(e, None, False, "barrier")
            o.deps = list(tails)
            self.ops[e].append(o)
        self.lastw = {}
        self.readers = {}

    def wait_all(self, eng, ops):
        o = Op(eng, None, False, "waitall")
        o.deps = list(ops)
        for d in o.deps:
            d.signal = True
        self.ops[eng].append(o)

    def emit(self, block, sems, dsems):
        for e in ENGS:
            c = 0
            for o in self.ops[e]:
                if o.dma:
                    o.sem = dsems[e][o.qidx % NS]
                    o.val = 16 * (o.qidx // NS + 1)
                elif o.signal and o.fn is not None:
                    c += 1
                    o.sem = sems[e]
                    o.val = c
            assert c < 60000, (e, c)
        handles = {"pe": block.tensor, "act": block.scalar, "dve": block.vector,
                   "pool": block.gpsimd, "sp": block.sync}
        for e in ENGS:
            ops = self.ops[e]

            def body(h, ops=ops):
                waited = {}
                for o in ops:
                    for d in o.deps:
                        k = id(d.sem)
                        if waited.get(k, 0) < d.val:
                            h.wait_ge(d.sem, d.val)
                            waited[k] = d.val
                    if o.fn is None:
                        continue
                    ins = o.fn(h)
                    if o.dma:
                        ins.then_inc(o.sem, 16)
                    elif o.signal:
                        ins.then_inc(o.sem, 1)

            handles[e](body)


def _bf(a):
    return np.ascontiguousarray(a.astype(ml_dtypes.bfloat16))


def make_constants(is_prompt):
    c = {}
    if is_prompt:
        L, nseq = 8192, 1
    else:
        L, nseq = 4096, 2
    N = 2 * L
    J = L // 128
    N1 = N // 128
    K1 = N1 // 2
    A = np.zeros((64, 2, 64), np.float64)
    for sq in range(nseq):
        j = np.arange(J)[:, None]
        k1 = np.arange(K1)[None, :]
        ang = -2 * np.pi * j * (k1 + 0.5) / N1
        A[sq * J:(sq + 1) * J, 0, sq * K1:(sq + 1) * K1] = np.cos(ang)
        A[sq * J:(sq + 1) * J, 1, sq * K1:(sq + 1) * K1] = np.sin(ang)
    c["dftA"] = _bf(A.reshape(64, 128))
    p = np.arange(128)[:, None]
    k2 = np.arange(128)[None, :]
    MB = np.zeros((128, 64, 3, 128), np.float64)
    for kc in range(64):
        k1p = kc % K1
        ang = -2 * np.pi * p * (k1p + N1 * k2 + 0.5) / N
        MB[:, kc, 0, :] = np.cos(ang)
        MB[:, kc, 1, :] = np.sin(ang)
        MB[:, kc, 2, :] = -np.sin(ang)
    c["dftB"] = _bf(MB.reshape(128, 64 * 3 * 128))
    ang = 2 * np.pi * np.arange(128)[:, None] * np.arange(128)[None, :] / 128
    c["dftF"] = _bf(np.concatenate([np.cos(ang), np.sin(ang)], axis=1))
    G1 = np.zeros((128, 128, 64), np.float64)
    G2 = np.zeros((128, 128, 64), np.float64)
    for sq in range(nseq):
        k1 = np.arange(K1)[:, None, None]
        pp = np.arange(128)[None, :, None]
        jj = np.arange(J)[None, None, :]
        ang = 2 * np.pi * (k1 + 0.5) * (pp + 128 * jj) / N
        Gr = np.cos(ang) * (2.0 / N)
        Gi = np.sin(ang) * (2.0 / N)
        r0 = sq * K1
        j0 = sq * J
        G1[r0:r0 + K1, :, j0:j0 + J] = Gr
        G1[64 + r0:64 + r0 + K1, :, j0:j0 + J] = -Gi
        G2[r0:r0 + K1, :, j0:j0 + J] = -Gi
        G2[64 + r0:64 + r0 + K1, :, j0:j0 + J] = -Gr
    c["dftG"] = _bf(np.stack([G1, G2], axis=2).reshape(128, 128 * 2 * 64))
    t = np.linspace(0.0, 1.0, L, dtype=np.float32)
    wt = (2.0 * np.float32(math.pi) * np.arange(L, dtype=np.float32) / np.float32(L)).astype(np.float32)
    bands = np.linspace(1e-4, 15, 16, dtype=np.float32)[None, :]
    z = np.concatenate([t[:, None], np.cos(bands * wt[:, None]), -np.sin(bands * wt[:, None])], axis=-1)
    z = np.tile(z.astype(np.float32), (nseq, 1))
    c["zfeatT"] = np.ascontiguousarray(z.T)
    c["tvals"] = np.ascontiguousarray(np.tile(t, nseq)[None, :])
    inv = np.zeros((4, NTOK), np.float32)
    tt = np.arange(L)
    for g, w in enumerate((2, 4, 8, 16)):
        lo = np.clip(tt - w // 2, 0, L)
        hi = np.clip(tt + (w - w // 2), 0, L)
        inv[g] = np.tile(1.0 / (hi - lo).astype(np.float32), nseq)
    c["invcnt"] = inv
    hm = np.ones((NT, 2), np.float32)
    for i in range(NT):
        t0 = i * T
        if t0 % L == 0:
            hm[i, 0] = 0.0
        if (t0 + T) % L == 0:
            hm[i, 1] = 0.0
    c["halomask"] = np.ascontiguousarray(np.broadcast_to(hm.reshape(1, NT * 2), (128, NT * 2)))
    c["ident"] = _bf(np.eye(128))
    misc = np.zeros((128, 16), np.float32)
    misc[:, 0] = 1.0 if is_prompt else 0.5
    max_decay = math.log(1e-2) / 0.3
    min_decay = math.log(1e-2) / 1.5
    deltas = np.abs(np.linspace(min_decay, max_decay, DH, dtype=np.float32))
    misc[:, 1:9] = -deltas.reshape(8, 128).T
    misc[:, 9] = 0.0 if is_prompt else 1.0
    c["misc"] = misc
    return c


CONST_SPECS = {
    "dftA": ([64, 128], BF16), "dftB": ([128, 64 * 3 * 128], BF16), "dftF": ([128, 256], BF16),
    "dftG": ([128, 128 * 2 * 64], BF16), "zfeatT": ([33, NTOK], F32), "tvals": ([1, NTOK], F32),
    "invcnt": ([4, NTOK], F32), "halomask": ([128, NT * 2], F32), "ident": ([128, 128], BF16),
    "misc": ([128, 16], F32),
}

WEIGHT_SPECS = [
    ("g_mix", [1, D]), ("w_in", [D, DIN]), ("pool_w", [4 * 256, 256]), ("pool_scale", [1, DP]),
    ("conv_w", [3, 3 * DH]), ("conv_b", [1, 3 * DH]), ("filt_w1", [33, 64]), ("filt_b1", [1, 64]),
    ("filt_freq1", [1, 64]), ("filt_w2", [64, 64]), ("filt_b2", [1, 64]), ("filt_freq2", [1, 64]),
    ("filt_w3", [64, 4 * DH]), ("hyena_bias", [2, DH]), ("w_branch_a", [DP, D]),
    ("w_branch_b", [DH, D]), ("w_out", [D, D]), ("g_ffn", [1, D]), ("w_gate", [D, DFF]),
    ("w_up", [D, DFF]), ("w_down", [DFF, D]), ("g_final", [1, D]),
]


class Builder:
    def __init__(self, debug=(), nt=NT, ncg=16):
        self.debug = set(debug)
        self.nt = nt
        self.ncg = ncg
        self._phase_id = 0
        self.nc = bass.Bass("TRN2", target_bir_lowering=False)
        self.s = Sched()
        nc = self.nc
        self.x = nc.dram_tensor("x", [NTOK, D], F32, kind="ExternalInput").ap()
        self.y = nc.dram_tensor("y", [NTOK, D], F32, kind="ExternalOutput").ap()
        self.w = {}
        for name, shp in WEIGHT_SPECS:
            self.w[name] = nc.dram_tensor(name, shp, F32, kind="ExternalInput").ap()
        self.c = {}
        for name, (shp, dt) in CONST_SPECS.items():
            self.c[name] = nc.dram_tensor(name, shp, dt, kind="ExternalInput").ap()
        self.scr = {}

    def _uid(self):
        return self._phase_id

    def scratch(self, name, shape, dt):
        kind = "ExternalOutput" if name in self.debug else "Internal"
        t = self.nc.dram_tensor(name, shape, dt, kind=kind).ap()
        self.scr[name] = t
        return t

    def build(self, phases):
        nc = self.nc
        s = self.s
        with ExitStack() as es:
            sems = {e: es.enter_context(nc.semaphore("s_" + e)) for e in ENGS}
            dsems = {e: [es.enter_context(nc.semaphore(f"d_{e}_{i}")) for i in range(NS)]
                     for e in ENGS}
            self.ps = [es.enter_context(nc.psum_tensor(f"ps{i}", [128, 512], F32)) for i in range(8)]
            self.es = es
            self.scratch("WB_in", [D, DIN], BF16)
            self.scratch("WB_a", [DP, D], BF16)
            self.scratch("WB_b", [DH, D], BF16)
            self.scratch("WB_out", [D, D], BF16)
            self.scratch("WB_gate", [D, DFF], BF16)
            self.scratch("WB_up", [D, DFF], BF16)
            self.scratch("WB_down", [DFF, D], BF16)
            self.scratch("PU", [DP, NTOK], F32)
            self.scratch("HV", [3 * DH, NTOK], F32)
            self.scratch("SG", [2 * D, NTOK], BF16)
            self.out_dmas = []
            if "0" in phases:
                self.phase0()
            self.scratch("AT", [DP, NTOK], BF16)
            self.scratch("HC", [3 * DH, NTOK], BF16)
            self.scratch("FT", [4 * DH, NTOK], BF16)
            self.scratch("HS", [2, 16, 128, 2 * 64 * 64], BF16)
            self.scratch("BT", [DH, NTOK], BF16)
            if "dbgS" in self.debug:
                for nm in ("dbgS", "dbgY", "dbgP0", "dbgP1"):
                    self.debug.add(nm)
                    self.scratch(nm, [128, 8192], BF16)
                self.debug.add("dbgZ1")
                self.scratch("dbgZ1", [64, 8192], BF16)
            if "B" in phases:
                self.phase1()
            if "F" in phases:
                self.phase2a()
            if "H" in phases:
                self.phase2b()
            if "C" in phases:
                self.phase2c()
            if "D" in phases:
                self.phase3()
            s.barrier()
            s.wait_all("sp", self.out_dmas)
            with nc.Block() as block:
                s.emit(block, sems, dsems)
        return nc

    def phase0(self):
        s = self.s
        for wn, sn in (("w_in", "WB_in"), ("w_branch_a", "WB_a"), ("w_branch_b", "WB_b"), ("w_out", "WB_out"),
                       ("w_gate", "WB_gate"), ("w_up", "WB_up"), ("w_down", "WB_down")):
            src = self.w[wn]
            dst = self.scr[sn]
            for r in range(src.shape[0] // 128):
                s.op("pool", lambda h, r=r, src=src, dst=dst: h.dma_start(
                    out=dst[r * 128:(r + 1) * 128, :], in_=src[r * 128:(r + 1) * 128, :]),
                    reads=[], writes=[(sn, r)], dma=True)

    def phase1(self):
        self._phase_id += 1
        nc, s = self.nc, self.s
        with ExitStack() as ph:
            def sb(name, shape, dt):
                return ph.enter_context(nc.sbuf_tensor(f"sb{self._uid()}_" + name, shape, dt))
            self._p1_alloc(sb)
            for tt in range(self.nt):
                self._p1_B(tt, self._p1_Cfront_steps(tt - 2) if tt >= 2 else ())
                if tt >= 2:
                    self._p1_Cback(tt - 2)
            for tt in range(max(self.nt - 2, 0), self.nt):
                self._p1_Cfront(tt)
                self._p1_Cback(tt)
            s.barrier()

    def _p1_alloc(self, sb):
        s = self.s
        self.gbc = sb("gbc", [128, D], F32)
        self.ident = sb("ident", [128, 128], BF16)
        self.xt = [sb(f"xt{i}", [128, D], F32) for i in range(4)]
        self.hb = [sb(f"hb{i}", [128, D], BF16) for i in range(2)]
        self.st = [sb(f"st{i}", [128, 4], F32) for i in range(2)]
        self.hT = [sb(f"hT{i}", [128, 16, T], BF16) for i in range(2)]
        self.wb = [sb(f"wb{i}", [128, 16, 512], BF16) for i in range(3)]
        self.ev32 = [sb(f"ev32_{i}", [128, 512], F32) for i in range(4)]
        self.ev16 = [sb(f"ev16_{i}", [128, 512], BF16) for i in range(4)]
        self.CV = [sb(f"CV{i}", [128, T + 16], F32) for i in range(3)]
        self.lv = [sb(f"lv{i}", [128, T + 16], F32) for i in range(4)]
        self.pooled = [sb(f"pooled{i}", [128, T], BF16) for i in range(16)]
        self.icnt = [sb(f"icnt{i}", [128, 4, T], F32) for i in range(2)]
        self.hmask = sb("hmask", [128, NT * 2], F32)
        self.poolw = sb("poolw", [128, 8, 256], BF16)
        self.pscale = sb("pscale", [128, 8], F32)
        self.aev = [sb(f"aev{i}", [128, T], BF16) for i in range(2)]
        self.cacc = [sb(f"cacc{i}", [128, T], F32) for i in range(3)]
        self.cout = [sb(f"cout{i}", [128, T], BF16) for i in range(3)]
        self.convw = sb("convw", [128, 3, 24], F32)
        self.convb = sb("convb", [128, 24], F32)
        self.cnt = {"ev": 0, "sub": 0, "u": 0, "pl": 0, "ae": 0, "v": 0, "cv": 0}
        nc = self.nc
        s.op("sp", lambda h: h.dma_start(out=self.gbc[:, :], in_=self.w["g_mix"].partition_broadcast(128)),
             writes=["gbc"], dma=True)
        s.op("sp", lambda h: h.dma_start(out=self.ident[:, :], in_=self.c["ident"]), writes=["ident"], dma=True)
        s.op("sp", lambda h: h.dma_start(out=self.hmask[:, :], in_=self.c["halomask"]), writes=["hmask"], dma=True)
        s.op("pool", lambda h: h.dma_start(out=self.poolw[:, :, :],
                                           in_=self.w["pool_w"].rearrange("(a p) n -> p a n", p=128)),
             writes=["poolw"], dma=True)
        s.op("sp", lambda h: h.dma_start(out=self.pscale[:, :],
                                         in_=self.w["pool_scale"].rearrange("o (c p) -> p (o c)", p=128),
                                         allow_slow_non_contiguous=True),
             writes=["pscale"], dma=True)
        for k in range(3):
            s.op("sp", lambda h, k=k: h.dma_start(
                out=self.convw[:, k, :],
                in_=self.w["conv_w"][k:k + 1, :].rearrange("o (c p) -> p (o c)", p=128),
                allow_slow_non_contiguous=True),
                writes=["convw"], dma=True)
        s.op("sp", lambda h: h.dma_start(out=self.convb[:, :],
                                         in_=self.w["conv_b"].rearrange("o (c p) -> p (o c)", p=128),
                                         allow_slow_non_contiguous=True),
             writes=["convb"], dma=True)

    def _p1_xload(self, tt, sub):
        s = self.s
        g = tt * 4 + sub
        xs = g % 4
        t0 = tt * T + sub * 128
        s.op("sp", lambda h: h.dma_start(out=self.xt[xs][:, :], in_=self.x[t0:t0 + 128, :]),
             writes=[("xt", xs)], dma=True)

    def _p1_N(self, tt, sub):
        s, ps = self.s, self.ps
        xt, hb, st, hT, gbc, ident = self.xt, self.hb, self.st, self.hT, self.gbc, self.ident
        psT = [ps[6][:, :].bitcast(BF16), ps[7][:, :].bitcast(BF16)]
        hs = tt % 2
        xq = (tt * 4 + sub) % 4
        xs = (tt * 4 + sub) % 2
        s.op("act", lambda h: h.activation(hb[xs][:, :], xt[xq][:, :], AF.Square, accum_out=st[xs][:, 0:1]),
             reads=[("xt", xq)], writes=[("hb", xs), ("st", xs)])
        s.op("act", lambda h: h.activation(st[xs][:, 1:2], st[xs][:, 0:1], AF.Sqrt, bias=EPS, scale=1.0 / D),
             reads=[("st", xs)], writes=[("st1", xs)])
        s.op("dve", lambda h: h.reciprocal(st[xs][:, 2:3], st[xs][:, 1:2]),
             reads=[("st1", xs)], writes=[("st2", xs)])
        s.op("dve", lambda h: h.scalar_tensor_tensor(
            hb[xs][:, :], xt[xq][:, :], st[xs][:, 2:3], gbc[:, :], ALU.mult, ALU.mult),
            reads=[("xt", xq), ("st2", xs), "gbc"], writes=[("hb", xs)])
        for half in range(2):
            for q in range(8):
                dc = half * 8 + q
                s.op("pe", lambda h, dc=dc, half=half, q=q: h.transpose(
                    psT[half][:, q * 128:(q + 1) * 128], hb[xs][:, dc * 128:(dc + 1) * 128], ident[:, :]),
                    reads=[("hb", xs), "ident"], writes=[("ps", 6 + half)])
            dstv = hT[hs][:, half * 8:(half + 1) * 8, sub * 128:(sub + 1) * 128]
            srcv = psT[half].rearrange("p (q t) -> p q t", q=8)
            s.op("dve", lambda h, dstv=dstv, srcv=srcv: h.tensor_copy(dstv, srcv),
                 reads=[("ps", 6 + half)], writes=[("hT", hs, sub)])

    def _p1_wload(self, w):
        s = self.s
        if w >= self.nt * 16:
            return
        ob = w % 16
        ws = w % 3
        WB = self.scr["WB_in"].rearrange("(dc p) n -> p dc n", p=128)
        s.op("sp", lambda h: h.dma_start(out=self.wb[ws][:, :, :], in_=WB[:, :, ob * 512:(ob + 1) * 512]),
             reads=[("WB_in", r) for r in range(16)], writes=[("wb", ws)], dma=True)

    def _p1_B(self, tt, csteps=()):
        s = self.s
        csteps = list(csteps)
        hT, wb, ev32, ev16, ps = self.hT, self.wb, self.ev32, self.ev16, self.ps
        hs = tt % 2
        if tt == 0:
            self._p1_wload(0)
            self._p1_wload(1)
            for sub in range(4):
                self._p1_xload(0, sub)
            for sub in range(4):
                self._p1_N(0, sub)
        for ob in range(DIN // 512):
            w = tt * 16 + ob
            ws = w % 3
            self._p1_wload(w + 2)
            nsteps = 3 if ob == 0 else 2
            for _ in range(nsteps):
                if csteps:
                    csteps.pop(0)()
            if ob == 15:
                while csteps:
                    csteps.pop(0)()
            if tt + 1 < self.nt:
                for sub in range(4):
                    if ob == sub:
                        self._p1_xload(tt + 1, sub)
                    if ob == 2 + 3 * sub:
                        self._p1_N(tt + 1, sub)
            for oc in range(4):
                col0 = ob * 512 + oc * 128
                ev_i = self.cnt["ev"]
                self.cnt["ev"] += 1
                bank = ev_i % 4
                for dc in range(16):
                    s.op("pe", lambda h, ws=ws, oc=oc, dc=dc, bank=bank, hs=hs: h.matmul(
                        ps[bank][:, :], wb[ws][:, dc, oc * 128:(oc + 1) * 128], hT[hs][:, dc, :],
                        start=(dc == 0), stop=(dc == 15)),
                        reads=[("wb", ws)] + [("hT", hs, q) for q in range(4)],
                        writes=[("ps", bank)])
                e = ev_i % 4
                if col0 < DP + 3 * DH:
                    s.op("act", lambda h, e=e, bank=bank: h.copy(ev32[e][:, :], ps[bank][:, :]),
                         reads=[("ps", bank)], writes=[("ev32", e)])
                    if col0 < DP:
                        dst = self.scr["PU"][col0:col0 + 128, tt * T:(tt + 1) * T]
                        key = ("PU", col0 // 128, tt)
                    else:
                        r0 = col0 - DP
                        dst = self.scr["HV"][r0:r0 + 128, tt * T:(tt + 1) * T]
                        key = ("HV", r0 // 128, tt)
                    s.op("act", lambda h, e=e, dst=dst: h.dma_start(out=dst, in_=ev32[e][:, :]),
                         reads=[("ev32", e)], writes=[key], dma=True)
                else:
                    r0 = col0 - (DP + 3 * DH)
                    s.op("act", lambda h, e=e, bank=bank: h.activation(ev16[e][:, :], ps[bank][:, :], AF.Sigmoid),
                         reads=[("ps", bank)], writes=[("ev16", e)])
                    dst = self.scr["SG"][r0:r0 + 128, tt * T:(tt + 1) * T]
                    s.op("act", lambda h, e=e, dst=dst: h.dma_start(out=dst, in_=ev16[e][:, :]),
                         reads=[("ev16", e)], writes=[("SG", r0 // 128, tt)], dma=True)

    def _load_halo(self, buf, bufkey, src, srckey, row0, tt, halo, eng_memset="pool"):
        s = self.s
        nt = self.nt
        lo = tt * T - halo
        hi = tt * T + T + halo
        clo, chi = max(lo, 0), min(hi, nt * T)
        if clo > lo:
            s.op(eng_memset, lambda h: h.memset(buf[:, 0:halo], 0.0), writes=[bufkey])
        if chi < hi:
            s.op(eng_memset, lambda h: h.memset(buf[:, T + halo:T + 2 * halo], 0.0), writes=[bufkey])
        rk = [(srckey, row0 // 128, q) for q in (tt - 1, tt, tt + 1) if 0 <= q < nt]
        s.op("pool", lambda h: h.dma_start(out=buf[:, clo - lo:chi - lo], in_=src[row0:row0 + 128, clo:chi]),
             reads=rk, writes=[bufkey], dma=True)
        hm = self.hmask
        if clo == lo:
            s.op(eng_memset, lambda h: h.tensor_scalar(buf[:, 0:halo], buf[:, 0:halo], hm[:, 2 * tt:2 * tt + 1],
                                                       None, ALU.mult),
                 reads=["hmask"], writes=[bufkey])
        if chi == hi:
            s.op(eng_memset, lambda h: h.tensor_scalar(buf[:, T + halo:T + 2 * halo], buf[:, T + halo:T + 2 * halo],
                                                       hm[:, 2 * tt + 1:2 * tt + 2], None, ALU.mult),
                 reads=["hmask"], writes=[bufkey])

    def _p1_Cfront(self, tt):
        for st_ in self._p1_Cfront_steps(tt):
            st_()

    def _p1_Cfront_steps(self, tt):
        s = self.s
        steps = []
        CV, lv, pooled, icnt = self.CV, self.lv, self.pooled, self.icnt
        ics = tt % 2
        cw, cb = self.convw, self.convb
        jobs = [("p", k) for k in range(8)] + [("c", k) for k in range(24)]
        base = self.cnt["cv"]
        self.cnt["cv"] += len(jobs)

        def slot(n):
            return (base + n) % 3

        def load(n):
            if n >= len(jobs):
                return
            kind, k = jobs[n]
            sl = slot(n)
            if kind == "p":
                self._load_halo(CV[sl], ("CV", sl), self.scr["PU"], "PU", k * 128, tt, 8)
            else:
                self._load_halo(CV[sl], ("CV", sl), self.scr["HV"], "HV", k * 128, tt, 1)

        def init():
            s.op("pool", lambda h: h.dma_start(
                out=icnt[ics][:, :, :],
                in_=self.c["invcnt"][:, tt * T:(tt + 1) * T].partition_broadcast(128)),
                writes=[("icnt", ics)], dma=True)
            load(0)
            load(1)
        steps.append(init)

        def job(n, kind, k):
            sl = slot(n)
            buf = CV[sl]
            bk = ("CV", sl)
            if kind == "p":
                g = k // 2
                cur, curkey = buf, bk
                rngs = [(1, T + 16, 1, 0), (2, T + 15, 1, 1), (4, T + 13, 2, 2), (8, T + 9, 4, 4)]
                for lvl in range(g + 1):
                    a, b, dl, dr = rngs[lvl]
                    dst = lv[lvl]
                    s.op("dve", lambda h, dst=dst, cur=cur, a=a, b=b, dl=dl, dr=dr: h.tensor_tensor(
                        dst[:, a:b], cur[:, a - dl:b - dl], cur[:, a + dr:b + dr], ALU.add),
                        reads=[curkey], writes=[("lv", lvl)])
                    cur, curkey = dst, ("lv", lvl)
                tmp = lv[g]
                s.op("dve", lambda h, tmp=tmp, cur=cur, g=g: h.tensor_tensor(
                    tmp[:, 8:T + 8], cur[:, 8:T + 8], icnt[ics][:, g, :], ALU.mult),
                    reads=[curkey, ("icnt", ics)], writes=[("lv", g)])
                pl = (tt % 2) * 8 + k
                s.op("dve", lambda h, tmp=tmp, buf=buf, pl=pl: h.tensor_tensor(
                    pooled[pl][:, :], tmp[:, 8:T + 8], buf[:, 8:T + 8], ALU.subtract),
                    reads=[("lv", g), bk], writes=[("pooled", pl)])
            else:
                hc = k
                vs = n % 3
                acc, co_ = self.cacc[vs], self.cout[vs]
                s.op("dve", lambda h, buf=buf, acc=acc, hc=hc: h.tensor_scalar(
                    acc[:, :], buf[:, 1:T + 1], cw[:, 1, hc:hc + 1], cb[:, hc:hc + 1], ALU.mult, ALU.add),
                    reads=[bk, "convw", "convb"], writes=[("cacc", vs)])
                s.op("dve", lambda h, buf=buf, acc=acc, hc=hc: h.scalar_tensor_tensor(
                    acc[:, :], buf[:, 0:T], cw[:, 0, hc:hc + 1], acc[:, :], ALU.mult, ALU.add),
                    reads=[bk, "convw"], writes=[("cacc", vs)])
                s.op("dve", lambda h, buf=buf, acc=acc, co_=co_, hc=hc: h.scalar_tensor_tensor(
                    co_[:, :], buf[:, 2:T + 2], cw[:, 2, hc:hc + 1], acc[:, :], ALU.mult, ALU.add),
                    reads=[bk, "convw", ("cacc", vs)], writes=[("cout", vs)])
                dst = self.scr["HC"][hc * 128:(hc + 1) * 128, tt * T:(tt + 1) * T]
                s.op("pool", lambda h, co_=co_, dst=dst: h.dma_start(out=dst, in_=co_[:, :]),
                     reads=[("cout", vs)], writes=[("HC", hc, tt)], dma=True)
            load(n + 2)
        for n, (kind, k) in enumerate(jobs):
            steps.append(lambda n=n, kind=kind, k=k: job(n, kind, k))
        return steps

    def _p1_Cback(self, tt):
        s, ps = self.s, self.ps
        pooled = self.pooled
        for g in range(4):
            pls = [(tt % 2) * 8 + 2 * g + kc for kc in range(2)]
            for oc in range(2):
                bank = 4 + (oc % 2)
                for kc in range(2):
                    s.op("pe", lambda h, g=g, kc=kc, oc=oc, bank=bank, pl=pls[kc]: h.matmul(
                        ps[bank][:, :], self.poolw[:, 2 * g + kc, oc * 128:(oc + 1) * 128], pooled[pl][:, :],
                        start=(kc == 0), stop=(kc == 1)),
                        reads=["poolw", ("pooled", pls[kc])], writes=[("ps", bank)])
                ae = self.cnt["ae"] % 2
                self.cnt["ae"] += 1
                co = 2 * g + oc
                s.op("act", lambda h, ae=ae, bank=bank, co=co: h.activation(
                    self.aev[ae][:, :], ps[bank][:, :], AF.Copy, scale=self.pscale[:, co:co + 1]),
                    reads=[("ps", bank), "pscale"], writes=[("aev", ae)])
                dst = self.scr["AT"][co * 128:(co + 1) * 128, tt * T:(tt + 1) * T]
                s.op("act", lambda h, ae=ae, dst=dst: h.dma_start(out=dst, in_=self.aev[ae][:, :]),
                     reads=[("aev", ae)], writes=[("AT", co, tt)], dma=True)

    def phase2a(self):
        self._phase_id += 1
        nc, s, ps = self.nc, self.s, self.ps
        TWO_PI = 2.0 * math.pi
        MAGIC = 12582912.0
        with ExitStack() as ph:
            def sb(name, shape, dt):
                return ph.enter_context(nc.sbuf_tensor(f"sb{self._uid()}_" + name, shape, dt))
            zf = [sb(f"zf{i}", [33, 512], F32) for i in range(2)]
            w1 = sb("fw1", [33, 64], F32)
            w2 = sb("fw2", [64, 64], F32)
            fcol = sb("fcol", [64, 8], F32)
            w3b = sb("w3b", [64, 4 * DH], BF16)
            h2T = sb("h2T", [64, NTOK], BF16)
            misc = sb("misc2a", [128, 16], F32)
            hbias = sb("hbias", [128, 16], F32)
            tch = [sb(f"tch{i}", [128, 2048], F32) for i in range(2)]
            dec = sb("dec", [128, NTOK], F32)
            fwb = [[sb(f"fwb{i}_{j}", [128, NTOK], BF16) for j in range(2)] for i in range(2)]
            junk = sb("junk", [128, 2048], BF16)
            fo = [sb(f"fo{i}", [128, 1024], BF16) for i in range(4)]
            pmt = [sb(f"pmt{i}", [128, 1024], F32) for i in range(4)]
            mt = [sb(f"mt{i}", [64, 512], F32) for i in range(8)]
            sm = [sb(f"sm{i}", [128, 16], F32) for i in range(2)]

            s.op("sp", lambda h: h.dma_start(out=w1[:, :], in_=self.w["filt_w1"]), writes=["fw1"], dma=True)
            s.op("sp", lambda h: h.dma_start(out=w2[:, :], in_=self.w["filt_w2"]), writes=["fw2"], dma=True)
            for i, nm in enumerate(("filt_b1", "filt_freq1", "filt_b2", "filt_freq2")):
                s.op("sp", lambda h, i=i, nm=nm: h.dma_start(
                    out=fcol[:, i:i + 1], in_=self.w[nm].rearrange("o (c p) -> p (o c)", p=64),
                    allow_slow_non_contiguous=True), writes=["fcol"], dma=True)
            s.op("pool", lambda h: h.dma_start(out=w3b[:, :], in_=self.w["filt_w3"]), writes=["w3b"], dma=True)
            s.op("sp", lambda h: h.dma_start(out=misc[:, :], in_=self.c["misc"]), writes=["misc"], dma=True)
            for o in range(2):
                s.op("sp", lambda h, o=o: h.dma_start(
                    out=hbias[:, o * 8:(o + 1) * 8],
                    in_=self.w["hyena_bias"][o:o + 1, :].rearrange("o (c p) -> p (o c)", p=128),
                    allow_slow_non_contiguous=True), writes=["hbias"], dma=True)
            s.op("dve", lambda h: h.tensor_tensor(fcol[:, 4:5], fcol[:, 0:1], fcol[:, 1:2], ALU.mult),
                 reads=["fcol"], writes=["fcol"])
            s.op("dve", lambda h: h.tensor_tensor(fcol[:, 5:6], fcol[:, 2:3], fcol[:, 3:4], ALU.mult),
                 reads=["fcol"], writes=["fcol"])

            def sin_layer(psb, fi, out_ap, outkey, ms):
                u, k, r = mt[ms * 4 + 0], mt[ms * 4 + 1], mt[ms * 4 + 2]
                ku, kk, kr = ("mt", ms, 0), ("mt", ms, 1), ("mt", ms, 2)
                s.op("dve", lambda h: h.tensor_scalar(u[:, :], ps[psb][0:64, :], fcol[:, fi:fi + 1],
                                                      fcol[:, 4 + fi // 2:5 + fi // 2], ALU.mult, ALU.add),
                     reads=[("ps", psb), "fcol"], writes=[ku])
                s.op("dve", lambda h: h.tensor_scalar(k[:, :], u[:, :], 1.0 / TWO_PI, MAGIC, ALU.mult, ALU.add),
                     reads=[ku], writes=[kk])
                s.op("dve", lambda h: h.tensor_scalar(k[:, :], k[:, :], -MAGIC, -TWO_PI, ALU.add, ALU.mult),
                     reads=[kk], writes=[kk])
                s.op("dve", lambda h: h.tensor_tensor(r[:, :], k[:, :], u[:, :], ALU.add),
                     reads=[ku, kk], writes=[kr])
                s.op("dve", lambda h: h.tensor_scalar(r[:, :], r[:, :], -math.pi, math.pi, ALU.max, ALU.min),
                     reads=[kr], writes=[kr])
                s.op("act", lambda h: h.activation(out_ap, r[:, :], AF.Sin), reads=[kr], writes=[outkey])

            nchk = NTOK // 512

            def mlp_l1(ch):
                ms = ch % 2
                zs = ch % 2
                b1_ = (0, 6)[ms]
                s.op("sp", lambda h: h.dma_start(out=zf[zs][:, :], in_=self.c["zfeatT"][:, ch * 512:(ch + 1) * 512]),
                     writes=[("zf", zs)], dma=True)
                s.op("pe", lambda h: h.matmul(ps[b1_][0:64, :], w1[:, :], zf[zs][:, :], start=True, stop=True),
                     reads=["fw1", ("zf", zs)], writes=[("ps", b1_)])
                sin_layer(b1_, 1, mt[ms * 4 + 3][:, :], ("mt", ms, 3), ms)

            def mlp_l2(ch):
                ms = ch % 2
                b2_ = (1, 7)[ms]
                s.op("pe", lambda h: h.matmul(ps[b2_][0:64, :], w2[:, :], mt[ms * 4 + 3][:, :], start=True, stop=True),
                     reads=["fw2", ("mt", ms, 3)], writes=[("ps", b2_)])
                sin_layer(b2_, 3, h2T[:, ch * 512:(ch + 1) * 512], "h2T", ms)

            mlp_l1(0)
            for ch in range(nchk):
                if ch + 1 < nchk:
                    mlp_l1(ch + 1)
                mlp_l2(ch)

            its = [(cc, o) for cc in range(8) for o in range(2)]
            cnt = {"fo": 0, "pc": 0, "tq": 0}

            def do_dec(cc):
                for q in range(4):
                    ts_ = cnt["tq"] % 2
                    cnt["tq"] += 1
                    s.op("sp", lambda h, ts_=ts_, q=q: h.dma_start(
                        out=tch[ts_][:, :], in_=self.c["tvals"][:, q * 2048:(q + 1) * 2048].partition_broadcast(128)),
                        writes=[("tch", ts_)], dma=True)
                    s.op("act", lambda h, ts_=ts_, q=q: h.activation(
                        dec[:, q * 2048:(q + 1) * 2048], tch[ts_][:, :], AF.Exp, scale=misc[:, 1 + cc:2 + cc]),
                        reads=[("tch", ts_), "misc"], writes=["dec"])

            def mults(it):
                cc, o = its[it]
                ss = it % 2
                fw = fwb[ss]
                for dr in range(2):
                    r0 = o * 2 * DH + dr * DH + cc * 128
                    for ch in range(nchk):
                        bank = 2 + ch % 4
                        s.op("pe", lambda h, r0=r0, ch=ch, bank=bank: h.matmul(
                            ps[bank][:, :], w3b[:, r0:r0 + 128], h2T[:, ch * 512:(ch + 1) * 512],
                            start=True, stop=True),
                            reads=["w3b", "h2T"], writes=[("ps", bank)])
                        s.op("dve", lambda h, dr=dr, ch=ch, bank=bank: h.tensor_tensor(
                            fw[dr][:, ch * 512:(ch + 1) * 512], ps[bank][:, :], dec[:, ch * 512:(ch + 1) * 512],
                            ALU.mult),
                            reads=[("ps", bank), "dec"], writes=[("fwb", ss, dr)])

            def tail(it):
                cc, o = its[it]
                ss = it % 2
                fw = fwb[ss]
                smt = sm[ss]
                for dr in range(2):
                    for q in range(4):
                        s.op("act", lambda h, dr=dr, q=q: h.activation(
                            junk[:, :], fw[dr][:, q * 2048:(q + 1) * 2048], AF.Square,
                            accum_out=smt[:, 8 + dr * 4 + q:9 + dr * 4 + q]),
                            reads=[("fwb", ss, dr)], writes=["junk", ("sq", ss)])
                s.op("dve", lambda h: h.tensor_reduce(smt[:, 2:3], smt[:, 8:16], AX.X, ALU.add),
                     reads=[("sq", ss)], writes=[("sm2", ss)])
                s.op("dve", lambda h: h.tensor_tensor(smt[:, 3:4], fw[0][:, 0:1], fw[1][:, 0:1], ALU.mult),
                     reads=[("fwb", ss, 0), ("fwb", ss, 1)], writes=[("sm3", ss)])
                s.op("dve", lambda h: h.tensor_scalar(smt[:, 2:3], smt[:, 2:3], misc[:, 0:1], None, ALU.mult),
                     reads=[("sm2", ss), "misc"], writes=[("sm2", ss)])
                s.op("dve", lambda h: h.scalar_tensor_tensor(smt[:, 4:5], smt[:, 3:4], 2.0, smt[:, 2:3],
                                                             ALU.mult, ALU.add),
                     reads=[("sm2", ss), ("sm3", ss)], writes=[("sm4", ss)])
                s.op("act", lambda h: h.activation(smt[:, 5:6], smt[:, 4:5], AF.Sqrt, bias=EPS, scale=1.0),
                     reads=[("sm4", ss)], writes=[("sm5", ss)])
                s.op("dve", lambda h: h.reciprocal(smt[:, 6:7], smt[:, 5:6]),
                     reads=[("sm5", ss)], writes=[("sm6", ss)])
                s.op("dve", lambda h: h.tensor_tensor(
                    smt[:, 7:8], hbias[:, o * 8 + cc:o * 8 + cc + 1], smt[:, 5:6], ALU.mult),
                    reads=[("sm5", ss), "hbias"], writes=[("sm7", ss)])
                s.op("dve", lambda h: h.tensor_tensor(fw[0][:, 0:1], fw[0][:, 0:1], smt[:, 7:8], ALU.add),
                     reads=[("sm7", ss), ("fwb", ss, 0), ("fwb", ss, 1)], writes=[("fwb", ss, 0)])
                s.op("dve", lambda h: h.scalar_tensor_tensor(
                    fw[0][:, 4096:4097], smt[:, 7:8], misc[:, 9:10], fw[0][:, 4096:4097], ALU.mult, ALU.add),
                    reads=[("sm7", ss), "misc"], writes=[("fwb", ss, 0)])
                for q in range(8):
                    csl = slice(q * 1024, (q + 1) * 1024)
                    for dr, op, eng in ((0, ALU.add, "dve"), (1, ALU.subtract, "pool")):
                        pcs = cnt["pc"] % 4
                        cnt["pc"] += 1
                        r0 = o * 2 * DH + dr * DH + cc * 128
                        s.op(eng, lambda h, csl=csl, op=op, pcs=pcs: h.tensor_tensor(
                            pmt[pcs][:, :], fw[0][:, csl], fw[1][:, csl], op),
                            reads=[("fwb", ss, 0), ("fwb", ss, 1)], writes=[("pmt", pcs)])
                        fs = cnt["fo"] % 4
                        cnt["fo"] += 1
                        s.op("act", lambda h, pcs=pcs, fs=fs: h.activation(
                            fo[fs][:, :], pmt[pcs][:, :], AF.Copy, scale=smt[:, 6:7]),
                            reads=[("pmt", pcs), ("sm6", ss)], writes=[("fo", fs)])
                        s.op("sp", lambda h, fs=fs, r0=r0, csl=csl: h.dma_start(
                            out=self.scr["FT"][r0:r0 + 128, csl], in_=fo[fs][:, :]),
                            reads=[("fo", fs)], writes=[("FT", r0 // 64), ("FT", r0 // 64 + 1)], dma=True)

            do_dec(0)
            mults(0)
            for it in range(len(its)):
                if it + 1 < len(its):
                    if its[it + 1][1] == 0:
                        do_dec(its[it + 1][0])
                    mults(it + 1)
                tail(it)
            s.barrier()

    def _fft_consts(self, sb, need_inv):
        s = self.s
        self.dA = sb("dA", [64, 128], BF16)
        self.dB = sb("dB", [128, 64, 3, 128], BF16)
        s.op("sp", lambda h: h.dma_start(out=self.dA[:, :], in_=self.c["dftA"]), writes=["dA"], dma=True)
        for i in range(4):
            s.op("sp", lambda h, i=i: h.dma_start(
                out=self.dB[:, i * 16:(i + 1) * 16, :, :].rearrange("p k m q -> p (k m q)"),
                in_=self.c["dftB"][:, i * 6144:(i + 1) * 6144]),
                writes=["dB"], dma=True)
        if need_inv:
            self.dF = sb("dF", [128, 2, 128], BF16)
            s.op("sp", lambda h: h.dma_start(out=self.dF[:, :, :],
                                             in_=self.c["dftF"].rearrange("k (f p) -> k f p", f=2)),
                 writes=["dF"], dma=True)

    def _load_jcp(self, dst, dstkey, src_rows, reads=()):
        self.s.op("sp", lambda h: h.dma_start(out=dst[:, :, :],
                                              in_=src_rows.rearrange("c (j p) -> j c p", p=128)),
                  reads=list(reads), writes=[dstkey], dma=True)

    def _stageA(self, src, srckey, YP, ypkey="YP", nbanks=2):
        s, ps = self.s, self.ps
        Yv = YP[:, :].rearrange("p (m c e) -> p m c e", c=64, e=2)
        for c4 in range(16):
            bank = (0, 1, 6, 7)[c4 % nbanks]
            for i in range(4):
                c = c4 * 4 + i
                s.op("pe", lambda h, c=c, i=i, bank=bank: h.matmul(
                    ps[bank][:, i * 128:(i + 1) * 128], src[:, c, :], self.dA[:, :], start=True, stop=True),
                    reads=[srckey, "dA"], writes=[("ps", bank)])
            s.op("act", lambda h, c4=c4, bank=bank: h.copy(
                Yv[:, :, c4 * 4:(c4 + 1) * 4, :], ps[bank][:, :].rearrange("p (c m e) -> p m c e", c=4, e=2)),
                reads=[("ps", bank)], writes=[ypkey])

    def _stageB(self, YP, evac, ypkey="YP", YPi=None, ypkey_i=None):
        s, ps = self.s, self.ps
        Y = YP[:, :].rearrange("p (m c e) -> p m c e", c=64, e=2)
        Y2 = Y if YPi is None else YPi[:, :].rearrange("p (m c e) -> p m c e", c=64, e=2)
        ypkey_i = ypkey if ypkey_i is None else ypkey_i
        dB = self.dB
        for kb in range(8):
            br = 2 + (kb % 2) * 2
            bi = br + 1
            for i in range(8):
                kc = kb * 8 + i
                o_r = ps[br][:, i * 64:(i + 1) * 64]
                o_i = ps[bi][:, i * 64:(i + 1) * 64]
                Yr = Y[:, kc // 2, :, kc % 2]
                Yi = Y[:, 32 + kc // 2, :, kc % 2]
                Yr2 = Y2[:, kc // 2, :, kc % 2]
                Yi2 = Y2[:, 32 + kc // 2, :, kc % 2]
                for (out, m, rhs, st, sp_, bk, yk) in ((o_r, 0, Yr, True, False, br, ypkey),
                                                       (o_r, 2, Yi, False, True, br, ypkey),
                                                       (o_i, 1, Yr2, True, False, bi, ypkey_i),
                                                       (o_i, 0, Yi2, False, True, bi, ypkey_i)):
                    s.op("pe", lambda h, out=out, m=m, rhs=rhs, st=st, sp_=sp_, kc=kc: h.matmul(
                        out, dB[:, kc, m, :], rhs, start=st, stop=sp_),
                        reads=[yk, "dB"], writes=[("ps", bk)])
            evac(kb, br, bi)

    def phase2b(self):
        self._phase_id += 1
        nc, s, ps = self.nc, self.s, self.ps
        with ExitStack() as ph:
            def sb(name, shape, dt):
                return ph.enter_context(nc.sbuf_tensor(f"sb{self._uid()}_" + name, shape, dt))
            self._fft_consts(sb, False)
            zin = [sb(f"fzin{i}", [64, 64, 128], BF16) for i in range(3)]
            YPs = [sb(f"fYP{i}", [128, 8192], BF16) for i in range(2)]
            Ho = [sb(f"Ho{i}", [128, 2, 64, 64], BF16) for i in range(2)]
            jobs = [(o, cg, dr) for o in range(2) for cg in range(self.ncg) for dr in range(2)]

            def load(n):
                if n >= len(jobs):
                    return
                o, cg, dr = jobs[n]
                r0 = o * 2 * DH + dr * DH + cg * 64
                self._load_jcp(zin[n % 3], ("fzin", n % 3), self.scr["FT"][r0:r0 + 64, :])
            load(0)
            load(1)
            for n, (o, cg, dr) in enumerate(jobs):
                hs_ = (o * 16 + cg) % 2
                zs = n % 3
                self._stageA(zin[zs], ("fzin", zs), YPs[dr], ("YP", dr), nbanks=4)
                load(n + 2)
                if dr == 1:
                    def evac(kb, br, bi, hs_=hs_):
                        s.op("act", lambda h: h.copy(
                            Ho[hs_][:, 0, kb * 8:(kb + 1) * 8, :], ps[br][:, :].rearrange("p (k c) -> p k c", k=8)),
                            reads=[("ps", br)], writes=[("Ho", hs_)])
                        s.op("dve", lambda h: h.tensor_copy(
                            Ho[hs_][:, 1, kb * 8:(kb + 1) * 8, :], ps[bi][:, :].rearrange("p (k c) -> p k c", k=8)),
                            reads=[("ps", bi)], writes=[("Ho1", hs_)])
                    self._stageB(YPs[0], evac, ("YP", 0), YPi=YPs[1], ypkey_i=("YP", 1))
                    s.op("pool", lambda h, o=o, cg=cg, hs_=hs_: h.dma_start(
                        out=self.scr["HS"][o, cg, :, :], in_=Ho[hs_][:, :, :, :].rearrange("p s k c -> p (s k c)")),
                        reads=[("Ho", hs_), ("Ho1", hs_)], writes=[("HS", o, cg)], dma=True)
            s.barrier()

    def phase2c(self):
        self._phase_id += 1
        nc, s, ps = self.nc, self.s, self.ps
        with ExitStack() as ph:
            def sb(name, shape, dt):
                return ph.enter_context(nc.sbuf_tensor(f"sb{self._uid()}_" + name, shape, dt))
            self._fft_consts(sb, True)
            dF = self.dF
            z = [sb(f"z{i}", [64, 64, 128], BF16) for i in range(3)]
            gate = sb("gate", [64, 64, 128], BF16)
            YP = sb("YP", [128, 8192], BF16)
            P1 = sb("P1", [128, 8192], BF16)
            S = sb("S", [128, 2, 64, 64], BF16)
            H = sb("H", [128, 2, 64, 64], BF16)
            Gs = [sb(f"Gs{i}", [128, 8, 2, 64], BF16) for i in range(4)]
            tmp = [sb(f"tmp{i}", [128, 512], F32) for i in range(8)]
            HC = self.scr["HC"]
            ng = 0
            def gload(pb):
                gs = pb % 4
                s.op("sp", lambda h: h.dma_start(
                    out=Gs[gs][:, :, :, :].rearrange("r a f j -> r (a f j)"),
                    in_=self.c["dftG"][:, pb * 1024:(pb + 1) * 1024]),
                    writes=[("Gs", gs)], dma=True)
            self._load_jcp(z[0], ("z", 0), HC[0:64, :])
            for cg in range(self.ncg):
                zi = [(cg + q) % 3 for q in range(3)]
                for o in range(2):
                    src, dst = z[zi[o]], z[zi[o + 1]]
                    srck, dstk = ("z", zi[o]), ("z", zi[o + 1])
                    s.op("sp", lambda h, o=o, cg=cg: h.dma_start(
                        out=H[:, :, :, :].rearrange("p s k c -> p (s k c)"), in_=self.scr["HS"][o, cg, :, :]),
                        writes=["H"], dma=True)
                    g0 = (1 + o) * DH + cg * 64
                    self._load_jcp(gate, "gate", HC[g0:g0 + 64, :])
                    self._stageA(src, srck, YP)
                    if o == 1 and cg + 1 < self.ncg:
                        self._load_jcp(z[zi[1]], ("z", zi[1]), HC[(cg + 1) * 64:(cg + 2) * 64, :])

                    def evac(kb, br, bi):
                        t = tmp[(kb % 2) * 4:(kb % 2) * 4 + 4]
                        tk = [("tmp", (kb % 2) * 4 + q) for q in range(4)]
                        Hr = H[:, 0, kb * 8:(kb + 1) * 8, :].rearrange("p k c -> p (k c)")
                        Hi = H[:, 1, kb * 8:(kb + 1) * 8, :].rearrange("p k c -> p (k c)")
                        for q, (bk, hh) in enumerate(((br, Hr), (bi, Hi), (br, Hi), (bi, Hr))):
                            s.op("dve", lambda h, q=q, bk=bk, hh=hh: h.tensor_tensor(
                                t[q][:, :], ps[bk][:, :], hh, ALU.mult),
                                reads=[("ps", bk), "H"], writes=[tk[q]])
                        s.op("pool", lambda h: h.tensor_tensor(
                            S[:, 0, kb * 8:(kb + 1) * 8, :].rearrange("p k c -> p (k c)"), t[0][:, :], t[1][:, :],
                            ALU.subtract), reads=[tk[0], tk[1]], writes=["S"])
                        s.op("pool", lambda h: h.tensor_tensor(
                            S[:, 1, kb * 8:(kb + 1) * 8, :].rearrange("p k c -> p (k c)"), t[2][:, :], t[3][:, :],
                            ALU.add), reads=[tk[2], tk[3]], writes=["S"])
                    self._stageB(YP, evac)
                    if "dbgS" in self.debug and cg == 0 and o == 0:
                        s.op("sp", lambda h: h.dma_start(out=self.scr["dbgS"], in_=S[:, :, :, :].rearrange("p s k c -> p (s k c)")),
                             reads=["S"], writes=["dbgS"], dma=True)
                        s.op("sp", lambda h: h.dma_start(out=self.scr["dbgY"], in_=YP[:, :]),
                             reads=["YP"], writes=["dbgY"], dma=True)
                    gload(0)
                    gload(1)
                    gload(2)
                    Sv = S[:, :, :, :].rearrange("k s q c -> k c (s q)")
                    Pb = [YP, P1]
                    Pk = ["YP", "P1"]
                    nev = 0
                    for phh in range(2):
                        Pv = Pb[phh][:, :].rearrange("r (q c e) -> r q c e", c=64, e=2)
                        for c4 in range(16):
                            bank = 4 + c4 % 4
                            for i in range(4):
                                c = c4 * 4 + i
                                s.op("pe", lambda h, c=c, i=i, bank=bank, phh=phh: h.matmul(
                                    ps[bank][:, i * 128:(i + 1) * 128].rearrange("r (f p) -> r f p", f=2),
                                    Sv[:, c, :], dF[:, :, phh * 64:(phh + 1) * 64], start=True, stop=True),
                                    reads=["S", "dF"], writes=[("ps", bank)])
                            eng = "act" if nev % 2 == 0 else "dve"
                            nev += 1
                            srcv = ps[bank][:, :].rearrange("r (c q e) -> r q c e", c=4, e=2)
                            dstv = Pv[:, :, c4 * 4:(c4 + 1) * 4, :]
                            if eng == "act":
                                fn = lambda h, dstv=dstv, srcv=srcv: h.copy(dstv, srcv)
                            else:
                                fn = lambda h, dstv=dstv, srcv=srcv: h.tensor_copy(dstv, srcv)
                            s.op(eng, fn, reads=[("ps", bank)], writes=[Pk[phh]])
                    if "dbgS" in self.debug and cg == 0 and o == 0:
                        s.op("sp", lambda h: h.dma_start(out=self.scr["dbgP0"], in_=YP[:, :]),
                             reads=["YP"], writes=["dbgP0"], dma=True)
                        s.op("sp", lambda h: h.dma_start(out=self.scr["dbgP1"], in_=P1[:, :]),
                             reads=["P1"], writes=["dbgP1"], dma=True)
                    dstv_all = dst
                    gatev = gate
                    for pb in range(16):
                        phh = pb // 8
                        P4 = Pb[phh][:, :].rearrange("r (q c e) -> r q c e", c=64, e=2)
                        gs = pb % 4
                        if pb + 3 < 16:
                            gload(pb + 3)
                        bank = pb % 2
                        for i in range(8):
                            pl = (pb * 8 + i) % 64
                            for f in range(2):
                                s.op("pe", lambda h, i=i, f=f, pl=pl, gs=gs, bank=bank, P4=P4: h.matmul(
                                    ps[bank][0:64, i * 64:(i + 1) * 64], Gs[gs][:, i, f, :], P4[:, f * 32 + pl // 2, :, pl % 2],
                                    start=(f == 0), stop=(f == 1)),
                                    reads=[("Gs", gs), Pk[phh]], writes=[("ps", bank)])
                        s.op("dve", lambda h, pb=pb, bank=bank, dstv_all=dstv_all, gatev=gatev: h.tensor_tensor(
                            dstv_all[:, :, pb * 8:(pb + 1) * 8],
                            ps[bank][0:64, :].rearrange("j (p c) -> j c p", p=8),
                            gatev[:, :, pb * 8:(pb + 1) * 8], ALU.mult),
                            reads=[("ps", bank), "gate"], writes=[dstk])
                if "dbgS" in self.debug and cg == 0:
                    pass
                zf = zi[2]
                s.op("pool", lambda h, cg=cg, zf=zf: h.dma_start(
                    out=self.scr["BT"][cg * 64:(cg + 1) * 64, :].rearrange("c (j p) -> j c p", p=128),
                    in_=z[zf][:, :, :]),
                    reads=[("z", zf)], writes=[("BT", cg)], dma=True)
            s.barrier()

    def phase3(self):
        self._phase_id += 1
        nc, s, ps = self.nc, self.s, self.ps
        NF = DFF // 128
        with ExitStack() as ph:
            def sb(name, shape, dt):
                return ph.enter_context(nc.sbuf_tensor(f"sb{self._uid()}_" + name, shape, dt))
            ident = sb("ident", [128, 128], BF16)
            gf = sb("gffn", [128, D], F32)
            gl = sb("gfin", [128, D], F32)
            xt = sb("x3", [128, 4, D], F32)
            act = sb("act3", [128, NF, T], BF16)
            abT = sb("abT", [128, 16, T], BF16)
            mh = sb("mh", [128, 16, T], BF16)
            ring = [sb(f"ring{i}", [128, 8192], BF16) for i in range(3)]
            sg = [sb(f"sg{i}", [128, 2, T], BF16) for i in range(4)]
            tmp = [sb(f"tm{i}", [128, T], F32) for i in range(4)]
            hb = [sb(f"hb3{i}", [128, D], BF16) for i in range(2)]
            st = [sb(f"st3{i}", [128, 4], F32) for i in range(2)]
            psT = [ps[6][:, :].bitcast(BF16), ps[7][:, :].bitcast(BF16)]
            W = self.scr
            rc = [0]

            def wload(fn_in, shape_view, srckeys):
                slot = rc[0] % 3
                rc[0] += 1
                s.op("sp", lambda h, slot=slot: fn_in(h, ring[slot]), reads=srckeys,
                     writes=[("ring", slot), ("ringb", slot)], dma=True)
                return slot

            s.op("pool", lambda h: h.dma_start(out=ident[:, :], in_=self.c["ident"]), writes=["ident"], dma=True)
            s.op("pool", lambda h: h.dma_start(out=gf[:, :], in_=self.w["g_ffn"].partition_broadcast(128)),
                 writes=["gf"], dma=True)
            s.op("pool", lambda h: h.dma_start(out=gl[:, :], in_=self.w["g_final"].partition_broadcast(128)),
                 writes=["gl"], dma=True)
            nsg = 0
            nhb = 0
            for tt in range(self.nt):
                tsl = slice(tt * T, (tt + 1) * T)

                def ab_load(tsl_):
                    s.op("pool", lambda h: h.dma_start(
                        out=abT[:, 0:8, :], in_=W["AT"][:, tsl_].rearrange("(k p) t -> p k t", p=128)),
                        writes=[("ab", k) for k in range(8)], dma=True)
                    s.op("pool", lambda h: h.dma_start(
                        out=abT[:, 8:16, :], in_=W["BT"][:, tsl_].rearrange("(k p) t -> p k t", p=128)),
                        writes=[("ab", k) for k in range(8, 16)], dma=True)

                def sgload(dc, tsl_=tsl):
                    if dc >= 16:
                        return
                    gs_ = dc % 4
                    s.op("pool", lambda h: h.dma_start(
                        out=sg[gs_][:, 0, :], in_=W["SG"][dc * 128:(dc + 1) * 128, tsl_]),
                        writes=[("sg", gs_)], dma=True)
                    s.op("pool", lambda h: h.dma_start(
                        out=sg[gs_][:, 1, :], in_=W["SG"][D + dc * 128:D + (dc + 1) * 128, tsl_]),
                        writes=[("sgb", gs_)], dma=True)
                if tt == 0:
                    ab_load(tsl)
                    sgload(0)
                    sgload(1)
                s.op("pool", lambda h, tsl=tsl: h.dma_start(
                    out=xt[:, :, :], in_=self.x[tsl, :].rearrange("(s p) d -> p s d", p=128)),
                    writes=[("x", q) for q in range(4)], dma=True)
                for db in range(4):
                    slot = wload(lambda h, r, db=db: (h.dma_start(
                        out=r[:, 0:4096].rearrange("p (k n) -> p k n", k=8),
                        in_=W["WB_a"][:, db * 512:(db + 1) * 512].rearrange("(k p) n -> p k n", p=128))), None, [])
                    s.op("sp", lambda h, slot=slot, db=db: h.dma_start(
                        out=ring[slot][:, 4096:8192].rearrange("p (k n) -> p k n", k=8),
                        in_=W["WB_b"][:, db * 512:(db + 1) * 512].rearrange("(k p) n -> p k n", p=128)),
                        writes=[("ringb", slot)], reads=[("ring", slot)], dma=True)
                    wa = ring[slot][:, 0:4096].rearrange("p (k n) -> p k n", k=8)
                    wbv = ring[slot][:, 4096:8192].rearrange("p (k n) -> p k n", k=8)
                    for dq in range(4):
                        dc = db * 4 + dq
                        gs_ = dc % 4
                        sgload(dc + 2)
                        ba = (dc % 2) * 2
                        bb = ba + 1
                        for kc in range(8):
                            s.op("pe", lambda h, kc=kc, dq=dq, ba=ba, wa=wa: h.matmul(
                                ps[ba][:, :], wa[:, kc, dq * 128:(dq + 1) * 128], abT[:, kc, :],
                                start=(kc == 0), stop=(kc == 7)),
                                reads=[("ring", slot), ("ab", kc)], writes=[("ps", ba)])
                        for kc in range(8):
                            s.op("pe", lambda h, kc=kc, dq=dq, bb=bb, wbv=wbv: h.matmul(
                                ps[bb][:, :], wbv[:, kc, dq * 128:(dq + 1) * 128], abT[:, 8 + kc, :],
                                start=(kc == 0), stop=(kc == 7)),
                                reads=[("ringb", slot), ("ab", 8 + kc)], writes=[("ps", bb)])
                        ta, tb = tmp[(dc % 2) * 2], tmp[(dc % 2) * 2 + 1]
                        ka, kb_ = ("tm", (dc % 2) * 2), ("tm", (dc % 2) * 2 + 1)
                        s.op("dve", lambda h, ta=ta, ba=ba, gs_=gs_: h.tensor_tensor(
                            ta[:, :], ps[ba][:, :], sg[gs_][:, 0, :], ALU.mult),
                            reads=[("ps", ba), ("sg", gs_)], writes=[ka])
                        s.op("dve", lambda h, tb=tb, bb=bb, gs_=gs_: h.tensor_tensor(
                            tb[:, :], ps[bb][:, :], sg[gs_][:, 1, :], ALU.mult),
                            reads=[("ps", bb), ("sgb", gs_)], writes=[kb_])
                        s.op("pool", lambda h, ta=ta, tb=tb, dc=dc: h.tensor_tensor(
                            mh[:, dc, :], ta[:, :], tb[:, :], ALU.add),
                            reads=[ka, kb_], writes=[("mh", dc)])
                for nb in range(4):
                    slot = wload(lambda h, r, nb=nb: h.dma_start(
                        out=r[:, :].rearrange("p (k n) -> p k n", k=16),
                        in_=W["WB_out"][:, nb * 512:(nb + 1) * 512].rearrange("(k p) n -> p k n", p=128)), None, [])
                    wv = ring[slot][:, :].rearrange("p (k n) -> p k n", k=16)
                    for sub in range(4):
                        bank = 4 + (nb * 4 + sub) % 2
                        for dc in range(16):
                            s.op("pe", lambda h, dc=dc, sub=sub, bank=bank, wv=wv: h.matmul(
                                ps[bank][:, :], mh[:, dc, sub * 128:(sub + 1) * 128], wv[:, dc, :],
                                start=(dc == 0), stop=(dc == 15)),
                                reads=[("ring", slot), ("mh", dc)], writes=[("ps", bank)])
                        s.op("dve", lambda h, sub=sub, nb=nb, bank=bank: h.tensor_tensor(
                            xt[:, sub, nb * 512:(nb + 1) * 512], ps[bank][:, :], xt[:, sub, nb * 512:(nb + 1) * 512],
                            ALU.add),
                            reads=[("ps", bank), ("x", sub)], writes=[("x", sub)])
                def nrm(sub):
                    xs = sub % 2
                    self._norm_T(xt[:, sub, :], ("x", sub), gf, "gf", hb[xs], ("hb3", xs), st[xs], ("st3", xs),
                                 ident, psT, mh, sub, part="norm")

                def trn(sub):
                    xs = sub % 2
                    self._norm_T(xt[:, sub, :], ("x", sub), gf, "gf", hb[xs], ("hb3", xs), st[xs], ("st3", xs),
                                 ident, psT, mh, sub, part="tr")
                nrm(0)
                nrm(1)
                trn(0)
                nrm(2)
                trn(1)
                nrm(3)
                trn(2)
                trn(3)
                for fb in range(DFF // 256):
                    slot = wload(lambda h, r, fb=fb: h.dma_start(
                        out=r[:, 0:4096].rearrange("p (k n) -> p k n", k=16),
                        in_=W["WB_gate"][:, fb * 256:(fb + 1) * 256].rearrange("(k p) n -> p k n", p=128)), None, [])
                    s.op("sp", lambda h, slot=slot, fb=fb: h.dma_start(
                        out=ring[slot][:, 4096:8192].rearrange("p (k n) -> p k n", k=16),
                        in_=W["WB_up"][:, fb * 256:(fb + 1) * 256].rearrange("(k p) n -> p k n", p=128)),
                        writes=[("ringb", slot)], reads=[("ring", slot)], dma=True)
                    wg = ring[slot][:, 0:4096].rearrange("p (k n) -> p k n", k=16)
                    wu = ring[slot][:, 4096:8192].rearrange("p (k n) -> p k n", k=16)
                    for fq in range(2):
                        fc = fb * 2 + fq
                        bg = (fc % 2) * 2
                        bu = bg + 1
                        for dc in range(16):
                            s.op("pe", lambda h, dc=dc, fq=fq, bg=bg, wg=wg: h.matmul(
                                ps[bg][:, :], wg[:, dc, fq * 128:(fq + 1) * 128], mh[:, dc, :],
                                start=(dc == 0), stop=(dc == 15)),
                                reads=[("ring", slot), ("mh", dc)] + [("mhT", q) for q in range(4)], writes=[("ps", bg)])
                        for dc in range(16):
                            s.op("pe", lambda h, dc=dc, fq=fq, bu=bu, wu=wu: h.matmul(
                                ps[bu][:, :], wu[:, dc, fq * 128:(fq + 1) * 128], mh[:, dc, :],
                                start=(dc == 0), stop=(dc == 15)),
                                reads=[("ringb", slot), ("mh", dc)] + [("mhT", q) for q in range(4)], writes=[("ps", bu)])
                        tg = tmp[fc % 2]
                        s.op("act", lambda h, tg=tg, bg=bg: h.activation(tg[:, :], ps[bg][:, :], AF.Silu),
                             reads=[("ps", bg)], writes=[("tm", fc % 2)])
                        s.op("dve", lambda h, tg=tg, bu=bu, fc=fc: h.tensor_tensor(
                            act[:, fc, :], ps[bu][:, :], tg[:, :], ALU.mult),
                            reads=[("ps", bu), ("tm", fc % 2)], writes=[("act", fc)])
                pieces = [(0, 16), (16, 16), (32, 12)]
                for nb in range(4):
                    for pi, (f0, nf) in enumerate(pieces):
                        slot = wload(lambda h, r, nb=nb, f0=f0, nf=nf: h.dma_start(
                            out=r[:, 0:nf * 512].rearrange("p (k n) -> p k n", k=nf),
                            in_=W["WB_down"][f0 * 128:(f0 + nf) * 128, nb * 512:(nb + 1) * 512].rearrange(
                                "(k p) n -> p k n", p=128)), None, [])
                        wv = ring[slot][:, 0:nf * 512].rearrange("p (k n) -> p k n", k=nf)
                        for sub in range(4):
                            bank = 4 + sub
                            for k in range(nf):
                                fc = f0 + k
                                s.op("pe", lambda h, fc=fc, k=k, sub=sub, bank=bank, wv=wv: h.matmul(
                                    ps[bank][:, :], act[:, fc, sub * 128:(sub + 1) * 128], wv[:, k, :],
                                    start=(fc == 0), stop=(fc == NF - 1)),
                                    reads=[("ring", slot), ("act", fc)], writes=[("ps", bank)])
                    for sub in range(4):
                        bank = 4 + sub
                        s.op("dve", lambda h, sub=sub, nb=nb, bank=bank: h.tensor_tensor(
                            xt[:, sub, nb * 512:(nb + 1) * 512], ps[bank][:, :], xt[:, sub, nb * 512:(nb + 1) * 512],
                            ALU.add),
                            reads=[("ps", bank), ("x", sub)], writes=[("x", sub)])
                if tt + 1 < self.nt:
                    ntsl = slice((tt + 1) * T, (tt + 2) * T)
                    ab_load(ntsl)
                    sgload(0, ntsl)
                    sgload(1, ntsl)
                for sub in range(4):
                    xs = nhb % 2
                    nhb += 1
                    stt, stk = st[xs], ("st3", xs)
                    xa = xt[:, sub, :]
                    s.op("act", lambda h, xs=xs, xa=xa, stt=stt: h.activation(hb[xs][:, :], xa, AF.Square,
                                                                             accum_out=stt[:, 0:1]),
                         reads=[("x", sub)], writes=[("hb3", xs), stk])
                    s.op("act", lambda h, stt=stt: h.activation(stt[:, 1:2], stt[:, 0:1], AF.Sqrt,
                                                                bias=EPS, scale=1.0 / D),
                         reads=[stk], writes=[(stk, 1)])
                    s.op("dve", lambda h, stt=stt: h.reciprocal(stt[:, 2:3], stt[:, 1:2]),
                         reads=[(stk, 1)], writes=[(stk, 2)])
                    s.op("dve", lambda h, xa=xa, stt=stt: h.scalar_tensor_tensor(
                        xa, xa, stt[:, 2:3], gl[:, :], ALU.mult, ALU.mult),
                        reads=[("x", sub), (stk, 2), "gl"], writes=[("x", sub)])
                    t0 = tt * T + sub * 128
                    o = s.op("pool", lambda h, xa=xa, t0=t0: h.dma_start(out=self.y[t0:t0 + 128, :], in_=xa),
                             reads=[("x", sub)], writes=[("y", t0)], dma=True)
                    self.out_dmas.append(o)
            s.barrier()

    def _norm_T(self, xa, xkey, g, gkey, hbt, hbk, stt, stk, ident, psT, dstT, sub, part="both"):
        s = self.s
        if part in ("both", "norm"):
            s.op("act", lambda h: h.activation(hbt[:, :], xa, AF.Square, accum_out=stt[:, 0:1]),
                 reads=[xkey], writes=[hbk, stk])
            s.op("act", lambda h: h.activation(stt[:, 1:2], stt[:, 0:1], AF.Sqrt, bias=EPS, scale=1.0 / D),
                 reads=[stk], writes=[(stk, 1)])
            s.op("dve", lambda h: h.reciprocal(stt[:, 2:3], stt[:, 1:2]), reads=[(stk, 1)], writes=[(stk, 2)])
            s.op("dve", lambda h: h.scalar_tensor_tensor(hbt[:, :], xa, stt[:, 2:3], g[:, :], ALU.mult, ALU.mult),
                 reads=[xkey, (stk, 2), gkey], writes=[hbk])
        if part == "norm":
            return
        for half in range(2):
            for q in range(8):
                dc = half * 8 + q
                s.op("pe", lambda h, dc=dc, half=half, q=q: h.transpose(
                    psT[half][:, q * 128:(q + 1) * 128], hbt[:, dc * 128:(dc + 1) * 128], ident[:, :]),
                    reads=[hbk, "ident"], writes=[("ps", 6 + half)])
            dstv = dstT[:, half * 8:(half + 1) * 8, sub * 128:(sub + 1) * 128]
            srcv = psT[half].rearrange("p (q t) -> p q t", q=8)
            if half == 0:
                fn = lambda h, dstv=dstv, srcv=srcv: h.copy(dstv, srcv)
                eng = "act"
            else:
                fn = lambda h, dstv=dstv, srcv=srcv: h.tensor_copy(dstv, srcv)
                eng = "dve"
            s.op(eng, fn, reads=[("ps", 6 + half)], writes=[("mhT", sub)] + [("mh", d_) for d_ in range(half * 8, half * 8 + 8)])

def core_inputs(inputs, core):
    if core < 4:
        xc = np.ascontiguousarray(inputs["x_prompt"][core])
    else:
        i = core - 4
        xc = np.ascontiguousarray(inputs["x_sample"][2 * i:2 * i + 2].reshape(NTOK, D))
    m = {"x": xc.astype(np.float32, copy=False)}
    for name, shp in WEIGHT_SPECS:
        m[name] = np.ascontiguousarray(np.asarray(inputs[name], dtype=np.float32).reshape(shp))
    m.update(make_constants(core < 4))
    return m


_NC_CACHE = {}


def kernel(**inputs):
    inputs = {k: np.asarray(v) for k, v in inputs.items()}
    if "nc" not in _NC_CACHE:
        _NC_CACHE["nc"] = Builder().build(phases="0BFHCD")
    nc = _NC_CACHE["nc"]
    in_maps = [core_inputs(inputs, c) for c in range(NCORES)]
    res = run_bass_kernel_spmd(nc, in_maps, core_ids=list(range(NCORES)))
    ys = [r["y"] for r in res.results]
    y_prompt = np.stack(ys[:4], axis=0).astype(np.float32)
    y_sample = np.concatenate([y.reshape(2, 4096, D) for y in ys[4:]], axis=0).astype(np.float32)
    return (y_prompt, y_sample)
```

```python
import math
from contextlib import ExitStack

import numpy as np
import ml_dtypes

import concourse.bass as bass
import concourse.mybir as mybir
from concourse.bass_utils import run_bass_kernel_spmd

F32 = mybir.dt.float32
BF16 = mybir.dt.bfloat16
AF = mybir.ActivationFunctionType
ALU = mybir.AluOpType
AX = mybir.AxisListType

D = 2048
DP = 1024
DH = 1024
DIN = 8192
DFF = 5632
NTOK = 8192
T = 512
NT = NTOK // T
EPS = 1e-6
NCORES = 8

ENGS = ("pe", "act", "dve", "pool", "sp")
NS = 12


class Op:
    __slots__ = ("eng", "fn", "deps", "signal", "dma", "sem", "val", "qidx", "name")

    def __init__(self, eng, fn, dma, name=None):
        self.eng = eng
        self.fn = fn
        self.dma = dma
        self.deps = []
        self.signal = False
        self.sem = None
        self.val = None
        self.qidx = None
        self.name = name


class Sched:
    def __init__(self):
        self.ops = {e: [] for e in ENGS}
        self.lastw = {}
        self.readers = {}
        self.qcount = {e: 0 for e in ENGS}
        self.qops = {e: [] for e in ENGS}

    def op(self, eng, fn, reads=(), writes=(), dma=False, name=None):
        o = Op(eng, fn, dma, name)
        deps = {}

        def add(d):
            if d is None or d is o:
                return
            if (not d.dma) and (not dma) and d.eng == eng and eng == "pe":
                return
            deps[id(d)] = d

        for b in reads:
            add(self.lastw.get(b))
        for b in writes:
            add(self.lastw.get(b))
            rd = self.readers.get(b)
            if rd:
                for r in rd.values():
                    add(r)
        for b in writes:
            self.lastw[b] = o
            self.readers[b] = {}
        for b in reads:
            rd = self.readers.setdefault(b, {})
            if dma:
                rd[("dma", id(o))] = o
            else:
                rd[eng] = o
        if dma:
            i = self.qcount[eng]
            o.qidx = i
            self.qcount[eng] = i + 1
            if i >= NS:
                prev = self.qops[eng][i - NS]
                deps[id(prev)] = prev
            self.qops[eng].append(o)
        o.deps = list(deps.values())
        for d in o.deps:
            d.signal = True
        self.ops[eng].append(o)
        return o

    def barrier(self):
        tails = []
        for e in ENGS:
            comp = [o for o in self.ops[e] if not o.dma and o.fn is not None]
            if comp:
                tails.append(comp[-1])
            tails.extend(self.qops[e][-NS:])
        for t in tails:
            t.signal = True
        for e in ENGS:
            o = Op(e, None, False, "barrier")
            o.deps = list(tails)
            self.ops[e].append(o)
        self.lastw = {}
        self.readers = {}

    def wait_all(self, eng, ops):
        o = Op(eng, None, False, "waitall")
        o.deps = list(ops)
        for d in o.deps:
            d.signal = True
        self.ops[eng].append(o)

    def emit(self, block, sems, dsems):
        for e in ENGS:
            c = 0
            for o in self.ops[e]:
                if o.dma:
                    o.sem = dsems[e][o.qidx % NS]
                    o.val = 16 * (o.qidx // NS + 1)
                elif o.signal and o.fn is not None:
                    c += 1
                    o.sem = sems[e]
                    o.val = c
            assert c < 60000, (e, c)
        handles = {"pe": block.tensor, "act": block.scalar, "dve": block.vector,
                   "pool": block.gpsimd, "sp": block.sync}
        for e in ENGS:
            ops = self.ops[e]

            def body(h, ops=ops):
                waited = {}
                for o in ops:
                    for d in o.deps:
                        k = id(d.sem)
                        if waited.get(k, 0) < d.val:
                            h.wait_ge(d.sem, d.val)
                            waited[k] = d.val
                    if o.fn is None:
                        continue
                    ins = o.fn(h)
                    if o.dma:
                        ins.then_inc(o.sem, 16)
                    elif o.signal:
                        ins.then_inc(o.sem, 1)

            handles[e](body)


def _bf(a):
    return np.ascontiguousarray(a.astype(ml_dtypes.bfloat16))


def make_constants(is_prompt):
    c = {}
    if is_prompt:
        L, nseq = 8192, 1
    else:
        L, nseq = 4096, 2
    N = 2 * L
    J = L // 128
    N1 = N // 128
    K1 = N1 // 2
    A = np.zeros((64, 2, 64), np.float64)
    for sq in range(nseq):
        j = np.arange(J)[:, None]
        k1 = np.arange(K1)[None, :]
        ang = -2 * np.pi * j * (k1 + 0.5) / N1
        A[sq * J:(sq + 1) * J, 0, sq * K1:(sq + 1) * K1] = np.cos(ang)
        A[sq * J:(sq + 1) * J, 1, sq * K1:(sq + 1) * K1] = np.sin(ang)
    c["dftA"] = _bf(A.reshape(64, 128))
    p = np.arange(128)[:, None]
    k2 = np.arange(128)[None, :]
    MB = np.zeros((128, 64, 3, 128), np.float64)
    for kc in range(64):
        k1p = kc % K1
        ang = -2 * np.pi * p * (k1p + N1 * k2 + 0.5) / N
        MB[:, kc, 0, :] = np.cos(ang)
        MB[:, kc, 1, :] = np.sin(ang)
        MB[:, kc, 2, :] = -np.sin(ang)
    c["dftB"] = _bf(MB.reshape(128, 64 * 3 * 128))
    ang = 2 * np.pi * np.arange(128)[:, None] * np.arange(128)[None, :] / 128
    c["dftF"] = _bf(np.concatenate([np.cos(ang), np.sin(ang)], axis=1))
    G1 = np.zeros((128, 128, 64), np.float64)
    G2 = np.zeros((128, 128, 64), np.float64)
    for sq in range(nseq):
        k1 = np.arange(K1)[:, None, None]
        pp = np.arange(128)[None, :, None]
        jj = np.arange(J)[None, None, :]
        ang = 2 * np.pi * (k1 + 0.5) * (pp + 128 * jj) / N
        Gr = np.cos(ang) * (2.0 / N)
        Gi = np.sin(ang) * (2.0 / N)
        r0 = sq * K1
        j0 = sq * J
        G1[r0:r0 + K1, :, j0:j0 + J] = Gr
        G1[64 + r0:64 + r0 + K1, :, j0:j0 + J] = -Gi
        G2[r0:r0 + K1, :, j0:j0 + J] = -Gi
        G2[64 + r0:64 + r0 + K1, :, j0:j0 + J] = -Gr
    c["dftG"] = _bf(np.stack([G1, G2], axis=2).reshape(128, 128 * 2 * 64))
    t = np.linspace(0.0, 1.0, L, dtype=np.float32)
    wt = (2.0 * np.float32(math.pi) * np.arange(L, dtype=np.float32) / np.float32(L)).astype(np.float32)
    bands = np.linspace(1e-4, 15, 16, dtype=np.float32)[None, :]
    z = np.concatenate([t[:, None], np.cos(bands * wt[:, None]), -np.sin(bands * wt[:, None])], axis=-1)
    z = np.tile(z.astype(np.float32), (nseq, 1))
    c["zfeatT"] = np.ascontiguousarray(z.T)
    c["tvals"] = np.ascontiguousarray(np.tile(t, nseq)[None, :])
    inv = np.zeros((4, NTOK), np.float32)
    tt = np.arange(L)
    for g, w in enumerate((2, 4, 8, 16)):
        lo = np.clip(tt - w // 2, 0, L)
        hi = np.clip(tt + (w - w // 2), 0, L)
        inv[g] = np.tile(1.0 / (hi - lo).astype(np.float32), nseq)
    c["invcnt"] = inv
    hm = np.ones((NT, 2), np.float32)
    for i in range(NT):
        t0 = i * T
        if t0 % L == 0:
            hm[i, 0] = 0.0
        if (t0 + T) % L == 0:
            hm[i, 1] = 0.0
    c["halomask"] = np.ascontiguousarray(np.broadcast_to(hm.reshape(1, NT * 2), (128, NT * 2)))
    c["ident"] = _bf(np.eye(128))
    misc = np.zeros((128, 16), np.float32)
    misc[:, 0] = 1.0 if is_prompt else 0.5
    max_decay = math.log(1e-2) / 0.3
    min_decay = math.log(1e-2) / 1.5
    deltas = np.abs(np.linspace(min_decay, max_decay, DH, dtype=np.float32))
    misc[:, 1:9] = -deltas.reshape(8, 128).T
    misc[:, 9] = 0.0 if is_prompt else 1.0
    c["misc"] = misc
    return c


CONST_SPECS = {
    "dftA": ([64, 128], BF16), "dftB": ([128, 64 * 3 * 128], BF16), "dftF": ([128, 256], BF16),
    "dftG": ([128, 128 * 2 * 64], BF16), "zfeatT": ([33, NTOK], F32), "tvals": ([1, NTOK], F32),
    "invcnt": ([4, NTOK], F32), "halomask": ([128, NT * 2], F32), "ident": ([128, 128], BF16),
    "misc": ([128, 16], F32),
}

WEIGHT_SPECS = [
    ("g_mix", [1, D]), ("w_in", [D, DIN]), ("pool_w", [4 * 256, 256]), ("pool_scale", [1, DP]),
    ("conv_w", [3, 3 * DH]), ("conv_b", [1, 3 * DH]), ("filt_w1", [33, 64]), ("filt_b1", [1, 64]),
    ("filt_freq1", [1, 64]), ("filt_w2", [64, 64]), ("filt_b2", [1, 64]), ("filt_freq2", [1, 64]),
    ("filt_w3", [64, 4 * DH]), ("hyena_bias", [2, DH]), ("w_branch_a", [DP, D]),
    ("w_branch_b", [DH, D]), ("w_out", [D, D]), ("g_ffn", [1, D]), ("w_gate", [D, DFF]),
    ("w_up", [D, DFF]), ("w_down", [DFF, D]), ("g_final", [1, D]),
]


class Builder:
    def __init__(self, debug=(), nt=NT, ncg=16):
        self.debug = set(debug)
        self.nt = nt
        self.ncg = ncg
        self._phase_id = 0
        self.nc = bass.Bass("TRN2", target_bir_lowering=False)
        self.s = Sched()
        nc = self.nc
        self.x = nc.dram_tensor("x", [NTOK, D], F32, kind="ExternalInput").ap()
        self.y = nc.dram_tensor("y", [NTOK, D], F32, kind="ExternalOutput").ap()
        self.w = {}
        for name, shp in WEIGHT_SPECS:
            self.w[name] = nc.dram_tensor(name, shp, F32, kind="ExternalInput").ap()
        self.c = {}
        for name, (shp, dt) in CONST_SPECS.items():
            self.c[name] = nc.dram_tensor(name, shp, dt, kind="ExternalInput").ap()
        self.scr = {}

    def _uid(self):
        return self._phase_id

    def scratch(self, name, shape, dt):
        kind = "ExternalOutput" if name in self.debug else "Internal"
        t = self.nc.dram_tensor(name, shape, dt, kind=kind).ap()
        self.scr[name] = t
        return t

    def build(self, phases):
        nc = self.nc
        s = self.s
        with ExitStack() as es:
            sems = {e: es.enter_context(nc.semaphore("s_" + e)) for e in ENGS}
            dsems = {e: [es.enter_context(nc.semaphore(f"d_{e}_{i}")) for i in range(NS)]
                     for e in ENGS}
            self.ps = [es.enter_context(nc.psum_tensor(f"ps{i}", [128, 512], F32)) for i in range(8)]
            self.es = es
            self.scratch("WB_in", [D, DIN], BF16)
            self.scratch("WB_a", [DP, D], BF16)
            self.scratch("WB_b", [DH, D], BF16)
            self.scratch("WB_out", [D, D], BF16)
            self.scratch("WB_gate", [D, DFF], BF16)
            self.scratch("WB_up", [D, DFF], BF16)
            self.scratch("WB_down", [DFF, D], BF16)
            self.scratch("PU", [DP, NTOK], F32)
            self.scratch("HV", [3 * DH, NTOK], F32)
            self.scratch("SG", [2 * D, NTOK], BF16)
            self.out_dmas = []
            if "0" in phases:
                self.phase0()
            self.scratch("AT", [DP, NTOK], BF16)
            self.scratch("HC", [3 * DH, NTOK], BF16)
            self.scratch("FT", [4 * DH, NTOK], BF16)
            self.scratch("HS", [2, 16, 128, 2 * 64 * 64], BF16)
            self.scratch("BT", [DH, NTOK], BF16)
            if "dbgS" in self.debug:
                for nm in ("dbgS", "dbgY", "dbgP0", "dbgP1"):
                    self.debug.add(nm)
                    self.scratch(nm, [128, 8192], BF16)
                self.debug.add("dbgZ1")
                self.scratch("dbgZ1", [64, 8192], BF16)
            if "B" in phases:
                self.phase1()
            if "F" in phases:
                self.phase2a()
            if "H" in phases:
                self.phase2b()
            if "C" in phases:
                self.phase2c()
            if "D" in phases:
                self.phase3()
            s.barrier()
            s.wait_all("sp", self.out_dmas)
            with nc.Block() as block:
                s.emit(block, sems, dsems)
        return nc

    def phase0(self):
        s = self.s
        for wn, sn in (("w_in", "WB_in"), ("w_branch_a", "WB_a"), ("w_branch_b", "WB_b"), ("w_out", "WB_out"),
                       ("w_gate", "WB_gate"), ("w_up", "WB_up"), ("w_down", "WB_down")):
            src = self.w[wn]
            dst = self.scr[sn]
            for r in range(src.shape[0] // 128):
                s.op("pool", lambda h, r=r, src=src, dst=dst: h.dma_start(
                    out=dst[r * 128:(r + 1) * 128, :], in_=src[r * 128:(r + 1) * 128, :]),
                    reads=[], writes=[(sn, r)], dma=True)

    def phase1(self):
        self._phase_id += 1
        nc, s = self.nc, self.s
        with ExitStack() as ph:
            def sb(name, shape, dt):
                return ph.enter_context(nc.sbuf_tensor(f"sb{self._uid()}_" + name, shape, dt))
            self._p1_alloc(sb)
            for tt in range(self.nt):
                self._p1_B(tt, self._p1_Cfront_steps(tt - 2) if tt >= 2 else ())
                if tt >= 2:
                    self._p1_Cback(tt - 2)
            for tt in range(max(self.nt - 2, 0), self.nt):
                self._p1_Cfront(tt)
                self._p1_Cback(tt)
            s.barrier()

    def _p1_alloc(self, sb):
        s = self.s
        self.gbc = sb("gbc", [128, D], F32)
        self.ident = sb("ident", [128, 128], BF16)
        self.xt = [sb(f"xt{i}", [128, D], F32) for i in range(2)]
        self.hb = [sb(f"hb{i}", [128, D], BF16) for i in range(2)]
        self.st = [sb(f"st{i}", [128, 4], F32) for i in range(2)]
        self.hT = [sb(f"hT{i}", [128, 16, T], BF16) for i in range(2)]
        self.wb = [sb(f"wb{i}", [128, 16, 512], BF16) for i in range(2)]
        self.ev32 = [sb(f"ev32_{i}", [128, 512], F32) for i in range(4)]
        self.ev16 = [sb(f"ev16_{i}", [128, 512], BF16) for i in range(4)]
        self.CV = [sb(f"CV{i}", [128, T + 16], F32) for i in range(3)]
        self.lv = [sb(f"lv{i}", [128, T + 16], F32) for i in range(4)]
        self.pooled = [sb(f"pooled{i}", [128, T], BF16) for i in range(16)]
        self.icnt = [sb(f"icnt{i}", [128, 4, T], F32) for i in range(2)]
        self.hmask = sb("hmask", [128, NT * 2], F32)
        self.poolw = sb("poolw", [128, 8, 256], BF16)
        self.pscale = sb("pscale", [128, 8], F32)
        self.aev = [sb(f"aev{i}", [128, T], BF16) for i in range(2)]
        self.cacc = [sb(f"cacc{i}", [128, T], F32) for i in range(3)]
        self.cout = [sb(f"cout{i}", [128, T], BF16) for i in range(3)]
        self.convw = sb("convw", [128, 3, 24], F32)
        self.convb = sb("convb", [128, 24], F32)
        self.cnt = {"ev": 0, "sub": 0, "u": 0, "pl": 0, "ae": 0, "v": 0, "cv": 0}
        nc = self.nc
        s.op("sp", lambda h: h.dma_start(out=self.gbc[:, :], in_=self.w["g_mix"].partition_broadcast(128)),
             writes=["gbc"], dma=True)
        s.op("sp", lambda h: h.dma_start(out=self.ident[:, :], in_=self.c["ident"]), writes=["ident"], dma=True)
        s.op("sp", lambda h: h.dma_start(out=self.hmask[:, :], in_=self.c["halomask"]), writes=["hmask"], dma=True)
        s.op("pool", lambda h: h.dma_start(out=self.poolw[:, :, :],
                                           in_=self.w["pool_w"].rearrange("(a p) n -> p a n", p=128)),
             writes=["poolw"], dma=True)
        s.op("sp", lambda h: h.dma_start(out=self.pscale[:, :],
                                         in_=self.w["pool_scale"].rearrange("o (c p) -> p (o c)", p=128),
                                         allow_slow_non_contiguous=True),
             writes=["pscale"], dma=True)
        for k in range(3):
            s.op("sp", lambda h, k=k: h.dma_start(
                out=self.convw[:, k, :],
                in_=self.w["conv_w"][k:k + 1, :].rearrange("o (c p) -> p (o c)", p=128),
                allow_slow_non_contiguous=True),
                writes=["convw"], dma=True)
        s.op("sp", lambda h: h.dma_start(out=self.convb[:, :],
                                         in_=self.w["conv_b"].rearrange("o (c p) -> p (o c)", p=128),
                                         allow_slow_non_contiguous=True),
             writes=["convb"], dma=True)

    def _p1_xload(self, tt, sub):
        s = self.s
        g = tt * 4 + sub
        xs = g % 2
        t0 = tt * T + sub * 128
        s.op("sp", lambda h: h.dma_start(out=self.xt[xs][:, :], in_=self.x[t0:t0 + 128, :]),
             writes=[("xt", xs)], dma=True)

    def _p1_N(self, tt, sub, part="both"):
        s, ps = self.s, self.ps
        xt, hb, st, hT, gbc, ident = self.xt, self.hb, self.st, self.hT, self.gbc, self.ident
        psT = [ps[6][:, :].bitcast(BF16), ps[7][:, :].bitcast(BF16)]
        hs = tt % 2
        xs = (tt * 4 + sub) % 2
        if part in ("both", "norm"):
            s.op("act", lambda h: h.activation(hb[xs][:, :], xt[xs][:, :], AF.Square, accum_out=st[xs][:, 0:1]),
                 reads=[("xt", xs)], writes=[("hb", xs), ("st", xs)])
            s.op("act", lambda h: h.activation(st[xs][:, 1:2], st[xs][:, 0:1], AF.Sqrt, bias=EPS, scale=1.0 / D),
                 reads=[("st", xs)], writes=[("st1", xs)])
            s.op("dve", lambda h: h.reciprocal(st[xs][:, 2:3], st[xs][:, 1:2]),
                 reads=[("st1", xs)], writes=[("st2", xs)])
            s.op("dve", lambda h: h.scalar_tensor_tensor(
                hb[xs][:, :], xt[xs][:, :], st[xs][:, 2:3], gbc[:, :], ALU.mult, ALU.mult),
                reads=[("xt", xs), ("st2", xs), "gbc"], writes=[("hb", xs)])
        if part == "norm":
            return
        for half in range(2):
            for q in range(8):
                dc = half * 8 + q
                s.op("pe", lambda h, dc=dc, half=half, q=q: h.transpose(
                    psT[half][:, q * 128:(q + 1) * 128], hb[xs][:, dc * 128:(dc + 1) * 128], ident[:, :]),
                    reads=[("hb", xs), "ident"], writes=[("ps", 6 + half)])
            dstv = hT[hs][:, half * 8:(half + 1) * 8, sub * 128:(sub + 1) * 128]
            srcv = psT[half].rearrange("p (q t) -> p q t", q=8)
            s.op("dve", lambda h, dstv=dstv, srcv=srcv: h.tensor_copy(dstv, srcv),
                 reads=[("ps", 6 + half)], writes=[("hT", hs, sub)])

    def _p1_wload(self, w):
        s = self.s
        if w >= self.nt * 16:
            return
        ob = w % 16
        ws = w % 2
        WB = self.scr["WB_in"].rearrange("(dc p) n -> p dc n", p=128)
        s.op("sp", lambda h: h.dma_start(out=self.wb[ws][:, :, :], in_=WB[:, :, ob * 512:(ob + 1) * 512]),
             reads=[("WB_in", r) for r in range(16)], writes=[("wb", ws)], dma=True)

    def _p1_B(self, tt, csteps=()):
        s = self.s
        csteps = list(csteps)
        hT, wb, ev32, ev16, ps = self.hT, self.wb, self.ev32, self.ev16, self.ps
        hs = tt % 2
        if tt == 0:
            self._p1_wload(0)
            self._p1_xload(0, 0)
            self._p1_xload(0, 1)
            for sub in range(4):
                self._p1_N(0, sub)
                if sub + 2 < 4:
                    self._p1_xload(0, sub + 2)
        for ob in range(DIN // 512):
            w = tt * 16 + ob
            ws = w % 2
            self._p1_wload(w + 1)
            nsteps = 3 if ob == 0 else 2
            for _ in range(nsteps):
                if csteps:
                    csteps.pop(0)()
            if ob == 15:
                while csteps:
                    csteps.pop(0)()
            if tt + 1 < self.nt:
                for sub in range(4):
                    if ob == max(3 * sub - 1, 0) and sub < 2 or ob == 3 * sub - 3 and sub >= 2:
                        self._p1_xload(tt + 1, sub)
                    if ob == 1 + 3 * sub:
                        self._p1_N(tt + 1, sub, part="norm")
                    if ob == 3 + 3 * sub:
                        self._p1_N(tt + 1, sub, part="tr")
            for oc in range(4):
                col0 = ob * 512 + oc * 128
                ev_i = self.cnt["ev"]
                self.cnt["ev"] += 1
                bank = ev_i % 4
                for dc in range(16):
                    s.op("pe", lambda h, ws=ws, oc=oc, dc=dc, bank=bank, hs=hs: h.matmul(
                        ps[bank][:, :], wb[ws][:, dc, oc * 128:(oc + 1) * 128], hT[hs][:, dc, :],
                        start=(dc == 0), stop=(dc == 15)),
                        reads=[("wb", ws)] + [("hT", hs, q) for q in range(4)],
                        writes=[("ps", bank)])
                e = ev_i % 4
                if col0 < DP + 3 * DH:
                    s.op("act", lambda h, e=e, bank=bank: h.copy(ev32[e][:, :], ps[bank][:, :]),
                         reads=[("ps", bank)], writes=[("ev32", e)])
                    if col0 < DP:
                        dst = self.scr["PU"][col0:col0 + 128, tt * T:(tt + 1) * T]
                        key = ("PU", col0 // 128, tt)
                    else:
                        r0 = col0 - DP
                        dst = self.scr["HV"][r0:r0 + 128, tt * T:(tt + 1) * T]
                        key = ("HV", r0 // 128, tt)
                    s.op("act", lambda h, e=e, dst=dst: h.dma_start(out=dst, in_=ev32[e][:, :]),
                         reads=[("ev32", e)], writes=[key], dma=True)
                else:
                    r0 = col0 - (DP + 3 * DH)
                    s.op("act", lambda h, e=e, bank=bank: h.activation(ev16[e][:, :], ps[bank][:, :], AF.Sigmoid),
                         reads=[("ps", bank)], writes=[("ev16", e)])
                    dst = self.scr["SG"][r0:r0 + 128, tt * T:(tt + 1) * T]
                    s.op("act", lambda h, e=e, dst=dst: h.dma_start(out=dst, in_=ev16[e][:, :]),
                         reads=[("ev16", e)], writes=[("SG", r0 // 128, tt)], dma=True)

    def _load_halo(self, buf, bufkey, src, srckey, row0, tt, halo, eng_memset="pool"):
        s = self.s
        nt = self.nt
        lo = tt * T - halo
        hi = tt * T + T + halo
        clo, chi = max(lo, 0), min(hi, nt * T)
        if clo > lo:
            s.op(eng_memset, lambda h: h.memset(buf[:, 0:halo], 0.0), writes=[bufkey])
        if chi < hi:
            s.op(eng_memset, lambda h: h.memset(buf[:, T + halo:T + 2 * halo], 0.0), writes=[bufkey])
        rk = [(srckey, row0 // 128, q) for q in (tt - 1, tt, tt + 1) if 0 <= q < nt]
        s.op("pool", lambda h: h.dma_start(out=buf[:, clo - lo:chi - lo], in_=src[row0:row0 + 128, clo:chi]),
             reads=rk, writes=[bufkey], dma=True)
        hm = self.hmask
        if clo == lo:
            s.op(eng_memset, lambda h: h.tensor_scalar(buf[:, 0:halo], buf[:, 0:halo], hm[:, 2 * tt:2 * tt + 1],
                                                       None, ALU.mult),
                 reads=["hmask"], writes=[bufkey])
        if chi == hi:
            s.op(eng_memset, lambda h: h.tensor_scalar(buf[:, T + halo:T + 2 * halo], buf[:, T + halo:T + 2 * halo],
                                                       hm[:, 2 * tt + 1:2 * tt + 2], None, ALU.mult),
                 reads=["hmask"], writes=[bufkey])

    def _p1_Cfront(self, tt):
        for st_ in self._p1_Cfront_steps(tt):
            st_()

    def _p1_Cfront_steps(self, tt):
        s = self.s
        steps = []
        CV, lv, pooled, icnt = self.CV, self.lv, self.pooled, self.icnt
        ics = tt % 2
        cw, cb = self.convw, self.convb
        jobs = [("p", k) for k in range(8)] + [("c", k) for k in range(24)]
        base = self.cnt["cv"]
        self.cnt["cv"] += len(jobs)

        def slot(n):
            return (base + n) % 3

        def load(n):
            if n >= len(jobs):
                return
            kind, k = jobs[n]
            sl = slot(n)
            if kind == "p":
                self._load_halo(CV[sl], ("CV", sl), self.scr["PU"], "PU", k * 128, tt, 8)
            else:
                self._load_halo(CV[sl], ("CV", sl), self.scr["HV"], "HV", k * 128, tt, 1)

        def init():
            s.op("pool", lambda h: h.dma_start(
                out=icnt[ics][:, :, :],
                in_=self.c["invcnt"][:, tt * T:(tt + 1) * T].partition_broadcast(128)),
                writes=[("icnt", ics)], dma=True)
            load(0)
            load(1)
        steps.append(init)

        def job(n, kind, k):
            sl = slot(n)
            buf = CV[sl]
            bk = ("CV", sl)
            if kind == "p":
                g = k // 2
                cur, curkey = buf, bk
                rngs = [(1, T + 16, 1, 0), (2, T + 15, 1, 1), (4, T + 13, 2, 2), (8, T + 9, 4, 4)]
                for lvl in range(g + 1):
                    a, b, dl, dr = rngs[lvl]
                    dst = lv[lvl]
                    s.op("dve", lambda h, dst=dst, cur=cur, a=a, b=b, dl=dl, dr=dr: h.tensor_tensor(
                        dst[:, a:b], cur[:, a - dl:b - dl], cur[:, a + dr:b + dr], ALU.add),
                        reads=[curkey], writes=[("lv", lvl)])
                    cur, curkey = dst, ("lv", lvl)
                tmp = lv[g]
                s.op("dve", lambda h, tmp=tmp, cur=cur, g=g: h.tensor_tensor(
                    tmp[:, 8:T + 8], cur[:, 8:T + 8], icnt[ics][:, g, :], ALU.mult),
                    reads=[curkey, ("icnt", ics)], writes=[("lv", g)])
                pl = (tt % 2) * 8 + k
                s.op("dve", lambda h, tmp=tmp, buf=buf, pl=pl: h.tensor_tensor(
                    pooled[pl][:, :], tmp[:, 8:T + 8], buf[:, 8:T + 8], ALU.subtract),
                    reads=[("lv", g), bk], writes=[("pooled", pl)])
            else:
                hc = k
                vs = n % 3
                acc, co_ = self.cacc[vs], self.cout[vs]
                s.op("dve", lambda h, buf=buf, acc=acc, hc=hc: h.tensor_scalar(
                    acc[:, :], buf[:, 1:T + 1], cw[:, 1, hc:hc + 1], cb[:, hc:hc + 1], ALU.mult, ALU.add),
                    reads=[bk, "convw", "convb"], writes=[("cacc", vs)])
                s.op("dve", lambda h, buf=buf, acc=acc, hc=hc: h.scalar_tensor_tensor(
                    acc[:, :], buf[:, 0:T], cw[:, 0, hc:hc + 1], acc[:, :], ALU.mult, ALU.add),
                    reads=[bk, "convw"], writes=[("cacc", vs)])
                s.op("dve", lambda h, buf=buf, acc=acc, co_=co_, hc=hc: h.scalar_tensor_tensor(
                    co_[:, :], buf[:, 2:T + 2], cw[:, 2, hc:hc + 1], acc[:, :], ALU.mult, ALU.add),
                    reads=[bk, "convw", ("cacc", vs)], writes=[("cout", vs)])
                dst = self.scr["HC"][hc * 128:(hc + 1) * 128, tt * T:(tt + 1) * T]
                s.op("pool", lambda h, co_=co_, dst=dst: h.dma_start(out=dst, in_=co_[:, :]),
                     reads=[("cout", vs)], writes=[("HC", hc, tt)], dma=True)
            load(n + 2)
        for n, (kind, k) in enumerate(jobs):
            steps.append(lambda n=n, kind=kind, k=k: job(n, kind, k))
        return steps

    def _p1_Cback(self, tt):
        s, ps = self.s, self.ps
        pooled = self.pooled
        for g in range(4):
            pls = [(tt % 2) * 8 + 2 * g + kc for kc in range(2)]
            for oc in range(2):
                bank = 4 + (oc % 2)
                for kc in range(2):
                    s.op("pe", lambda h, g=g, kc=kc, oc=oc, bank=bank, pl=pls[kc]: h.matmul(
                        ps[bank][:, :], self.poolw[:, 2 * g + kc, oc * 128:(oc + 1) * 128], pooled[pl][:, :],
                        start=(kc == 0), stop=(kc == 1)),
                        reads=["poolw", ("pooled", pls[kc])], writes=[("ps", bank)])
                ae = self.cnt["ae"] % 2
                self.cnt["ae"] += 1
                co = 2 * g + oc
                s.op("act", lambda h, ae=ae, bank=bank, co=co: h.activation(
                    self.aev[ae][:, :], ps[bank][:, :], AF.Copy, scale=self.pscale[:, co:co + 1]),
                    reads=[("ps", bank), "pscale"], writes=[("aev", ae)])
                dst = self.scr["AT"][co * 128:(co + 1) * 128, tt * T:(tt + 1) * T]
                s.op("act", lambda h, ae=ae, dst=dst: h.dma_start(out=dst, in_=self.aev[ae][:, :]),
                     reads=[("aev", ae)], writes=[("AT", co, tt)], dma=True)

    def phase2a(self):
        self._phase_id += 1
        nc, s, ps = self.nc, self.s, self.ps
        TWO_PI = 2.0 * math.pi
        MAGIC = 12582912.0
        with ExitStack() as ph:
            def sb(name, shape, dt):
                return ph.enter_context(nc.sbuf_tensor(f"sb{self._uid()}_" + name, shape, dt))
            zf = [sb(f"zf{i}", [33, 512], F32) for i in range(2)]
            w1 = sb("fw1", [33, 64], F32)
            w2 = sb("fw2", [64, 64], F32)
            fcol = sb("fcol", [64, 8], F32)
            w3b = sb("w3b", [64, 4 * DH], BF16)
            h2T = sb("h2T", [64, NTOK], BF16)
            misc = sb("misc2a", [128, 16], F32)
            hbias = sb("hbias", [128, 16], F32)
            tch = [sb(f"tch{i}", [128, 2048], F32) for i in range(2)]
            dec = sb("dec", [128, NTOK], F32)
            fwb = [[sb(f"fwb{i}_{j}", [128, NTOK], BF16) for j in range(2)] for i in range(2)]
            junk = sb("junk", [128, 2048], BF16)
            fo = [sb(f"fo{i}", [128, 1024], BF16) for i in range(4)]
            pmt = [sb(f"pmt{i}", [128, 1024], F32) for i in range(4)]
            mt = [sb(f"mt{i}", [64, 512], F32) for i in range(8)]
            sm = [sb(f"sm{i}", [128, 16], F32) for i in range(2)]

            s.op("sp", lambda h: h.dma_start(out=w1[:, :], in_=self.w["filt_w1"]), writes=["fw1"], dma=True)
            s.op("sp", lambda h: h.dma_start(out=w2[:, :], in_=self.w["filt_w2"]), writes=["fw2"], dma=True)
            for i, nm in enumerate(("filt_b1", "filt_freq1", "filt_b2", "filt_freq2")):
                s.op("sp", lambda h, i=i, nm=nm: h.dma_start(
                    out=fcol[:, i:i + 1], in_=self.w[nm].rearrange("o (c p) -> p (o c)", p=64),
                    allow_slow_non_contiguous=True), writes=["fcol"], dma=True)
            s.op("pool", lambda h: h.dma_start(out=w3b[:, :], in_=self.w["filt_w3"]), writes=["w3b"], dma=True)
            s.op("sp", lambda h: h.dma_start(out=misc[:, :], in_=self.c["misc"]), writes=["misc"], dma=True)
            for o in range(2):
                s.op("sp", lambda h, o=o: h.dma_start(
                    out=hbias[:, o * 8:(o + 1) * 8],
                    in_=self.w["hyena_bias"][o:o + 1, :].rearrange("o (c p) -> p (o c)", p=128),
                    allow_slow_non_contiguous=True), writes=["hbias"], dma=True)
            s.op("dve", lambda h: h.tensor_tensor(fcol[:, 4:5], fcol[:, 0:1], fcol[:, 1:2], ALU.mult),
                 reads=["fcol"], writes=["fcol"])
            s.op("dve", lambda h: h.tensor_tensor(fcol[:, 5:6], fcol[:, 2:3], fcol[:, 3:4], ALU.mult),
                 reads=["fcol"], writes=["fcol"])

            def sin_layer(psb, fi, out_ap, outkey, ms):
                u, k, r = mt[ms * 4 + 0], mt[ms * 4 + 1], mt[ms * 4 + 2]
                ku, kk, kr = ("mt", ms, 0), ("mt", ms, 1), ("mt", ms, 2)
                s.op("dve", lambda h: h.tensor_scalar(u[:, :], ps[psb][0:64, :], fcol[:, fi:fi + 1],
                                                      fcol[:, 4 + fi // 2:5 + fi // 2], ALU.mult, ALU.add),
                     reads=[("ps", psb), "fcol"], writes=[ku])
                s.op("dve", lambda h: h.tensor_scalar(k[:, :], u[:, :], 1.0 / TWO_PI, MAGIC, ALU.mult, ALU.add),
                     reads=[ku], writes=[kk])
                s.op("dve", lambda h: h.tensor_scalar(k[:, :], k[:, :], -MAGIC, -TWO_PI, ALU.add, ALU.mult),
                     reads=[kk], writes=[kk])
                s.op("dve", lambda h: h.tensor_tensor(r[:, :], k[:, :], u[:, :], ALU.add),
                     reads=[ku, kk], writes=[kr])
                s.op("dve", lambda h: h.tensor_scalar(r[:, :], r[:, :], -math.pi, math.pi, ALU.max, ALU.min),
                     reads=[kr], writes=[kr])
                s.op("act", lambda h: h.activation(out_ap, r[:, :], AF.Sin), reads=[kr], writes=[outkey])

            nchk = NTOK // 512

            def mlp_l1(ch):
                ms = ch % 2
                zs = ch % 2
                b1_ = (0, 6)[ms]
                s.op("sp", lambda h: h.dma_start(out=zf[zs][:, :], in_=self.c["zfeatT"][:, ch * 512:(ch + 1) * 512]),
                     writes=[("zf", zs)], dma=True)
                s.op("pe", lambda h: h.matmul(ps[b1_][0:64, :], w1[:, :], zf[zs][:, :], start=True, stop=True),
                     reads=["fw1", ("zf", zs)], writes=[("ps", b1_)])
                sin_layer(b1_, 1, mt[ms * 4 + 3][:, :], ("mt", ms, 3), ms)

            def mlp_l2(ch):
                ms = ch % 2
                b2_ = (1, 7)[ms]
                s.op("pe", lambda h: h.matmul(ps[b2_][0:64, :], w2[:, :], mt[ms * 4 + 3][:, :], start=True, stop=True),
                     reads=["fw2", ("mt", ms, 3)], writes=[("ps", b2_)])
                sin_layer(b2_, 3, h2T[:, ch * 512:(ch + 1) * 512], "h2T", ms)

            mlp_l1(0)
            for ch in range(nchk):
                if ch + 1 < nchk:
                    mlp_l1(ch + 1)
                mlp_l2(ch)

            its = [(cc, o) for cc in range(8) for o in range(2)]
            cnt = {"fo": 0, "pc": 0, "tq": 0}

            def do_dec(cc):
                for q in range(4):
                    ts_ = cnt["tq"] % 2
                    cnt["tq"] += 1
                    s.op("sp", lambda h, ts_=ts_, q=q: h.dma_start(
                        out=tch[ts_][:, :], in_=self.c["tvals"][:, q * 2048:(q + 1) * 2048].partition_broadcast(128)),
                        writes=[("tch", ts_)], dma=True)
                    s.op("act", lambda h, ts_=ts_, q=q: h.activation(
                        dec[:, q * 2048:(q + 1) * 2048], tch[ts_][:, :], AF.Exp, scale=misc[:, 1 + cc:2 + cc]),
                        reads=[("tch", ts_), "misc"], writes=["dec"])

            def mults(it):
                cc, o = its[it]
                ss = it % 2
                fw = fwb[ss]
                for dr in range(2):
                    r0 = o * 2 * DH + dr * DH + cc * 128
                    for ch in range(nchk):
                        bank = 2 + ch % 4
                        s.op("pe", lambda h, r0=r0, ch=ch, bank=bank: h.matmul(
                            ps[bank][:, :], w3b[:, r0:r0 + 128], h2T[:, ch * 512:(ch + 1) * 512],
                            start=True, stop=True),
                            reads=["w3b", "h2T"], writes=[("ps", bank)])
                        s.op("dve", lambda h, dr=dr, ch=ch, bank=bank: h.tensor_tensor(
                            fw[dr][:, ch * 512:(ch + 1) * 512], ps[bank][:, :], dec[:, ch * 512:(ch + 1) * 512],
                            ALU.mult),
                            reads=[("ps", bank), "dec"], writes=[("fwb", ss, dr)])

            def tail(it):
                cc, o = its[it]
                ss = it % 2
                fw = fwb[ss]
                smt = sm[ss]
                for dr in range(2):
                    for q in range(4):
                        s.op("act", lambda h, dr=dr, q=q: h.activation(
                            junk[:, :], fw[dr][:, q * 2048:(q + 1) * 2048], AF.Square,
                            accum_out=smt[:, 8 + dr * 4 + q:9 + dr * 4 + q]),
                            reads=[("fwb", ss, dr)], writes=["junk", ("sq", ss)])
                s.op("dve", lambda h: h.tensor_reduce(smt[:, 2:3], smt[:, 8:16], AX.X, ALU.add),
                     reads=[("sq", ss)], writes=[("sm2", ss)])
                s.op("dve", lambda h: h.tensor_tensor(smt[:, 3:4], fw[0][:, 0:1], fw[1][:, 0:1], ALU.mult),
                     reads=[("fwb", ss, 0), ("fwb", ss, 1)], writes=[("sm3", ss)])
                s.op("dve", lambda h: h.tensor_scalar(smt[:, 2:3], smt[:, 2:3], misc[:, 0:1], None, ALU.mult),
                     reads=[("sm2", ss), "misc"], writes=[("sm2", ss)])
                s.op("dve", lambda h: h.scalar_tensor_tensor(smt[:, 4:5], smt[:, 3:4], 2.0, smt[:, 2:3],
                                                             ALU.mult, ALU.add),
                     reads=[("sm2", ss), ("sm3", ss)], writes=[("sm4", ss)])
                s.op("act", lambda h: h.activation(smt[:, 5:6], smt[:, 4:5], AF.Sqrt, bias=EPS, scale=1.0),
                     reads=[("sm4", ss)], writes=[("sm5", ss)])
                s.op("dve", lambda h: h.reciprocal(smt[:, 6:7], smt[:, 5:6]),
                     reads=[("sm5", ss)], writes=[("sm6", ss)])
                s.op("dve", lambda h: h.tensor_tensor(
                    smt[:, 7:8], hbias[:, o * 8 + cc:o * 8 + cc + 1], smt[:, 5:6], ALU.mult),
                    reads=[("sm5", ss), "hbias"], writes=[("sm7", ss)])
                s.op("dve", lambda h: h.tensor_tensor(fw[0][:, 0:1], fw[0][:, 0:1], smt[:, 7:8], ALU.add),
                     reads=[("sm7", ss), ("fwb", ss, 0), ("fwb", ss, 1)], writes=[("fwb", ss, 0)])
                s.op("dve", lambda h: h.scalar_tensor_tensor(
                    fw[0][:, 4096:4097], smt[:, 7:8], misc[:, 9:10], fw[0][:, 4096:4097], ALU.mult, ALU.add),
                    reads=[("sm7", ss), "misc"], writes=[("fwb", ss, 0)])
                for q in range(8):
                    csl = slice(q * 1024, (q + 1) * 1024)
                    for dr, op, eng in ((0, ALU.add, "dve"), (1, ALU.subtract, "pool")):
                        pcs = cnt["pc"] % 4
                        cnt["pc"] += 1
                        r0 = o * 2 * DH + dr * DH + cc * 128
                        s.op(eng, lambda h, csl=csl, op=op, pcs=pcs: h.tensor_tensor(
                            pmt[pcs][:, :], fw[0][:, csl], fw[1][:, csl], op),
                            reads=[("fwb", ss, 0), ("fwb", ss, 1)], writes=[("pmt", pcs)])
                        fs = cnt["fo"] % 4
                        cnt["fo"] += 1
                        s.op("act", lambda h, pcs=pcs, fs=fs: h.activation(
                            fo[fs][:, :], pmt[pcs][:, :], AF.Copy, scale=smt[:, 6:7]),
                            reads=[("pmt", pcs), ("sm6", ss)], writes=[("fo", fs)])
                        s.op("sp", lambda h, fs=fs, r0=r0, csl=csl: h.dma_start(
                            out=self.scr["FT"][r0:r0 + 128, csl], in_=fo[fs][:, :]),
                            reads=[("fo", fs)], writes=[("FT", r0 // 64), ("FT", r0 // 64 + 1)], dma=True)

            do_dec(0)
            mults(0)
            for it in range(len(its)):
                if it + 1 < len(its):
                    if its[it + 1][1] == 0:
                        do_dec(its[it + 1][0])
                    mults(it + 1)
                tail(it)
            s.barrier()

    def _fft_consts(self, sb, need_inv):
        s = self.s
        self.dA = sb("dA", [64, 128], BF16)
        self.dB = sb("dB", [128, 64, 3, 128], BF16)
        s.op("sp", lambda h: h.dma_start(out=self.dA[:, :], in_=self.c["dftA"]), writes=["dA"], dma=True)
        for i in range(4):
            s.op("sp", lambda h, i=i: h.dma_start(
                out=self.dB[:, i * 16:(i + 1) * 16, :, :].rearrange("p k m q -> p (k m q)"),
                in_=self.c["dftB"][:, i * 6144:(i + 1) * 6144]),
                writes=["dB"], dma=True)
        if need_inv:
            self.dF = sb("dF", [128, 2, 128], BF16)
            s.op("sp", lambda h: h.dma_start(out=self.dF[:, :, :],
                                             in_=self.c["dftF"].rearrange("k (f p) -> k f p", f=2)),
                 writes=["dF"], dma=True)

    def _load_jcp(self, dst, dstkey, src_rows, reads=()):
        self.s.op("sp", lambda h: h.dma_start(out=dst[:, :, :],
                                              in_=src_rows.rearrange("c (j p) -> j c p", p=128)),
                  reads=list(reads), writes=[dstkey], dma=True)

    def _stageA(self, src, srckey, YP, ypkey="YP", nbanks=2):
        s, ps = self.s, self.ps
        Yv = YP[:, :].rearrange("p (m c e) -> p m c e", c=64, e=2)
        for c4 in range(16):
            bank = (0, 1, 6, 7)[c4 % nbanks]
            for i in range(4):
                c = c4 * 4 + i
                s.op("pe", lambda h, c=c, i=i, bank=bank: h.matmul(
                    ps[bank][:, i * 128:(i + 1) * 128], src[:, c, :], self.dA[:, :], start=True, stop=True),
                    reads=[srckey, "dA"], writes=[("ps", bank)])
            s.op("act", lambda h, c4=c4, bank=bank: h.copy(
                Yv[:, :, c4 * 4:(c4 + 1) * 4, :], ps[bank][:, :].rearrange("p (c m e) -> p m c e", c=4, e=2)),
                reads=[("ps", bank)], writes=[ypkey])

    def _stageB(self, YP, evac, ypkey="YP", YPi=None, ypkey_i=None):
        s, ps = self.s, self.ps
        Y = YP[:, :].rearrange("p (m c e) -> p m c e", c=64, e=2)
        Y2 = Y if YPi is None else YPi[:, :].rearrange("p (m c e) -> p m c e", c=64, e=2)
        ypkey_i = ypkey if ypkey_i is None else ypkey_i
        dB = self.dB
        for kb in range(8):
            br = 2 + (kb % 2) * 2
            bi = br + 1
            for i in range(8):
                kc = kb * 8 + i
                o_r = ps[br][:, i * 64:(i + 1) * 64]
                o_i = ps[bi][:, i * 64:(i + 1) * 64]
                Yr = Y[:, kc // 2, :, kc % 2]
                Yi = Y[:, 32 + kc // 2, :, kc % 2]
                Yr2 = Y2[:, kc // 2, :, kc % 2]
                Yi2 = Y2[:, 32 + kc // 2, :, kc % 2]
                for (out, m, rhs, st, sp_, bk, yk) in ((o_r, 0, Yr, True, False, br, ypkey),
                                                       (o_r, 2, Yi, False, True, br, ypkey),
                                                       (o_i, 1, Yr2, True, False, bi, ypkey_i),
                                                       (o_i, 0, Yi2, False, True, bi, ypkey_i)):
                    s.op("pe", lambda h, out=out, m=m, rhs=rhs, st=st, sp_=sp_, kc=kc: h.matmul(
                        out, dB[:, kc, m, :], rhs, start=st, stop=sp_),
                        reads=[yk, "dB"], writes=[("ps", bk)])
            evac(kb, br, bi)

    def phase2b(self):
        self._phase_id += 1
        nc, s, ps = self.nc, self.s, self.ps
        with ExitStack() as ph:
            def sb(name, shape, dt):
                return ph.enter_context(nc.sbuf_tensor(f"sb{self._uid()}_" + name, shape, dt))
            self._fft_consts(sb, False)
            zin = [sb(f"fzin{i}", [64, 64, 128], BF16) for i in range(3)]
            YPs = [sb(f"fYP{i}", [128, 8192], BF16) for i in range(2)]
            Ho = [sb(f"Ho{i}", [128, 2, 64, 64], BF16) for i in range(2)]
            jobs = [(o, cg, dr) for o in range(2) for cg in range(self.ncg) for dr in range(2)]

            def load(n):
                if n >= len(jobs):
                    return
                o, cg, dr = jobs[n]
                r0 = o * 2 * DH + dr * DH + cg * 64
                self._load_jcp(zin[n % 3], ("fzin", n % 3), self.scr["FT"][r0:r0 + 64, :])
            load(0)
            load(1)
            for n, (o, cg, dr) in enumerate(jobs):
                hs_ = (o * 16 + cg) % 2
                zs = n % 3
                self._stageA(zin[zs], ("fzin", zs), YPs[dr], ("YP", dr), nbanks=4)
                load(n + 2)
                if dr == 1:
                    def evac(kb, br, bi, hs_=hs_):
                        s.op("act", lambda h: h.copy(
                            Ho[hs_][:, 0, kb * 8:(kb + 1) * 8, :], ps[br][:, :].rearrange("p (k c) -> p k c", k=8)),
                            reads=[("ps", br)], writes=[("Ho", hs_)])
                        s.op("dve", lambda h: h.tensor_copy(
                            Ho[hs_][:, 1, kb * 8:(kb + 1) * 8, :], ps[bi][:, :].rearrange("p (k c) -> p k c", k=8)),
                            reads=[("ps", bi)], writes=[("Ho1", hs_)])
                    self._stageB(YPs[0], evac, ("YP", 0), YPi=YPs[1], ypkey_i=("YP", 1))
                    s.op("pool", lambda h, o=o, cg=cg, hs_=hs_: h.dma_start(
                        out=self.scr["HS"][o, cg, :, :], in_=Ho[hs_][:, :, :, :].rearrange("p s k c -> p (s k c)")),
                        reads=[("Ho", hs_), ("Ho1", hs_)], writes=[("HS", o, cg)], dma=True)
            s.barrier()

    def phase2c(self):
        self._phase_id += 1
        nc, s, ps = self.nc, self.s, self.ps
        with ExitStack() as ph:
            def sb(name, shape, dt):
                return ph.enter_context(nc.sbuf_tensor(f"sb{self._uid()}_" + name, shape, dt))
            self._fft_consts(sb, True)
            dF = self.dF
            z = [sb(f"z{i}", [64, 64, 128], BF16) for i in range(3)]
            gate = sb("gate", [64, 64, 128], BF16)
            YP = sb("YP", [128, 8192], BF16)
            P1 = sb("P1", [128, 8192], BF16)
            S = sb("S", [128, 2, 64, 64], BF16)
            H = sb("H", [128, 2, 64, 64], BF16)
            Gs = [sb(f"Gs{i}", [128, 8, 2, 64], BF16) for i in range(4)]
            tmp = [sb(f"tmp{i}", [128, 512], F32) for i in range(8)]
            HC = self.scr["HC"]
            ng = 0
            def gload(pb):
                gs = pb % 4
                s.op("sp", lambda h: h.dma_start(
                    out=Gs[gs][:, :, :, :].rearrange("r a f j -> r (a f j)"),
                    in_=self.c["dftG"][:, pb * 1024:(pb + 1) * 1024]),
                    writes=[("Gs", gs)], dma=True)
            self._load_jcp(z[0], ("z", 0), HC[0:64, :])
            for cg in range(self.ncg):
                zi = [(cg + q) % 3 for q in range(3)]
                for o in range(2):
                    src, dst = z[zi[o]], z[zi[o + 1]]
                    srck, dstk = ("z", zi[o]), ("z", zi[o + 1])
                    s.op("sp", lambda h, o=o, cg=cg: h.dma_start(
                        out=H[:, :, :, :].rearrange("p s k c -> p (s k c)"), in_=self.scr["HS"][o, cg, :, :]),
                        writes=["H"], dma=True)
                    g0 = (1 + o) * DH + cg * 64
                    self._load_jcp(gate, "gate", HC[g0:g0 + 64, :])
                    self._stageA(src, srck, YP)
                    if o == 1 and cg + 1 < self.ncg:
                        self._load_jcp(z[zi[1]], ("z", zi[1]), HC[(cg + 1) * 64:(cg + 2) * 64, :])

                    def evac(kb, br, bi):
                        t = tmp[(kb % 2) * 4:(kb % 2) * 4 + 4]
                        tk = [("tmp", (kb % 2) * 4 + q) for q in range(4)]
                        Hr = H[:, 0, kb * 8:(kb + 1) * 8, :].rearrange("p k c -> p (k c)")
                        Hi = H[:, 1, kb * 8:(kb + 1) * 8, :].rearrange("p k c -> p (k c)")
                        for q, (bk, hh) in enumerate(((br, Hr), (bi, Hi), (br, Hi), (bi, Hr))):
                            s.op("dve", lambda h, q=q, bk=bk, hh=hh: h.tensor_tensor(
                                t[q][:, :], ps[bk][:, :], hh, ALU.mult),
                                reads=[("ps", bk), "H"], writes=[tk[q]])
                        s.op("pool", lambda h: h.tensor_tensor(
                            S[:, 0, kb * 8:(kb + 1) * 8, :].rearrange("p k c -> p (k c)"), t[0][:, :], t[1][:, :],
                            ALU.subtract), reads=[tk[0], tk[1]], writes=["S"])
                        s.op("pool", lambda h: h.tensor_tensor(
                            S[:, 1, kb * 8:(kb + 1) * 8, :].rearrange("p k c -> p (k c)"), t[2][:, :], t[3][:, :],
                            ALU.add), reads=[tk[2], tk[3]], writes=["S"])
                    self._stageB(YP, evac)
                    if "dbgS" in self.debug and cg == 0 and o == 0:
                        s.op("sp", lambda h: h.dma_start(out=self.scr["dbgS"], in_=S[:, :, :, :].rearrange("p s k c -> p (s k c)")),
                             reads=["S"], writes=["dbgS"], dma=True)
                        s.op("sp", lambda h: h.dma_start(out=self.scr["dbgY"], in_=YP[:, :]),
                             reads=["YP"], writes=["dbgY"], dma=True)
                    gload(0)
                    gload(1)
                    gload(2)
                    Sv = S[:, :, :, :].rearrange("k s q c -> k c (s q)")
                    Pb = [YP, P1]
                    Pk = ["YP", "P1"]
                    nev = 0
                    for phh in range(2):
                        Pv = Pb[phh][:, :].rearrange("r (q c e) -> r q c e", c=64, e=2)
                        for c4 in range(16):
                            bank = 4 + c4 % 4
                            for i in range(4):
                                c = c4 * 4 + i
                                s.op("pe", lambda h, c=c, i=i, bank=bank, phh=phh: h.matmul(
                                    ps[bank][:, i * 128:(i + 1) * 128].rearrange("r (f p) -> r f p", f=2),
                                    Sv[:, c, :], dF[:, :, phh * 64:(phh + 1) * 64], start=True, stop=True),
                                    reads=["S", "dF"], writes=[("ps", bank)])
                            eng = "act" if nev % 2 == 0 else "dve"
                            nev += 1
                            srcv = ps[bank][:, :].rearrange("r (c q e) -> r q c e", c=4, e=2)
                            dstv = Pv[:, :, c4 * 4:(c4 + 1) * 4, :]
                            if eng == "act":
                                fn = lambda h, dstv=dstv, srcv=srcv: h.copy(dstv, srcv)
                            else:
                                fn = lambda h, dstv=dstv, srcv=srcv: h.tensor_copy(dstv, srcv)
                            s.op(eng, fn, reads=[("ps", bank)], writes=[Pk[phh]])
                    if "dbgS" in self.debug and cg == 0 and o == 0:
                        s.op("sp", lambda h: h.dma_start(out=self.scr["dbgP0"], in_=YP[:, :]),
                             reads=["YP"], writes=["dbgP0"], dma=True)
                        s.op("sp", lambda h: h.dma_start(out=self.scr["dbgP1"], in_=P1[:, :]),
                             reads=["P1"], writes=["dbgP1"], dma=True)
                    dstv_all = dst
                    gatev = gate
                    for pb in range(16):
                        phh = pb // 8
                        P4 = Pb[phh][:, :].rearrange("r (q c e) -> r q c e", c=64, e=2)
                        gs = pb % 4
                        if pb + 3 < 16:
                            gload(pb + 3)
                        bank = pb % 2
                        for i in range(8):
                            pl = (pb * 8 + i) % 64
                            for f in range(2):
                                s.op("pe", lambda h, i=i, f=f, pl=pl, gs=gs, bank=bank, P4=P4: h.matmul(
                                    ps[bank][0:64, i * 64:(i + 1) * 64], Gs[gs][:, i, f, :], P4[:, f * 32 + pl // 2, :, pl % 2],
                                    start=(f == 0), stop=(f == 1)),
                                    reads=[("Gs", gs), Pk[phh]], writes=[("ps", bank)])
                        s.op("dve", lambda h, pb=pb, bank=bank, dstv_all=dstv_all, gatev=gatev: h.tensor_tensor(
                            dstv_all[:, :, pb * 8:(pb + 1) * 8],
                            ps[bank][0:64, :].rearrange("j (p c) -> j c p", p=8),
                            gatev[:, :, pb * 8:(pb + 1) * 8], ALU.mult),
                            reads=[("ps", bank), "gate"], writes=[dstk])
                if "dbgS" in self.debug and cg == 0:
                    pass
                zf = zi[2]
                s.op("pool", lambda h, cg=cg, zf=zf: h.dma_start(
                    out=self.scr["BT"][cg * 64:(cg + 1) * 64, :].rearrange("c (j p) -> j c p", p=128),
                    in_=z[zf][:, :, :]),
                    reads=[("z", zf)], writes=[("BT", cg)], dma=True)
            s.barrier()

    def phase3(self):
        self._phase_id += 1
        nc, s, ps = self.nc, self.s, self.ps
        NF = DFF // 128
        with ExitStack() as ph:
            def sb(name, shape, dt):
                return ph.enter_context(nc.sbuf_tensor(f"sb{self._uid()}_" + name, shape, dt))
            ident = sb("ident", [128, 128], BF16)
            gf = sb("gffn", [128, D], F32)
            gl = sb("gfin", [128, D], F32)
            xt = sb("x3", [128, 4, D], F32)
            act = sb("act3", [128, NF, T], BF16)
            abT = sb("abT", [128, 16, T], BF16)
            mh = sb("mh", [128, 16, T], BF16)
            ring = [sb(f"ring{i}", [128, 8192], BF16) for i in range(3)]
            sg = [sb(f"sg{i}", [128, 2, T], BF16) for i in range(4)]
            tmp = [sb(f"tm{i}", [128, T], F32) for i in range(4)]
            hb = [sb(f"hb3{i}", [128, D], BF16) for i in range(2)]
            st = [sb(f"st3{i}", [128, 4], F32) for i in range(2)]
            psT = [ps[6][:, :].bitcast(BF16), ps[7][:, :].bitcast(BF16)]
            W = self.scr
            rc = [0]

            def wload(fn_in, shape_view, srckeys):
                slot = rc[0] % 3
                rc[0] += 1
                s.op("sp", lambda h, slot=slot: fn_in(h, ring[slot]), reads=srckeys,
                     writes=[("ring", slot), ("ringb", slot)], dma=True)
                return slot

            s.op("pool", lambda h: h.dma_start(out=ident[:, :], in_=self.c["ident"]), writes=["ident"], dma=True)
            s.op("pool", lambda h: h.dma_start(out=gf[:, :], in_=self.w["g_ffn"].partition_broadcast(128)),
                 writes=["gf"], dma=True)
            s.op("pool", lambda h: h.dma_start(out=gl[:, :], in_=self.w["g_final"].partition_broadcast(128)),
                 writes=["gl"], dma=True)
            nsg = 0
            nhb = 0
            for tt in range(self.nt):
                tsl = slice(tt * T, (tt + 1) * T)

                def ab_load(tsl_):
                    s.op("pool", lambda h: h.dma_start(
                        out=abT[:, 0:8, :], in_=W["AT"][:, tsl_].rearrange("(k p) t -> p k t", p=128)),
                        writes=[("ab", k) for k in range(8)], dma=True)
                    s.op("pool", lambda h: h.dma_start(
                        out=abT[:, 8:16, :], in_=W["BT"][:, tsl_].rearrange("(k p) t -> p k t", p=128)),
                        writes=[("ab", k) for k in range(8, 16)], dma=True)

                def sgload(dc, tsl_=tsl):
                    if dc >= 16:
                        return
                    gs_ = dc % 4
                    s.op("pool", lambda h: h.dma_start(
                        out=sg[gs_][:, 0, :], in_=W["SG"][dc * 128:(dc + 1) * 128, tsl_]),
                        writes=[("sg", gs_)], dma=True)
                    s.op("pool", lambda h: h.dma_start(
                        out=sg[gs_][:, 1, :], in_=W["SG"][D + dc * 128:D + (dc + 1) * 128, tsl_]),
                        writes=[("sgb", gs_)], dma=True)
                if tt == 0:
                    ab_load(tsl)
                    sgload(0)
                    sgload(1)
                s.op("pool", lambda h, tsl=tsl: h.dma_start(
                    out=xt[:, :, :], in_=self.x[tsl, :].rearrange("(s p) d -> p s d", p=128)),
                    writes=[("x", q) for q in range(4)], dma=True)
                for db in range(4):
                    slot = wload(lambda h, r, db=db: (h.dma_start(
                        out=r[:, 0:4096].rearrange("p (k n) -> p k n", k=8),
                        in_=W["WB_a"][:, db * 512:(db + 1) * 512].rearrange("(k p) n -> p k n", p=128))), None, [])
                    s.op("sp", lambda h, slot=slot, db=db: h.dma_start(
                        out=ring[slot][:, 4096:8192].rearrange("p (k n) -> p k n", k=8),
                        in_=W["WB_b"][:, db * 512:(db + 1) * 512].rearrange("(k p) n -> p k n", p=128)),
                        writes=[("ringb", slot)], reads=[("ring", slot)], dma=True)
                    wa = ring[slot][:, 0:4096].rearrange("p (k n) -> p k n", k=8)
                    wbv = ring[slot][:, 4096:8192].rearrange("p (k n) -> p k n", k=8)
                    for dq in range(4):
                        dc = db * 4 + dq
                        gs_ = dc % 4
                        sgload(dc + 2)
                        ba = (dc % 2) * 2
                        bb = ba + 1
                        for kc in range(8):
                            s.op("pe", lambda h, kc=kc, dq=dq, ba=ba, wa=wa: h.matmul(
                                ps[ba][:, :], wa[:, kc, dq * 128:(dq + 1) * 128], abT[:, kc, :],
                                start=(kc == 0), stop=(kc == 7)),
                                reads=[("ring", slot), ("ab", kc)], writes=[("ps", ba)])
                        for kc in range(8):
                            s.op("pe", lambda h, kc=kc, dq=dq, bb=bb, wbv=wbv: h.matmul(
                                ps[bb][:, :], wbv[:, kc, dq * 128:(dq + 1) * 128], abT[:, 8 + kc, :],
                                start=(kc == 0), stop=(kc == 7)),
                                reads=[("ringb", slot), ("ab", 8 + kc)], writes=[("ps", bb)])
                        ta, tb = tmp[(dc % 2) * 2], tmp[(dc % 2) * 2 + 1]
                        ka, kb_ = ("tm", (dc % 2) * 2), ("tm", (dc % 2) * 2 + 1)
                        s.op("dve", lambda h, ta=ta, ba=ba, gs_=gs_: h.tensor_tensor(
                            ta[:, :], ps[ba][:, :], sg[gs_][:, 0, :], ALU.mult),
                            reads=[("ps", ba), ("sg", gs_)], writes=[ka])
                        s.op("dve", lambda h, tb=tb, bb=bb, gs_=gs_: h.tensor_tensor(
                            tb[:, :], ps[bb][:, :], sg[gs_][:, 1, :], ALU.mult),
                            reads=[("ps", bb), ("sgb", gs_)], writes=[kb_])
                        s.op("pool", lambda h, ta=ta, tb=tb, dc=dc: h.tensor_tensor(
                            mh[:, dc, :], ta[:, :], tb[:, :], ALU.add),
                            reads=[ka, kb_], writes=[("mh", dc)])
                for nb in range(4):
                    slot = wload(lambda h, r, nb=nb: h.dma_start(
                        out=r[:, :].rearrange("p (k n) -> p k n", k=16),
                        in_=W["WB_out"][:, nb * 512:(nb + 1) * 512].rearrange("(k p) n -> p k n", p=128)), None, [])
                    wv = ring[slot][:, :].rearrange("p (k n) -> p k n", k=16)
                    for sub in range(4):
                        bank = 4 + (nb * 4 + sub) % 2
                        for dc in range(16):
                            s.op("pe", lambda h, dc=dc, sub=sub, bank=bank, wv=wv: h.matmul(
                                ps[bank][:, :], mh[:, dc, sub * 128:(sub + 1) * 128], wv[:, dc, :],
                                start=(dc == 0), stop=(dc == 15)),
                                reads=[("ring", slot), ("mh", dc)], writes=[("ps", bank)])
                        s.op("dve", lambda h, sub=sub, nb=nb, bank=bank: h.tensor_tensor(
                            xt[:, sub, nb * 512:(nb + 1) * 512], ps[bank][:, :], xt[:, sub, nb * 512:(nb + 1) * 512],
                            ALU.add),
                            reads=[("ps", bank), ("x", sub)], writes=[("x", sub)])
                def nrm(sub):
                    xs = sub % 2
                    self._norm_T(xt[:, sub, :], ("x", sub), gf, "gf", hb[xs], ("hb3", xs), st[xs], ("st3", xs),
                                 ident, psT, mh, sub, part="norm")

                def trn(sub):
                    xs = sub % 2
                    self._norm_T(xt[:, sub, :], ("x", sub), gf, "gf", hb[xs], ("hb3", xs), st[xs], ("st3", xs),
                                 ident, psT, mh, sub, part="tr")
                nrm(0)
                nrm(1)
                trn(0)
                nrm(2)
                trn(1)
                nrm(3)
                trn(2)
                trn(3)
                for fb in range(DFF // 256):
                    slot = wload(lambda h, r, fb=fb: h.dma_start(
                        out=r[:, 0:4096].rearrange("p (k n) -> p k n", k=16),
                        in_=W["WB_gate"][:, fb * 256:(fb + 1) * 256].rearrange("(k p) n -> p k n", p=128)), None, [])
                    s.op("sp", lambda h, slot=slot, fb=fb: h.dma_start(
                        out=ring[slot][:, 4096:8192].rearrange("p (k n) -> p k n", k=16),
                        in_=W["WB_up"][:, fb * 256:(fb + 1) * 256].rearrange("(k p) n -> p k n", p=128)),
                        writes=[("ringb", slot)], reads=[("ring", slot)], dma=True)
                    wg = ring[slot][:, 0:4096].rearrange("p (k n) -> p k n", k=16)
                    wu = ring[slot][:, 4096:8192].rearrange("p (k n) -> p k n", k=16)
                    for fq in range(2):
                        fc = fb * 2 + fq
                        bg = (fc % 2) * 2
                        bu = bg + 1
                        for dc in range(16):
                            s.op("pe", lambda h, dc=dc, fq=fq, bg=bg, wg=wg: h.matmul(
                                ps[bg][:, :], wg[:, dc, fq * 128:(fq + 1) * 128], mh[:, dc, :],
                                start=(dc == 0), stop=(dc == 15)),
                                reads=[("ring", slot), ("mh", dc)] + [("mhT", q) for q in range(4)], writes=[("ps", bg)])
                        for dc in range(16):
                            s.op("pe", lambda h, dc=dc, fq=fq, bu=bu, wu=wu: h.matmul(
                                ps[bu][:, :], wu[:, dc, fq * 128:(fq + 1) * 128], mh[:, dc, :],
                                start=(dc == 0), stop=(dc == 15)),
                                reads=[("ringb", slot), ("mh", dc)] + [("mhT", q) for q in range(4)], writes=[("ps", bu)])
                        tg = tmp[fc % 2]
                        s.op("act", lambda h, tg=tg, bg=bg: h.activation(tg[:, :], ps[bg][:, :], AF.Silu),
                             reads=[("ps", bg)], writes=[("tm", fc % 2)])
                        s.op("dve", lambda h, tg=tg, bu=bu, fc=fc: h.tensor_tensor(
                            act[:, fc, :], ps[bu][:, :], tg[:, :], ALU.mult),
                            reads=[("ps", bu), ("tm", fc % 2)], writes=[("act", fc)])
                pieces = [(0, 16), (16, 16), (32, 12)]
                for nb in range(4):
                    for pi, (f0, nf) in enumerate(pieces):
                        slot = wload(lambda h, r, nb=nb, f0=f0, nf=nf: h.dma_start(
                            out=r[:, 0:nf * 512].rearrange("p (k n) -> p k n", k=nf),
                            in_=W["WB_down"][f0 * 128:(f0 + nf) * 128, nb * 512:(nb + 1) * 512].rearrange(
                                "(k p) n -> p k n", p=128)), None, [])
                        wv = ring[slot][:, 0:nf * 512].rearrange("p (k n) -> p k n", k=nf)
                        for sub in range(4):
                            bank = 4 + sub
                            for k in range(nf):
                                fc = f0 + k
                                s.op("pe", lambda h, fc=fc, k=k, sub=sub, bank=bank, wv=wv: h.matmul(
                                    ps[bank][:, :], act[:, fc, sub * 128:(sub + 1) * 128], wv[:, k, :],
                                    start=(fc == 0), stop=(fc == NF - 1)),
                                    reads=[("ring", slot), ("act", fc)], writes=[("ps", bank)])
                    for sub in range(4):
                        bank = 4 + sub
                        s.op("dve", lambda h, sub=sub, nb=nb, bank=bank: h.tensor_tensor(
                            xt[:, sub, nb * 512:(nb + 1) * 512], ps[bank][:, :], xt[:, sub, nb * 512:(nb + 1) * 512],
                            ALU.add),
                            reads=[("ps", bank), ("x", sub)], writes=[("x", sub)])
                if tt + 1 < self.nt:
                    ntsl = slice((tt + 1) * T, (tt + 2) * T)
                    ab_load(ntsl)
                    sgload(0, ntsl)
                    sgload(1, ntsl)
                for sub in range(4):
                    xs = nhb % 2
                    nhb += 1
                    stt, stk = st[xs], ("st3", xs)
                    xa = xt[:, sub, :]
                    s.op("act", lambda h, xs=xs, xa=xa, stt=stt: h.activation(hb[xs][:, :], xa, AF.Square,
                                                                             accum_out=stt[:, 0:1]),
                         reads=[("x", sub)], writes=[("hb3", xs), stk])
                    s.op("act", lambda h, stt=stt: h.activation(stt[:, 1:2], stt[:, 0:1], AF.Sqrt,
                                                                bias=EPS, scale=1.0 / D),
                         reads=[stk], writes=[(stk, 1)])
                    s.op("dve", lambda h, stt=stt: h.reciprocal(stt[:, 2:3], stt[:, 1:2]),
                         reads=[(stk, 1)], writes=[(stk, 2)])
                    s.op("dve", lambda h, xa=xa, stt=stt: h.scalar_tensor_tensor(
                        xa, xa, stt[:, 2:3], gl[:, :], ALU.mult, ALU.mult),
                        reads=[("x", sub), (stk, 2), "gl"], writes=[("x", sub)])
                    t0 = tt * T + sub * 128
                    o = s.op("pool", lambda h, xa=xa, t0=t0: h.dma_start(out=self.y[t0:t0 + 128, :], in_=xa),
                             reads=[("x", sub)], writes=[("y", t0)], dma=True)
                    self.out_dmas.append(o)
            s.barrier()

    def _norm_T(self, xa, xkey, g, gkey, hbt, hbk, stt, stk, ident, psT, dstT, sub, part="both"):
        s = self.s
        if part in ("both", "norm"):
            s.op("act", lambda h: h.activation(hbt[:, :], xa, AF.Square, accum_out=stt[:, 0:1]),
                 reads=[xkey], writes=[hbk, stk])
            s.op("act", lambda h: h.activation(stt[:, 1:2], stt[:, 0:1], AF.Sqrt, bias=EPS, scale=1.0 / D),
                 reads=[stk], writes=[(stk, 1)])
            s.op("dve", lambda h: h.reciprocal(stt[:, 2:3], stt[:, 1:2]), reads=[(stk, 1)], writes=[(stk, 2)])
            s.op("dve", lambda h: h.scalar_tensor_tensor(hbt[:, :], xa, stt[:, 2:3], g[:, :], ALU.mult, ALU.mult),
                 reads=[xkey, (stk, 2), gkey], writes=[hbk])
        if part == "norm":
            return
        for half in range(2):
            for q in range(8):
                dc = half * 8 + q
                s.op("pe", lambda h, dc=dc, half=half, q=q: h.transpose(
                    psT[half][:, q * 128:(q + 1) * 128], hbt[:, dc * 128:(dc + 1) * 128], ident[:, :]),
                    reads=[hbk, "ident"], writes=[("ps", 6 + half)])
            dstv = dstT[:, half * 8:(half + 1) * 8, sub * 128:(sub + 1) * 128]
            srcv = psT[half].rearrange("p (q t) -> p q t", q=8)
            if half == 0:
                fn = lambda h, dstv=dstv, srcv=srcv: h.copy(dstv, srcv)
                eng = "act"
            else:
                fn = lambda h, dstv=dstv, srcv=srcv: h.tensor_copy(dstv, srcv)
                eng = "dve"
            s.op(eng, fn, reads=[("ps", 6 + half)], writes=[("mhT", sub)] + [("mh", d_) for d_ in range(half * 8, half * 8 + 8)])

def core_inputs(inputs, core):
    if core < 4:
        xc = np.ascontiguousarray(inputs["x_prompt"][core])
    else:
        i = core - 4
        xc = np.ascontiguousarray(inputs["x_sample"][2 * i:2 * i + 2].reshape(NTOK, D))
    m = {"x": xc.astype(np.float32, copy=False)}
    for name, shp in WEIGHT_SPECS:
        m[name] = np.ascontiguousarray(np.asarray(inputs[name], dtype=np.float32).reshape(shp))
    m.update(make_constants(core < 4))
    return m


_NC_CACHE = {}


def kernel(**inputs):
    inputs = {k: np.asarray(v) for k, v in inputs.items()}
    if "nc" not in _NC_CACHE:
        _NC_CACHE["nc"] = Builder().build(phases="0BFHCD")
    nc = _NC_CACHE["nc"]
    in_maps = [core_inputs(inputs, c) for c in range(NCORES)]
    res = run_bass_kernel_spmd(nc, in_maps, core_ids=list(range(NCORES)))
    ys = [r["y"] for r in res.results]
    y_prompt = np.stack(ys[:4], axis=0).astype(np.float32)
    y_sample = np.concatenate([y.reshape(2, 4096, D) for y in ys[4:]], axis=0).astype(np.float32)
    return (y_prompt, y_sample)
```

```python
import math
from contextlib import ExitStack

import numpy as np
import ml_dtypes

import concourse.bass as bass
import concourse.mybir as mybir
from concourse.bass_utils import run_bass_kernel_spmd

F32 = mybir.dt.float32
BF16 = mybir.dt.bfloat16
AF = mybir.ActivationFunctionType
ALU = mybir.AluOpType
AX = mybir.AxisListType

D = 2048
DP = 1024
DH = 1024
DIN = 8192
DFF = 5632
NTOK = 8192
T = 512
NT = NTOK // T
EPS = 1e-6
NCORES = 8

ENGS = ("pe", "act", "dve", "pool", "sp")
NS = 12


class Op:
    __slots__ = ("eng", "fn", "deps", "signal", "dma", "sem", "val", "qidx", "name")

    def __init__(self, eng, fn, dma, name=None):
        self.eng = eng
        self.fn = fn
        self.dma = dma
        self.deps = []
        self.signal = False
        self.sem = None
        self.val = None
        self.qidx = None
        self.name = name


class Sched:
    def __init__(self):
        self.ops = {e: [] for e in ENGS}
        self.lastw = {}
        self.readers = {}
        self.qcount = {e: 0 for e in ENGS}
        self.qops = {e: [] for e in ENGS}

    def op(self, eng, fn, reads=(), writes=(), dma=False, name=None):
        o = Op(eng, fn, dma, name)
        deps = {}

        def add(d):
            if d is None or d is o:
                return
            if (not d.dma) and (not dma) and d.eng == eng and eng == "pe":
                return
            deps[id(d)] = d

        for b in reads:
            add(self.lastw.get(b))
        for b in writes:
            add(self.lastw.get(b))
            rd = self.readers.get(b)
            if rd:
                for r in rd.values():
                    add(r)
        for b in writes:
            self.lastw[b] = o
            self.readers[b] = {}
        for b in reads:
            rd = self.readers.setdefault(b, {})
            if dma:
                rd[("dma", id(o))] = o
            else:
                rd[eng] = o
        if dma:
            i = self.qcount[eng]
            o.qidx = i
            self.qcount[eng] = i + 1
            if i >= NS:
                prev = self.qops[eng][i - NS]
                deps[id(prev)] = prev
            self.qops[eng].append(o)
        o.deps = list(deps.values())
        for d in o.deps:
            d.signal = True
        self.ops[eng].append(o)
        return o

    def barrier(self):
        tails = []
        for e in ENGS:
            comp = [o for o in self.ops[e] if not o.dma and o.fn is not None]
            if comp:
                tails.append(comp[-1])
            tails.extend(self.qops[e][-NS:])
        for t in tails:
            t.signal = True
        for e in ENGS:
            o = Op(e, None, False, "barrier")
            o.deps = list(tails)
            self.ops[e].append(o)
        self.lastw = {}
        self.readers = {}

    def wait_all(self, eng, ops):
        o = Op(eng, None, False, "waitall")
        o.deps = list(ops)
        for d in o.deps:
            d.signal = True
        self.ops[eng].append(o)

    def emit(self, block, sems, dsems):
        for e in ENGS:
            c = 0
            for o in self.ops[e]:
                if o.dma:
                    o.sem = dsems[e][o.qidx % NS]
                    o.val = 16 * (o.qidx // NS + 1)
                elif o.signal and o.fn is not None:
                    c += 1
                    o.sem = sems[e]
                    o.val = c
            assert c < 60000, (e, c)
        handles = {"pe": block.tensor, "act": block.scalar, "dve": block.vector,
                   "pool": block.gpsimd, "sp": block.sync}
        for e in ENGS:
            ops = self.ops[e]

            def body(h, ops=ops):
                waited = {}
                for o in ops:
                    for d in o.deps:
                        k = id(d.sem)
                        if waited.get(k, 0) < d.val:
                            h.wait_ge(d.sem, d.val)
                            waited[k] = d.val
                    if o.fn is None:
                        continue
                    ins = o.fn(h)
                    if o.dma:
                        ins.then_inc(o.sem, 16)
                    elif o.signal:
                        ins.then_inc(o.sem, 1)

            handles[e](body)


def _bf(a):
    return np.ascontiguousarray(a.astype(ml_dtypes.bfloat16))


def make_constants(is_prompt):
    c = {}
    if is_prompt:
        L, nseq = 8192, 1
    else:
        L, nseq = 4096, 2
    N = 2 * L
    J = L // 128
    N1 = N // 128
    K1 = N1 // 2
    A = np.zeros((64, 2, 64), np.float64)
    for sq in range(nseq):
        j = np.arange(J)[:, None]
        k1 = np.arange(K1)[None, :]
        ang = -2 * np.pi * j * (k1 + 0.5) / N1
        A[sq * J:(sq + 1) * J, 0, sq * K1:(sq + 1) * K1] = np.cos(ang)
        A[sq * J:(sq + 1) * J, 1, sq * K1:(sq + 1) * K1] = np.sin(ang)
    c["dftA"] = _bf(A.reshape(64, 128))
    p = np.arange(128)[:, None]
    k2 = np.arange(128)[None, :]
    MB = np.zeros((128, 64, 3, 128), np.float64)
    for kc in range(64):
        k1p = kc % K1
        ang = -2 * np.pi * p * (k1p + N1 * k2 + 0.5) / N
        MB[:, kc, 0, :] = np.cos(ang)
        MB[:, kc, 1, :] = np.sin(ang)
        MB[:, kc, 2, :] = -np.sin(ang)
    c["dftB"] = _bf(MB.reshape(128, 64 * 3 * 128))
    ang = 2 * np.pi * np.arange(128)[:, None] * np.arange(128)[None, :] / 128
    c["dftF"] = _bf(np.concatenate([np.cos(ang), np.sin(ang)], axis=1))
    G1 = np.zeros((128, 128, 64), np.float64)
    G2 = np.zeros((128, 128, 64), np.float64)
    for sq in range(nseq):
        k1 = np.arange(K1)[:, None, None]
        pp = np.arange(128)[None, :, None]
        jj = np.arange(J)[None, None, :]
        ang = 2 * np.pi * (k1 + 0.5) * (pp + 128 * jj) / N
        Gr = np.cos(ang) * (2.0 / N)
        Gi = np.sin(ang) * (2.0 / N)
        r0 = sq * K1
        j0 = sq * J
        G1[r0:r0 + K1, :, j0:j0 + J] = Gr
        G1[64 + r0:64 + r0 + K1, :, j0:j0 + J] = -Gi
        G2[r0:r0 + K1, :, j0:j0 + J] = -Gi
        G2[64 + r0:64 + r0 + K1, :, j0:j0 + J] = -Gr
    c["dftG"] = _bf(np.stack([G1, G2], axis=2).reshape(128, 128 * 2 * 64))
    t = np.linspace(0.0, 1.0, L, dtype=np.float32)
    wt = (2.0 * np.float32(math.pi) * np.arange(L, dtype=np.float32) / np.float32(L)).astype(np.float32)
    bands = np.linspace(1e-4, 15, 16, dtype=np.float32)[None, :]
    z = np.concatenate([t[:, None], np.cos(bands * wt[:, None]), -np.sin(bands * wt[:, None])], axis=-1)
    z = np.tile(z.astype(np.float32), (nseq, 1))
    c["zfeatT"] = np.ascontiguousarray(z.T)
    c["tvals"] = np.ascontiguousarray(np.tile(t, nseq)[None, :])
    inv = np.zeros((4, NTOK), np.float32)
    tt = np.arange(L)
    for g, w in enumerate((2, 4, 8, 16)):
        lo = np.clip(tt - w // 2, 0, L)
        hi = np.clip(tt + (w - w // 2), 0, L)
        inv[g] = np.tile(1.0 / (hi - lo).astype(np.float32), nseq)
    c["invcnt"] = inv
    hm = np.ones((NT, 2), np.float32)
    for i in range(NT):
        t0 = i * T
        if t0 % L == 0:
            hm[i, 0] = 0.0
        if (t0 + T) % L == 0:
            hm[i, 1] = 0.0
    c["halomask"] = np.ascontiguousarray(np.broadcast_to(hm.reshape(1, NT * 2), (128, NT * 2)))
    c["ident"] = _bf(np.eye(128))
    misc = np.zeros((128, 16), np.float32)
    misc[:, 0] = 1.0 if is_prompt else 0.5
    max_decay = math.log(1e-2) / 0.3
    min_decay = math.log(1e-2) / 1.5
    deltas = np.abs(np.linspace(min_decay, max_decay, DH, dtype=np.float32))
    misc[:, 1:9] = -deltas.reshape(8, 128).T
    misc[:, 9] = 0.0 if is_prompt else 1.0
    c["misc"] = misc
    return c


CONST_SPECS = {
    "dftA": ([64, 128], BF16), "dftB": ([128, 64 * 3 * 128], BF16), "dftF": ([128, 256], BF16),
    "dftG": ([128, 128 * 2 * 64], BF16), "zfeatT": ([33, NTOK], F32), "tvals": ([1, NTOK], F32),
    "invcnt": ([4, NTOK], F32), "halomask": ([128, NT * 2], F32), "ident": ([128, 128], BF16),
    "misc": ([128, 16], F32),
}

WEIGHT_SPECS = [
    ("g_mix", [1, D]), ("w_in", [D, DIN]), ("pool_w", [4 * 256, 256]), ("pool_scale", [1, DP]),
    ("conv_w", [3, 3 * DH]), ("conv_b", [1, 3 * DH]), ("filt_w1", [33, 64]), ("filt_b1", [1, 64]),
    ("filt_freq1", [1, 64]), ("filt_w2", [64, 64]), ("filt_b2", [1, 64]), ("filt_freq2", [1, 64]),
    ("filt_w3", [64, 4 * DH]), ("hyena_bias", [2, DH]), ("w_branch_a", [DP, D]),
    ("w_branch_b", [DH, D]), ("w_out", [D, D]), ("g_ffn", [1, D]), ("w_gate", [D, DFF]),
    ("w_up", [D, DFF]), ("w_down", [DFF, D]), ("g_final", [1, D]),
]


class Builder:
    def __init__(self, debug=(), nt=NT, ncg=16):
        self.debug = set(debug)
        self.nt = nt
        self.ncg = ncg
        self._phase_id = 0
        self.nc = bass.Bass("TRN2", target_bir_lowering=False)
        self.s = Sched()
        nc = self.nc
        self.x = nc.dram_tensor("x", [NTOK, D], F32, kind="ExternalInput").ap()
        self.y = nc.dram_tensor("y", [NTOK, D], F32, kind="ExternalOutput").ap()
        self.w = {}
        for name, shp in WEIGHT_SPECS:
            self.w[name] = nc.dram_tensor(name, shp, F32, kind="ExternalInput").ap()
        self.c = {}
        for name, (shp, dt) in CONST_SPECS.items():
            self.c[name] = nc.dram_tensor(name, shp, dt, kind="ExternalInput").ap()
        self.scr = {}

    def _uid(self):
        return self._phase_id

    def scratch(self, name, shape, dt):
        kind = "ExternalOutput" if name in self.debug else "Internal"
        t = self.nc.dram_tensor(name, shape, dt, kind=kind).ap()
        self.scr[name] = t
        return t

    def build(self, phases):
        nc = self.nc
        s = self.s
        with ExitStack() as es:
            sems = {e: es.enter_context(nc.semaphore("s_" + e)) for e in ENGS}
            dsems = {e: [es.enter_context(nc.semaphore(f"d_{e}_{i}")) for i in range(NS)]
                     for e in ENGS}
            self.ps = [es.enter_context(nc.psum_tensor(f"ps{i}", [128, 512], F32)) for i in range(8)]
            self.es = es
            self.scratch("WB_in", [D, DIN], BF16)
            self.scratch("WB_a", [DP, D], BF16)
            self.scratch("WB_b", [DH, D], BF16)
            self.scratch("WB_out", [D, D], BF16)
            self.scratch("WB_gate", [D, DFF], BF16)
            self.scratch("WB_up", [D, DFF], BF16)
            self.scratch("WB_down", [DFF, D], BF16)
            self.scratch("PU", [DP, NTOK], F32)
            self.scratch("HV", [3 * DH, NTOK], F32)
            self.scratch("SG", [2 * D, NTOK], BF16)
            self.out_dmas = []
            if "0" in phases:
                self.phase0()
            self.scratch("AT", [DP, NTOK], BF16)
            self.scratch("HC", [3 * DH, NTOK], BF16)
            self.scratch("FT", [4 * DH, NTOK], BF16)
            self.scratch("HS", [2, 16, 128, 2 * 64 * 64], BF16)
            self.scratch("BT", [DH, NTOK], BF16)
            if "dbgS" in self.debug:
                for nm in ("dbgS", "dbgY", "dbgP0", "dbgP1"):
                    self.debug.add(nm)
                    self.scratch(nm, [128, 8192], BF16)
                self.debug.add("dbgZ1")
                self.scratch("dbgZ1", [64, 8192], BF16)
            if "B" in phases:
                self.phase1()
            if "F" in phases:
                self.phase2a()
            if "H" in phases:
                self.phase2b()
            if "C" in phases:
                self.phase2c()
            if "D" in phases:
                self.phase3()
            s.barrier()
            s.wait_all("sp", self.out_dmas)
            with nc.Block() as block:
                s.emit(block, sems, dsems)
        return nc

    def phase0(self):
        s = self.s
        for wn, sn in (("w_in", "WB_in"), ("w_branch_a", "WB_a"), ("w_branch_b", "WB_b"), ("w_out", "WB_out"),
                       ("w_gate", "WB_gate"), ("w_up", "WB_up"), ("w_down", "WB_down")):
            src = self.w[wn]
            dst = self.scr[sn]
            for r in range(src.shape[0] // 128):
                s.op("pool", lambda h, r=r, src=src, dst=dst: h.dma_start(
                    out=dst[r * 128:(r + 1) * 128, :], in_=src[r * 128:(r + 1) * 128, :]),
                    reads=[], writes=[(sn, r)], dma=True)

    def phase1(self):
        self._phase_id += 1
        nc, s = self.nc, self.s
        with ExitStack() as ph:
            def sb(name, shape, dt):
                return ph.enter_context(nc.sbuf_tensor(f"sb{self._uid()}_" + name, shape, dt))
            self._p1_alloc(sb)
            for tt in range(self.nt):
                self._p1_B(tt, self._p1_Cfront_steps(tt - 2) if tt >= 2 else ())
                if tt >= 2:
                    self._p1_Cback(tt - 2)
            for tt in range(max(self.nt - 2, 0), self.nt):
                self._p1_Cfront(tt)
                self._p1_Cback(tt)
            s.barrier()

    def _p1_alloc(self, sb):
        s = self.s
        self.gbc = sb("gbc", [128, D], F32)
        self.ident = sb("ident", [128, 128], BF16)
        self.xt = [sb(f"xt{i}", [128, D], F32) for i in range(2)]
        self.hb = [sb(f"hb{i}", [128, D], BF16) for i in range(2)]
        self.st = [sb(f"st{i}", [128, 4], F32) for i in range(2)]
        self.hT = [sb(f"hT{i}", [128, 16, T], BF16) for i in range(2)]
        self.wb = [sb(f"wb{i}", [128, 16, 512], BF16) for i in range(2)]
        self.ev32 = [sb(f"ev32_{i}", [128, 512], F32) for i in range(4)]
        self.ev16 = [sb(f"ev16_{i}", [128, 512], BF16) for i in range(4)]
        self.CV = [sb(f"CV{i}", [128, T + 16], F32) for i in range(3)]
        self.lv = [sb(f"lv{i}", [128, T + 16], F32) for i in range(4)]
        self.pooled = [sb(f"pooled{i}", [128, T], BF16) for i in range(16)]
        self.icnt = [sb(f"icnt{i}", [128, 4, T], F32) for i in range(2)]
        self.hmask = sb("hmask", [128, NT * 2], F32)
        self.poolw = sb("poolw", [128, 8, 256], BF16)
        self.pscale = sb("pscale", [128, 8], F32)
        self.aev = [sb(f"aev{i}", [128, T], BF16) for i in range(2)]
        self.cacc = [sb(f"cacc{i}", [128, T], F32) for i in range(3)]
        self.cout = [sb(f"cout{i}", [128, T], BF16) for i in range(3)]
        self.convw = sb("convw", [128, 3, 24], F32)
        self.convb = sb("convb", [128, 24], F32)
        self.cnt = {"ev": 0, "sub": 0, "u": 0, "pl": 0, "ae": 0, "v": 0, "cv": 0}
        nc = self.nc
        s.op("sp", lambda h: h.dma_start(out=self.gbc[:, :], in_=self.w["g_mix"].partition_broadcast(128)),
             writes=["gbc"], dma=True)
        s.op("sp", lambda h: h.dma_start(out=self.ident[:, :], in_=self.c["ident"]), writes=["ident"], dma=True)
        s.op("sp", lambda h: h.dma_start(out=self.hmask[:, :], in_=self.c["halomask"]), writes=["hmask"], dma=True)
        s.op("pool", lambda h: h.dma_start(out=self.poolw[:, :, :],
                                           in_=self.w["pool_w"].rearrange("(a p) n -> p a n", p=128)),
             writes=["poolw"], dma=True)
        s.op("sp", lambda h: h.dma_start(out=self.pscale[:, :],
                                         in_=self.w["pool_scale"].rearrange("o (c p) -> p (o c)", p=128),
                                         allow_slow_non_contiguous=True),
             writes=["pscale"], dma=True)
        for k in range(3):
            s.op("sp", lambda h, k=k: h.dma_start(
                out=self.convw[:, k, :],
                in_=self.w["conv_w"][k:k + 1, :].rearrange("o (c p) -> p (o c)", p=128),
                allow_slow_non_contiguous=True),
                writes=["convw"], dma=True)
        s.op("sp", lambda h: h.dma_start(out=self.convb[:, :],
                                         in_=self.w["conv_b"].rearrange("o (c p) -> p (o c)", p=128),
                                         allow_slow_non_contiguous=True),
             writes=["convb"], dma=True)

    def _p1_xload(self, tt, sub):
        s = self.s
        g = tt * 4 + sub
        xs = g % 2
        t0 = tt * T + sub * 128
        s.op("sp", lambda h: h.dma_start(out=self.xt[xs][:, :], in_=self.x[t0:t0 + 128, :]),
             writes=[("xt", xs)], dma=True)

    def _p1_N(self, tt, sub, part="both"):
        s, ps = self.s, self.ps
        xt, hb, st, hT, gbc, ident = self.xt, self.hb, self.st, self.hT, self.gbc, self.ident
        psT = [ps[6][:, :].bitcast(BF16), ps[7][:, :].bitcast(BF16)]
        hs = tt % 2
        xs = (tt * 4 + sub) % 2
        if part in ("both", "norm"):
            s.op("act", lambda h: h.activation(hb[xs][:, :], xt[xs][:, :], AF.Square, accum_out=st[xs][:, 0:1]),
                 reads=[("xt", xs)], writes=[("hb", xs), ("st", xs)])
            s.op("act", lambda h: h.activation(st[xs][:, 1:2], st[xs][:, 0:1], AF.Sqrt, bias=EPS, scale=1.0 / D),
                 reads=[("st", xs)], writes=[("st1", xs)])
            s.op("dve", lambda h: h.reciprocal(st[xs][:, 2:3], st[xs][:, 1:2]),
                 reads=[("st1", xs)], writes=[("st2", xs)])
            s.op("dve", lambda h: h.scalar_tensor_tensor(
                hb[xs][:, :], xt[xs][:, :], st[xs][:, 2:3], gbc[:, :], ALU.mult, ALU.mult),
                reads=[("xt", xs), ("st2", xs), "gbc"], writes=[("hb", xs)])
        if part == "norm":
            return
        for half in range(2):
            for q in range(8):
                dc = half * 8 + q
                s.op("pe", lambda h, dc=dc, half=half, q=q: h.transpose(
                    psT[half][:, q * 128:(q + 1) * 128], hb[xs][:, dc * 128:(dc + 1) * 128], ident[:, :]),
                    reads=[("hb", xs), "ident"], writes=[("ps", 6 + half)])
            dstv = hT[hs][:, half * 8:(half + 1) * 8, sub * 128:(sub + 1) * 128]
            srcv = psT[half].rearrange("p (q t) -> p q t", q=8)
            s.op("dve", lambda h, dstv=dstv, srcv=srcv: h.tensor_copy(dstv, srcv),
                 reads=[("ps", 6 + half)], writes=[("hT", hs, sub)])

    def _p1_wload(self, w):
        s = self.s
        if w >= self.nt * 16:
            return
        ob = w % 16
        ws = w % 2
        WB = self.scr["WB_in"].rearrange("(dc p) n -> p dc n", p=128)
        s.op("sp", lambda h: h.dma_start(out=self.wb[ws][:, :, :], in_=WB[:, :, ob * 512:(ob + 1) * 512]),
             reads=[("WB_in", r) for r in range(16)], writes=[("wb", ws)], dma=True)

    def _p1_B(self, tt, csteps=()):
        s = self.s
        csteps = list(csteps)
        hT, wb, ev32, ev16, ps = self.hT, self.wb, self.ev32, self.ev16, self.ps
        hs = tt % 2
        if tt == 0:
            self._p1_wload(0)
            self._p1_xload(0, 0)
            self._p1_xload(0, 1)
            for sub in range(4):
                self._p1_N(0, sub)
                if sub + 2 < 4:
                    self._p1_xload(0, sub + 2)
        for ob in range(DIN // 512):
            w = tt * 16 + ob
            ws = w % 2
            self._p1_wload(w + 1)
            nsteps = 3 if ob == 0 else 2
            for _ in range(nsteps):
                if csteps:
                    csteps.pop(0)()
            if ob == 15:
                while csteps:
                    csteps.pop(0)()
            if tt + 1 < self.nt:
                for sub in range(4):
                    if ob == max(3 * sub - 1, 0) and sub < 2 or ob == 3 * sub - 3 and sub >= 2:
                        self._p1_xload(tt + 1, sub)
                    if ob == 1 + 3 * sub:
                        self._p1_N(tt + 1, sub, part="norm")
                    if ob == 3 + 3 * sub:
                        self._p1_N(tt + 1, sub, part="tr")
            for oc in range(4):
                col0 = ob * 512 + oc * 128
                ev_i = self.cnt["ev"]
                self.cnt["ev"] += 1
                bank = ev_i % 4
                for dc in range(16):
                    s.op("pe", lambda h, ws=ws, oc=oc, dc=dc, bank=bank, hs=hs: h.matmul(
                        ps[bank][:, :], wb[ws][:, dc, oc * 128:(oc + 1) * 128], hT[hs][:, dc, :],
                        start=(dc == 0), stop=(dc == 15)),
                        reads=[("wb", ws)] + [("hT", hs, q) for q in range(4)],
                        writes=[("ps", bank)])
                e = ev_i % 4
                if col0 < DP + 3 * DH:
                    s.op("act", lambda h, e=e, bank=bank: h.copy(ev32[e][:, :], ps[bank][:, :]),
                         reads=[("ps", bank)], writes=[("ev32", e)])
                    if col0 < DP:
                        dst = self.scr["PU"][col0:col0 + 128, tt * T:(tt + 1) * T]
                        key = ("PU", col0 // 128, tt)
                    else:
                        r0 = col0 - DP
                        dst = self.scr["HV"][r0:r0 + 128, tt * T:(tt + 1) * T]
                        key = ("HV", r0 // 128, tt)
                    s.op("act", lambda h, e=e, dst=dst: h.dma_start(out=dst, in_=ev32[e][:, :]),
                         reads=[("ev32", e)], writes=[key], dma=True)
                else:
                    r0 = col0 - (DP + 3 * DH)
                    s.op("act", lambda h, e=e, bank=bank: h.activation(ev16[e][:, :], ps[bank][:, :], AF.Sigmoid),
                         reads=[("ps", bank)], writes=[("ev16", e)])
                    dst = self.scr["SG"][r0:r0 + 128, tt * T:(tt + 1) * T]
                    s.op("act", lambda h, e=e, dst=dst: h.dma_start(out=dst, in_=ev16[e][:, :]),
                         reads=[("ev16", e)], writes=[("SG", r0 // 128, tt)], dma=True)

    def _load_halo(self, buf, bufkey, src, srckey, row0, tt, halo, eng_memset="pool"):
        s = self.s
        nt = self.nt
        lo = tt * T - halo
        hi = tt * T + T + halo
        clo, chi = max(lo, 0), min(hi, nt * T)
        if clo > lo:
            s.op(eng_memset, lambda h: h.memset(buf[:, 0:halo], 0.0), writes=[bufkey])
        if chi < hi:
            s.op(eng_memset, lambda h: h.memset(buf[:, T + halo:T + 2 * halo], 0.0), writes=[bufkey])
        rk = [(srckey, row0 // 128, q) for q in (tt - 1, tt, tt + 1) if 0 <= q < nt]
        s.op("pool", lambda h: h.dma_start(out=buf[:, clo - lo:chi - lo], in_=src[row0:row0 + 128, clo:chi]),
             reads=rk, writes=[bufkey], dma=True)
        hm = self.hmask
        if clo == lo:
            s.op(eng_memset, lambda h: h.tensor_scalar(buf[:, 0:halo], buf[:, 0:halo], hm[:, 2 * tt:2 * tt + 1],
                                                       None, ALU.mult),
                 reads=["hmask"], writes=[bufkey])
        if chi == hi:
            s.op(eng_memset, lambda h: h.tensor_scalar(buf[:, T + halo:T + 2 * halo], buf[:, T + halo:T + 2 * halo],
                                                       hm[:, 2 * tt + 1:2 * tt + 2], None, ALU.mult),
                 reads=["hmask"], writes=[bufkey])

    def _p1_Cfront(self, tt):
        for st_ in self._p1_Cfront_steps(tt):
            st_()

    def _p1_Cfront_steps(self, tt):
        s = self.s
        steps = []
        CV, lv, pooled, icnt = self.CV, self.lv, self.pooled, self.icnt
        ics = tt % 2
        cw, cb = self.convw, self.convb
        jobs = [("p", k) for k in range(8)] + [("c", k) for k in range(24)]
        base = self.cnt["cv"]
        self.cnt["cv"] += len(jobs)

        def slot(n):
            return (base + n) % 3

        def load(n):
            if n >= len(jobs):
                return
            kind, k = jobs[n]
            sl = slot(n)
            if kind == "p":
                self._load_halo(CV[sl], ("CV", sl), self.scr["PU"], "PU", k * 128, tt, 8)
            else:
                self._load_halo(CV[sl], ("CV", sl), self.scr["HV"], "HV", k * 128, tt, 1)

        def init():
            s.op("pool", lambda h: h.dma_start(
                out=icnt[ics][:, :, :],
                in_=self.c["invcnt"][:, tt * T:(tt + 1) * T].partition_broadcast(128)),
                writes=[("icnt", ics)], dma=True)
            load(0)
            load(1)
        steps.append(init)

        def job(n, kind, k):
            sl = slot(n)
            buf = CV[sl]
            bk = ("CV", sl)
            if kind == "p":
                g = k // 2
                cur, curkey = buf, bk
                rngs = [(1, T + 16, 1, 0), (2, T + 15, 1, 1), (4, T + 13, 2, 2), (8, T + 9, 4, 4)]
                for lvl in range(g + 1):
                    a, b, dl, dr = rngs[lvl]
                    dst = lv[lvl]
                    s.op("dve", lambda h, dst=dst, cur=cur, a=a, b=b, dl=dl, dr=dr: h.tensor_tensor(
                        dst[:, a:b], cur[:, a - dl:b - dl], cur[:, a + dr:b + dr], ALU.add),
                        reads=[curkey], writes=[("lv", lvl)])
                    cur, curkey = dst, ("lv", lvl)
                tmp = lv[g]
                s.op("dve", lambda h, tmp=tmp, cur=cur, g=g: h.tensor_tensor(
                    tmp[:, 8:T + 8], cur[:, 8:T + 8], icnt[ics][:, g, :], ALU.mult),
                    reads=[curkey, ("icnt", ics)], writes=[("lv", g)])
                pl = (tt % 2) * 8 + k
                s.op("dve", lambda h, tmp=tmp, buf=buf, pl=pl: h.tensor_tensor(
                    pooled[pl][:, :], tmp[:, 8:T + 8], buf[:, 8:T + 8], ALU.subtract),
                    reads=[("lv", g), bk], writes=[("pooled", pl)])
            else:
                hc = k
                vs = n % 3
                acc, co_ = self.cacc[vs], self.cout[vs]
                s.op("dve", lambda h, buf=buf, acc=acc, hc=hc: h.tensor_scalar(
                    acc[:, :], buf[:, 1:T + 1], cw[:, 1, hc:hc + 1], cb[:, hc:hc + 1], ALU.mult, ALU.add),
                    reads=[bk, "convw", "convb"], writes=[("cacc", vs)])
                s.op("dve", lambda h, buf=buf, acc=acc, hc=hc: h.scalar_tensor_tensor(
                    acc[:, :], buf[:, 0:T], cw[:, 0, hc:hc + 1], acc[:, :], ALU.mult, ALU.add),
                    reads=[bk, "convw"], writes=[("cacc", vs)])
                s.op("dve", lambda h, buf=buf, acc=acc, co_=co_, hc=hc: h.scalar_tensor_tensor(
                    co_[:, :], buf[:, 2:T + 2], cw[:, 2, hc:hc + 1], acc[:, :], ALU.mult, ALU.add),
                    reads=[bk, "convw", ("cacc", vs)], writes=[("cout", vs)])
                dst = self.scr["HC"][hc * 128:(hc + 1) * 128, tt * T:(tt + 1) * T]
                s.op("pool", lambda h, co_=co_, dst=dst: h.dma_start(out=dst, in_=co_[:, :]),
                     reads=[("cout", vs)], writes=[("HC", hc, tt)], dma=True)
            load(n + 2)
        for n, (kind, k) in enumerate(jobs):
            steps.append(lambda n=n, kind=kind, k=k: job(n, kind, k))
        return steps

    def _p1_Cback(self, tt):
        s, ps = self.s, self.ps
        pooled = self.pooled
        for g in range(4):
            pls = [(tt % 2) * 8 + 2 * g + kc for kc in range(2)]
            for oc in range(2):
                bank = 4 + (oc % 2)
                for kc in range(2):
                    s.op("pe", lambda h, g=g, kc=kc, oc=oc, bank=bank, pl=pls[kc]: h.matmul(
                        ps[bank][:, :], self.poolw[:, 2 * g + kc, oc * 128:(oc + 1) * 128], pooled[pl][:, :],
                        start=(kc == 0), stop=(kc == 1)),
                        reads=["poolw", ("pooled", pls[kc])], writes=[("ps", bank)])
                ae = self.cnt["ae"] % 2
                self.cnt["ae"] += 1
                co = 2 * g + oc
                s.op("act", lambda h, ae=ae, bank=bank, co=co: h.activation(
                    self.aev[ae][:, :], ps[bank][:, :], AF.Copy, scale=self.pscale[:, co:co + 1]),
                    reads=[("ps", bank), "pscale"], writes=[("aev", ae)])
                dst = self.scr["AT"][co * 128:(co + 1) * 128, tt * T:(tt + 1) * T]
                s.op("act", lambda h, ae=ae, dst=dst: h.dma_start(out=dst, in_=self.aev[ae][:, :]),
                     reads=[("aev", ae)], writes=[("AT", co, tt)], dma=True)

    def phase2a(self):
        self._phase_id += 1
        nc, s, ps = self.nc, self.s, self.ps
        TWO_PI = 2.0 * math.pi
        MAGIC = 12582912.0
        with ExitStack() as ph:
            def sb(name, shape, dt):
                return ph.enter_context(nc.sbuf_tensor(f"sb{self._uid()}_" + name, shape, dt))
            zf = [sb(f"zf{i}", [33, 512], F32) for i in range(2)]
            w1 = sb("fw1", [33, 64], F32)
            w2 = sb("fw2", [64, 64], F32)
            fcol = sb("fcol", [64, 8], F32)
            w3b = sb("w3b", [64, 4 * DH], BF16)
            h2T = sb("h2T", [64, NTOK], BF16)
            misc = sb("misc2a", [128, 16], F32)
            hbias = sb("hbias", [128, 16], F32)
            tch = [sb(f"tch{i}", [128, 2048], F32) for i in range(2)]
            dec = sb("dec", [128, NTOK], F32)
            fwb = [[sb(f"fwb{i}_{j}", [128, NTOK], BF16) for j in range(2)] for i in range(2)]
            junk = sb("junk", [128, 2048], BF16)
            fo = [sb(f"fo{i}", [128, 1024], BF16) for i in range(4)]
            pmt = [sb(f"pmt{i}", [128, 1024], F32) for i in range(4)]
            mt = [sb(f"mt{i}", [64, 512], F32) for i in range(8)]
            sm = [sb(f"sm{i}", [128, 16], F32) for i in range(2)]

            s.op("sp", lambda h: h.dma_start(out=w1[:, :], in_=self.w["filt_w1"]), writes=["fw1"], dma=True)
            s.op("sp", lambda h: h.dma_start(out=w2[:, :], in_=self.w["filt_w2"]), writes=["fw2"], dma=True)
            for i, nm in enumerate(("filt_b1", "filt_freq1", "filt_b2", "filt_freq2")):
                s.op("sp", lambda h, i=i, nm=nm: h.dma_start(
                    out=fcol[:, i:i + 1], in_=self.w[nm].rearrange("o (c p) -> p (o c)", p=64),
                    allow_slow_non_contiguous=True), writes=["fcol"], dma=True)
            s.op("pool", lambda h: h.dma_start(out=w3b[:, :], in_=self.w["filt_w3"]), writes=["w3b"], dma=True)
            s.op("sp", lambda h: h.dma_start(out=misc[:, :], in_=self.c["misc"]), writes=["misc"], dma=True)
            for o in range(2):
                s.op("sp", lambda h, o=o: h.dma_start(
                    out=hbias[:, o * 8:(o + 1) * 8],
                    in_=self.w["hyena_bias"][o:o + 1, :].rearrange("o (c p) -> p (o c)", p=128),
                    allow_slow_non_contiguous=True), writes=["hbias"], dma=True)
            s.op("dve", lambda h: h.tensor_tensor(fcol[:, 4:5], fcol[:, 0:1], fcol[:, 1:2], ALU.mult),
                 reads=["fcol"], writes=["fcol"])
            s.op("dve", lambda h: h.tensor_tensor(fcol[:, 5:6], fcol[:, 2:3], fcol[:, 3:4], ALU.mult),
                 reads=["fcol"], writes=["fcol"])

            def sin_layer(psb, fi, out_ap, outkey, ms):
                u, k, r = mt[ms * 4 + 0], mt[ms * 4 + 1], mt[ms * 4 + 2]
                ku, kk, kr = ("mt", ms, 0), ("mt", ms, 1), ("mt", ms, 2)
                s.op("dve", lambda h: h.tensor_scalar(u[:, :], ps[psb][0:64, :], fcol[:, fi:fi + 1],
                                                      fcol[:, 4 + fi // 2:5 + fi // 2], ALU.mult, ALU.add),
                     reads=[("ps", psb), "fcol"], writes=[ku])
                s.op("dve", lambda h: h.tensor_scalar(k[:, :], u[:, :], 1.0 / TWO_PI, MAGIC, ALU.mult, ALU.add),
                     reads=[ku], writes=[kk])
                s.op("dve", lambda h: h.tensor_scalar(k[:, :], k[:, :], -MAGIC, -TWO_PI, ALU.add, ALU.mult),
                     reads=[kk], writes=[kk])
                s.op("dve", lambda h: h.tensor_tensor(r[:, :], k[:, :], u[:, :], ALU.add),
                     reads=[ku, kk], writes=[kr])
                s.op("dve", lambda h: h.tensor_scalar(r[:, :], r[:, :], -math.pi, math.pi, ALU.max, ALU.min),
                     reads=[kr], writes=[kr])
                s.op("act", lambda h: h.activation(out_ap, r[:, :], AF.Sin), reads=[kr], writes=[outkey])

            nchk = NTOK // 512

            def mlp_l1(ch):
                ms = ch % 2
                zs = ch % 2
                b1_ = (0, 6)[ms]
                s.op("sp", lambda h: h.dma_start(out=zf[zs][:, :], in_=self.c["zfeatT"][:, ch * 512:(ch + 1) * 512]),
                     writes=[("zf", zs)], dma=True)
                s.op("pe", lambda h: h.matmul(ps[b1_][0:64, :], w1[:, :], zf[zs][:, :], start=True, stop=True),
                     reads=["fw1", ("zf", zs)], writes=[("ps", b1_)])
                sin_layer(b1_, 1, mt[ms * 4 + 3][:, :], ("mt", ms, 3), ms)

            def mlp_l2(ch):
                ms = ch % 2
                b2_ = (1, 7)[ms]
                s.op("pe", lambda h: h.matmul(ps[b2_][0:64, :], w2[:, :], mt[ms * 4 + 3][:, :], start=True, stop=True),
                     reads=["fw2", ("mt", ms, 3)], writes=[("ps", b2_)])
                sin_layer(b2_, 3, h2T[:, ch * 512:(ch + 1) * 512], "h2T", ms)

            mlp_l1(0)
            for ch in range(nchk):
                if ch + 1 < nchk:
                    mlp_l1(ch + 1)
                mlp_l2(ch)

            its = [(cc, o) for cc in range(8) for o in range(2)]
            cnt = {"fo": 0, "pc": 0, "tq": 0}

            def do_dec(cc):
                for q in range(4):
                    ts_ = cnt["tq"] % 2
                    cnt["tq"] += 1
                    s.op("sp", lambda h, ts_=ts_, q=q: h.dma_start(
                        out=tch[ts_][:, :], in_=self.c["tvals"][:, q * 2048:(q + 1) * 2048].partition_broadcast(128)),
                        writes=[("tch", ts_)], dma=True)
                    s.op("act", lambda h, ts_=ts_, q=q: h.activation(
                        dec[:, q * 2048:(q + 1) * 2048], tch[ts_][:, :], AF.Exp, scale=misc[:, 1 + cc:2 + cc]),
                        reads=[("tch", ts_), "misc"], writes=["dec"])

            def mults(it):
                cc, o = its[it]
                ss = it % 2
                fw = fwb[ss]
                for dr in range(2):
                    r0 = o * 2 * DH + dr * DH + cc * 128
                    for ch in range(nchk):
                        bank = 2 + ch % 4
                        s.op("pe", lambda h, r0=r0, ch=ch, bank=bank: h.matmul(
                            ps[bank][:, :], w3b[:, r0:r0 + 128], h2T[:, ch * 512:(ch + 1) * 512],
                            start=True, stop=True),
                            reads=["w3b", "h2T"], writes=[("ps", bank)])
                        s.op("dve", lambda h, dr=dr, ch=ch, bank=bank: h.tensor_tensor(
                            fw[dr][:, ch * 512:(ch + 1) * 512], ps[bank][:, :], dec[:, ch * 512:(ch + 1) * 512],
                            ALU.mult),
                            reads=[("ps", bank), "dec"], writes=[("fwb", ss, dr)])

            def tail(it):
                cc, o = its[it]
                ss = it % 2
                fw = fwb[ss]
                smt = sm[ss]
                for dr in range(2):
                    for q in range(4):
                        s.op("act", lambda h, dr=dr, q=q: h.activation(
                            junk[:, :], fw[dr][:, q * 2048:(q + 1) * 2048], AF.Square,
                            accum_out=smt[:, 8 + dr * 4 + q:9 + dr * 4 + q]),
                            reads=[("fwb", ss, dr)], writes=["junk", ("sq", ss)])
                s.op("dve", lambda h: h.tensor_reduce(smt[:, 2:3], smt[:, 8:16], AX.X, ALU.add),
                     reads=[("sq", ss)], writes=[("sm2", ss)])
                s.op("dve", lambda h: h.tensor_tensor(smt[:, 3:4], fw[0][:, 0:1], fw[1][:, 0:1], ALU.mult),
                     reads=[("fwb", ss, 0), ("fwb", ss, 1)], writes=[("sm3", ss)])
                s.op("dve", lambda h: h.tensor_scalar(smt[:, 2:3], smt[:, 2:3], misc[:, 0:1], None, ALU.mult),
                     reads=[("sm2", ss), "misc"], writes=[("sm2", ss)])
                s.op("dve", lambda h: h.scalar_tensor_tensor(smt[:, 4:5], smt[:, 3:4], 2.0, smt[:, 2:3],
                                                             ALU.mult, ALU.add),
                     reads=[("sm2", ss), ("sm3", ss)], writes=[("sm4", ss)])
                s.op("act", lambda h: h.activation(smt[:, 5:6], smt[:, 4:5], AF.Sqrt, bias=EPS, scale=1.0),
                     reads=[("sm4", ss)], writes=[("sm5", ss)])
                s.op("dve", lambda h: h.reciprocal(smt[:, 6:7], smt[:, 5:6]),
                     reads=[("sm5", ss)], writes=[("sm6", ss)])
                s.op("dve", lambda h: h.tensor_tensor(
                    smt[:, 7:8], hbias[:, o * 8 + cc:o * 8 + cc + 1], smt[:, 5:6], ALU.mult),
                    reads=[("sm5", ss), "hbias"], writes=[("sm7", ss)])
                s.op("dve", lambda h: h.tensor_tensor(fw[0][:, 0:1], fw[0][:, 0:1], smt[:, 7:8], ALU.add),
                     reads=[("sm7", ss), ("fwb", ss, 0), ("fwb", ss, 1)], writes=[("fwb", ss, 0)])
                s.op("dve", lambda h: h.scalar_tensor_tensor(
                    fw[0][:, 4096:4097], smt[:, 7:8], misc[:, 9:10], fw[0][:, 4096:4097], ALU.mult, ALU.add),
                    reads=[("sm7", ss), "misc"], writes=[("fwb", ss, 0)])
                for q in range(8):
                    csl = slice(q * 1024, (q + 1) * 1024)
                    for dr, op, eng in ((0, ALU.add, "dve"), (1, ALU.subtract, "pool")):
                        pcs = cnt["pc"] % 4
                        cnt["pc"] += 1
                        r0 = o * 2 * DH + dr * DH + cc * 128
                        s.op(eng, lambda h, csl=csl, op=op, pcs=pcs: h.tensor_tensor(
                            pmt[pcs][:, :], fw[0][:, csl], fw[1][:, csl], op),
                            reads=[("fwb", ss, 0), ("fwb", ss, 1)], writes=[("pmt", pcs)])
                        fs = cnt["fo"] % 4
                        cnt["fo"] += 1
                        s.op("act", lambda h, pcs=pcs, fs=fs: h.activation(
                            fo[fs][:, :], pmt[pcs][:, :], AF.Copy, scale=smt[:, 6:7]),
                            reads=[("pmt", pcs), ("sm6", ss)], writes=[("fo", fs)])
                        s.op("sp", lambda h, fs=fs, r0=r0, csl=csl: h.dma_start(
                            out=self.scr["FT"][r0:r0 + 128, csl], in_=fo[fs][:, :]),
                            reads=[("fo", fs)], writes=[("FT", r0 // 64), ("FT", r0 // 64 + 1)], dma=True)

            do_dec(0)
            mults(0)
            for it in range(len(its)):
                if it + 1 < len(its):
                    if its[it + 1][1] == 0:
                        do_dec(its[it + 1][0])
                    mults(it + 1)
                tail(it)
            s.barrier()

    def _fft_consts(self, sb, need_inv):
        s = self.s
        self.dA = sb("dA", [64, 128], BF16)
        self.dB = sb("dB", [128, 64, 3, 128], BF16)
        s.op("sp", lambda h: h.dma_start(out=self.dA[:, :], in_=self.c["dftA"]), writes=["dA"], dma=True)
        for i in range(4):
            s.op("sp", lambda h, i=i: h.dma_start(
                out=self.dB[:, i * 16:(i + 1) * 16, :, :].rearrange("p k m q -> p (k m q)"),
                in_=self.c["dftB"][:, i * 6144:(i + 1) * 6144]),
                writes=["dB"], dma=True)
        if need_inv:
            self.dF = sb("dF", [128, 2, 128], BF16)
            s.op("sp", lambda h: h.dma_start(out=self.dF[:, :, :],
                                             in_=self.c["dftF"].rearrange("k (f p) -> k f p", f=2)),
                 writes=["dF"], dma=True)

    def _load_jcp(self, dst, dstkey, src_rows, reads=()):
        self.s.op("sp", lambda h: h.dma_start(out=dst[:, :, :],
                                              in_=src_rows.rearrange("c (j p) -> j c p", p=128)),
                  reads=list(reads), writes=[dstkey], dma=True)

    def _stageA(self, src, srckey, YP, ypkey="YP", nbanks=2, alt=False):
        s, ps = self.s, self.ps
        Yv = YP[:, :].rearrange("p (m c e) -> p m c e", c=64, e=2)
        for c4 in range(16):
            bank = (0, 1, 6, 7)[c4 % nbanks]
            for i in range(4):
                c = c4 * 4 + i
                s.op("pe", lambda h, c=c, i=i, bank=bank: h.matmul(
                    ps[bank][:, i * 128:(i + 1) * 128], src[:, c, :], self.dA[:, :], start=True, stop=True),
                    reads=[srckey, "dA"], writes=[("ps", bank)])
            if alt and c4 % 2 == 1:
                s.op("dve", lambda h, c4=c4, bank=bank: h.tensor_copy(
                    Yv[:, :, c4 * 4:(c4 + 1) * 4, :], ps[bank][:, :].rearrange("p (c m e) -> p m c e", c=4, e=2)),
                    reads=[("ps", bank)], writes=[ypkey])
            else:
                s.op("act", lambda h, c4=c4, bank=bank: h.copy(
                    Yv[:, :, c4 * 4:(c4 + 1) * 4, :], ps[bank][:, :].rearrange("p (c m e) -> p m c e", c=4, e=2)),
                    reads=[("ps", bank)], writes=[ypkey])

    def _stageB(self, YP, evac, ypkey="YP", YPi=None, ypkey_i=None):
        s, ps = self.s, self.ps
        Y = YP[:, :].rearrange("p (m c e) -> p m c e", c=64, e=2)
        Y2 = Y if YPi is None else YPi[:, :].rearrange("p (m c e) -> p m c e", c=64, e=2)
        ypkey_i = ypkey if ypkey_i is None else ypkey_i
        dB = self.dB
        for kb in range(8):
            br = 2 + (kb % 2) * 2
            bi = br + 1
            for i in range(8):
                kc = kb * 8 + i
                o_r = ps[br][:, i * 64:(i + 1) * 64]
                o_i = ps[bi][:, i * 64:(i + 1) * 64]
                Yr = Y[:, kc // 2, :, kc % 2]
                Yi = Y[:, 32 + kc // 2, :, kc % 2]
                Yr2 = Y2[:, kc // 2, :, kc % 2]
                Yi2 = Y2[:, 32 + kc // 2, :, kc % 2]
                for (out, m, rhs, st, sp_, bk, yk) in ((o_r, 0, Yr, True, False, br, ypkey),
                                                       (o_r, 2, Yi, False, True, br, ypkey),
                                                       (o_i, 1, Yr2, True, False, bi, ypkey_i),
                                                       (o_i, 0, Yi2, False, True, bi, ypkey_i)):
                    s.op("pe", lambda h, out=out, m=m, rhs=rhs, st=st, sp_=sp_, kc=kc: h.matmul(
                        out, dB[:, kc, m, :], rhs, start=st, stop=sp_),
                        reads=[yk, "dB"], writes=[("ps", bk)])
            evac(kb, br, bi)

    def phase2b(self):
        self._phase_id += 1
        nc, s, ps = self.nc, self.s, self.ps
        with ExitStack() as ph:
            def sb(name, shape, dt):
                return ph.enter_context(nc.sbuf_tensor(f"sb{self._uid()}_" + name, shape, dt))
            self._fft_consts(sb, False)
            zin = [sb(f"fzin{i}", [64, 64, 128], BF16) for i in range(3)]
            YPs = [sb(f"fYP{i}", [128, 8192], BF16) for i in range(2)]
            Ho = [sb(f"Ho{i}", [128, 2, 64, 64], BF16) for i in range(2)]
            jobs = [(o, cg, dr) for o in range(2) for cg in range(self.ncg) for dr in range(2)]

            def load(n):
                if n >= len(jobs):
                    return
                o, cg, dr = jobs[n]
                r0 = o * 2 * DH + dr * DH + cg * 64
                self._load_jcp(zin[n % 3], ("fzin", n % 3), self.scr["FT"][r0:r0 + 64, :])
            load(0)
            load(1)
            for n, (o, cg, dr) in enumerate(jobs):
                hs_ = (o * 16 + cg) % 2
                zs = n % 3
                self._stageA(zin[zs], ("fzin", zs), YPs[dr], ("YP", dr), nbanks=4)
                load(n + 2)
                if dr == 1:
                    def evac(kb, br, bi, hs_=hs_):
                        s.op("act", lambda h: h.copy(
                            Ho[hs_][:, 0, kb * 8:(kb + 1) * 8, :], ps[br][:, :].rearrange("p (k c) -> p k c", k=8)),
                            reads=[("ps", br)], writes=[("Ho", hs_)])
                        s.op("dve", lambda h: h.tensor_copy(
                            Ho[hs_][:, 1, kb * 8:(kb + 1) * 8, :], ps[bi][:, :].rearrange("p (k c) -> p k c", k=8)),
                            reads=[("ps", bi)], writes=[("Ho1", hs_)])
                    self._stageB(YPs[0], evac, ("YP", 0), YPi=YPs[1], ypkey_i=("YP", 1))
                    s.op("pool", lambda h, o=o, cg=cg, hs_=hs_: h.dma_start(
                        out=self.scr["HS"][o, cg, :, :], in_=Ho[hs_][:, :, :, :].rearrange("p s k c -> p (s k c)")),
                        reads=[("Ho", hs_), ("Ho1", hs_)], writes=[("HS", o, cg)], dma=True)
            s.barrier()

    def phase2c(self):
        self._phase_id += 1
        nc, s, ps = self.nc, self.s, self.ps
        with ExitStack() as ph:
            def sb(name, shape, dt):
                return ph.enter_context(nc.sbuf_tensor(f"sb{self._uid()}_" + name, shape, dt))
            self._fft_consts(sb, True)
            dF = self.dF
            z = [sb(f"z{i}", [64, 64, 128], BF16) for i in range(3)]
            gate = sb("gate", [64, 64, 128], BF16)
            YP = sb("YP", [128, 8192], BF16)
            P1 = sb("P1", [128, 8192], BF16)
            S = sb("S", [128, 2, 64, 64], BF16)
            H = sb("H", [128, 2, 64, 64], BF16)
            Gs = [sb(f"Gs{i}", [128, 8, 2, 64], BF16) for i in range(4)]
            tmp = [sb(f"tmp{i}", [128, 512], F32) for i in range(8)]
            HC = self.scr["HC"]
            ng = 0
            def gload(pb):
                gs = pb % 4
                s.op("sp", lambda h: h.dma_start(
                    out=Gs[gs][:, :, :, :].rearrange("r a f j -> r (a f j)"),
                    in_=self.c["dftG"][:, pb * 1024:(pb + 1) * 1024]),
                    writes=[("Gs", gs)], dma=True)
            self._load_jcp(z[0], ("z", 0), HC[0:64, :])
            for cg in range(self.ncg):
                zi = [(cg + q) % 3 for q in range(3)]
                for o in range(2):
                    src, dst = z[zi[o]], z[zi[o + 1]]
                    srck, dstk = ("z", zi[o]), ("z", zi[o + 1])
                    s.op("sp", lambda h, o=o, cg=cg: h.dma_start(
                        out=H[:, :, :, :].rearrange("p s k c -> p (s k c)"), in_=self.scr["HS"][o, cg, :, :]),
                        writes=["H"], dma=True)
                    g0 = (1 + o) * DH + cg * 64
                    self._load_jcp(gate, "gate", HC[g0:g0 + 64, :])
                    self._stageA(src, srck, YP, alt=True)
                    if o == 1 and cg + 1 < self.ncg:
                        self._load_jcp(z[zi[1]], ("z", zi[1]), HC[(cg + 1) * 64:(cg + 2) * 64, :])

                    def evac(kb, br, bi):
                        t = tmp[(kb % 2) * 4:(kb % 2) * 4 + 4]
                        tk = [("tmp", (kb % 2) * 4 + q) for q in range(4)]
                        Hr = H[:, 0, kb * 8:(kb + 1) * 8, :].rearrange("p k c -> p (k c)")
                        Hi = H[:, 1, kb * 8:(kb + 1) * 8, :].rearrange("p k c -> p (k c)")
                        for q, (bk, hh) in enumerate(((br, Hr), (bi, Hi), (br, Hi), (bi, Hr))):
                            s.op("dve", lambda h, q=q, bk=bk, hh=hh: h.tensor_tensor(
                                t[q][:, :], ps[bk][:, :], hh, ALU.mult),
                                reads=[("ps", bk), "H"], writes=[tk[q]])
                        s.op("pool", lambda h: h.tensor_tensor(
                            S[:, 0, kb * 8:(kb + 1) * 8, :].rearrange("p k c -> p (k c)"), t[0][:, :], t[1][:, :],
                            ALU.subtract), reads=[tk[0], tk[1]], writes=["S"])
                        s.op("pool", lambda h: h.tensor_tensor(
                            S[:, 1, kb * 8:(kb + 1) * 8, :].rearrange("p k c -> p (k c)"), t[2][:, :], t[3][:, :],
                            ALU.add), reads=[tk[2], tk[3]], writes=["S"])
                    self._stageB(YP, evac)
                    if "dbgS" in self.debug and cg == 0 and o == 0:
                        s.op("sp", lambda h: h.dma_start(out=self.scr["dbgS"], in_=S[:, :, :, :].rearrange("p s k c -> p (s k c)")),
                             reads=["S"], writes=["dbgS"], dma=True)
                        s.op("sp", lambda h: h.dma_start(out=self.scr["dbgY"], in_=YP[:, :]),
                             reads=["YP"], writes=["dbgY"], dma=True)
                    gload(0)
                    gload(1)
                    gload(2)
                    Sv = S[:, :, :, :].rearrange("k s q c -> k c (s q)")
                    Pb = [YP, P1]
                    Pk = ["YP", "P1"]
                    nev = 0
                    for phh in range(2):
                        Pv = Pb[phh][:, :].rearrange("r (q c e) -> r q c e", c=64, e=2)
                        for c4 in range(16):
                            bank = 4 + c4 % 4
                            for i in range(4):
                                c = c4 * 4 + i
                                s.op("pe", lambda h, c=c, i=i, bank=bank, phh=phh: h.matmul(
                                    ps[bank][:, i * 128:(i + 1) * 128].rearrange("r (f p) -> r f p", f=2),
                                    Sv[:, c, :], dF[:, :, phh * 64:(phh + 1) * 64], start=True, stop=True),
                                    reads=["S", "dF"], writes=[("ps", bank)])
                            eng = "act" if nev % 2 == 0 else "dve"
                            nev += 1
                            srcv = ps[bank][:, :].rearrange("r (c q e) -> r q c e", c=4, e=2)
                            dstv = Pv[:, :, c4 * 4:(c4 + 1) * 4, :]
                            if eng == "act":
                                fn = lambda h, dstv=dstv, srcv=srcv: h.copy(dstv, srcv)
                            else:
                                fn = lambda h, dstv=dstv, srcv=srcv: h.tensor_copy(dstv, srcv)
                            s.op(eng, fn, reads=[("ps", bank)], writes=[Pk[phh]])
                    if "dbgS" in self.debug and cg == 0 and o == 0:
                        s.op("sp", lambda h: h.dma_start(out=self.scr["dbgP0"], in_=YP[:, :]),
                             reads=["YP"], writes=["dbgP0"], dma=True)
                        s.op("sp", lambda h: h.dma_start(out=self.scr["dbgP1"], in_=P1[:, :]),
                             reads=["P1"], writes=["dbgP1"], dma=True)
                    dstv_all = dst
                    gatev = gate
                    for pb in range(16):
                        phh = pb // 8
                        P4 = Pb[phh][:, :].rearrange("r (q c e) -> r q c e", c=64, e=2)
                        gs = pb % 4
                        if pb + 3 < 16:
                            gload(pb + 3)
                        bank = pb % 2
                        for i in range(8):
                            pl = (pb * 8 + i) % 64
                            for f in range(2):
                                s.op("pe", lambda h, i=i, f=f, pl=pl, gs=gs, bank=bank, P4=P4: h.matmul(
                                    ps[bank][0:64, i * 64:(i + 1) * 64], Gs[gs][:, i, f, :], P4[:, f * 32 + pl // 2, :, pl % 2],
                                    start=(f == 0), stop=(f == 1)),
                                    reads=[("Gs", gs), Pk[phh]], writes=[("ps", bank)])
                        s.op("dve", lambda h, pb=pb, bank=bank, dstv_all=dstv_all, gatev=gatev: h.tensor_tensor(
                            dstv_all[:, :, pb * 8:(pb + 1) * 8],
                            ps[bank][0:64, :].rearrange("j (p c) -> j c p", p=8),
                            gatev[:, :, pb * 8:(pb + 1) * 8], ALU.mult),
                            reads=[("ps", bank), "gate"], writes=[dstk])
                if "dbgS" in self.debug and cg == 0:
                    pass
                zf = zi[2]
                s.op("pool", lambda h, cg=cg, zf=zf: h.dma_start(
                    out=self.scr["BT"][cg * 64:(cg + 1) * 64, :].rearrange("c (j p) -> j c p", p=128),
                    in_=z[zf][:, :, :]),
                    reads=[("z", zf)], writes=[("BT", cg)], dma=True)
            s.barrier()

    def phase3(self):
        self._phase_id += 1
        nc, s, ps = self.nc, self.s, self.ps
        NF = DFF // 128
        with ExitStack() as ph:
            def sb(name, shape, dt):
                return ph.enter_context(nc.sbuf_tensor(f"sb{self._uid()}_" + name, shape, dt))
            ident = sb("ident", [128, 128], BF16)
            gf = sb("gffn", [128, D], F32)
            gl = sb("gfin", [128, D], F32)
            xt = sb("x3", [128, 4, D], F32)
            act = sb("act3", [128, NF, T], BF16)
            abT = sb("abT", [128, 16, T], BF16)
            mh = sb("mh", [128, 16, T], BF16)
            ring = [sb(f"ring{i}", [128, 8192], BF16) for i in range(3)]
            sg = [sb(f"sg{i}", [128, 2, T], BF16) for i in range(4)]
            tmp = [sb(f"tm{i}", [128, T], F32) for i in range(4)]
            hb = [sb(f"hb3{i}", [128, D], BF16) for i in range(2)]
            st = [sb(f"st3{i}", [128, 4], F32) for i in range(2)]
            psT = [ps[6][:, :].bitcast(BF16), ps[7][:, :].bitcast(BF16)]
            W = self.scr
            rc = [0]

            def wload(fn_in, shape_view, srckeys):
                slot = rc[0] % 3
                rc[0] += 1
                s.op("sp", lambda h, slot=slot: fn_in(h, ring[slot]), reads=srckeys,
                     writes=[("ring", slot), ("ringb", slot)], dma=True)
                return slot

            s.op("pool", lambda h: h.dma_start(out=ident[:, :], in_=self.c["ident"]), writes=["ident"], dma=True)
            s.op("pool", lambda h: h.dma_start(out=gf[:, :], in_=self.w["g_ffn"].partition_broadcast(128)),
                 writes=["gf"], dma=True)
            s.op("pool", lambda h: h.dma_start(out=gl[:, :], in_=self.w["g_final"].partition_broadcast(128)),
                 writes=["gl"], dma=True)
            nsg = 0
            nhb = 0
            for tt in range(self.nt):
                tsl = slice(tt * T, (tt + 1) * T)

                def ab_load(tsl_):
                    s.op("pool", lambda h: h.dma_start(
                        out=abT[:, 0:8, :], in_=W["AT"][:, tsl_].rearrange("(k p) t -> p k t", p=128)),
                        writes=[("ab", k) for k in range(8)], dma=True)
                    s.op("pool", lambda h: h.dma_start(
                        out=abT[:, 8:16, :], in_=W["BT"][:, tsl_].rearrange("(k p) t -> p k t", p=128)),
                        writes=[("ab", k) for k in range(8, 16)], dma=True)

                def sgload(dc, tsl_=tsl):
                    if dc >= 16:
                        return
                    gs_ = dc % 4
                    s.op("pool", lambda h: h.dma_start(
                        out=sg[gs_][:, 0, :], in_=W["SG"][dc * 128:(dc + 1) * 128, tsl_]),
                        writes=[("sg", gs_)], dma=True)
                    s.op("pool", lambda h: h.dma_start(
                        out=sg[gs_][:, 1, :], in_=W["SG"][D + dc * 128:D + (dc + 1) * 128, tsl_]),
                        writes=[("sgb", gs_)], dma=True)
                if tt == 0:
                    ab_load(tsl)
                    sgload(0)
                    sgload(1)
                s.op("pool", lambda h, tsl=tsl: h.dma_start(
                    out=xt[:, :, :], in_=self.x[tsl, :].rearrange("(s p) d -> p s d", p=128)),
                    writes=[("x", q) for q in range(4)], dma=True)
                for db in range(4):
                    slot = wload(lambda h, r, db=db: (h.dma_start(
                        out=r[:, 0:4096].rearrange("p (k n) -> p k n", k=8),
                        in_=W["WB_a"][:, db * 512:(db + 1) * 512].rearrange("(k p) n -> p k n", p=128))), None, [])
                    s.op("sp", lambda h, slot=slot, db=db: h.dma_start(
                        out=ring[slot][:, 4096:8192].rearrange("p (k n) -> p k n", k=8),
                        in_=W["WB_b"][:, db * 512:(db + 1) * 512].rearrange("(k p) n -> p k n", p=128)),
                        writes=[("ringb", slot)], reads=[("ring", slot)], dma=True)
                    wa = ring[slot][:, 0:4096].rearrange("p (k n) -> p k n", k=8)
                    wbv = ring[slot][:, 4096:8192].rearrange("p (k n) -> p k n", k=8)
                    for dq in range(4):
                        dc = db * 4 + dq
                        gs_ = dc % 4
                        sgload(dc + 2)
                        ba = (dc % 2) * 2
                        bb = ba + 1
                        for kc in range(8):
                            s.op("pe", lambda h, kc=kc, dq=dq, ba=ba, wa=wa: h.matmul(
                                ps[ba][:, :], wa[:, kc, dq * 128:(dq + 1) * 128], abT[:, kc, :],
                                start=(kc == 0), stop=(kc == 7)),
                                reads=[("ring", slot), ("ab", kc)], writes=[("ps", ba)])
                        for kc in range(8):
                            s.op("pe", lambda h, kc=kc, dq=dq, bb=bb, wbv=wbv: h.matmul(
                                ps[bb][:, :], wbv[:, kc, dq * 128:(dq + 1) * 128], abT[:, 8 + kc, :],
                                start=(kc == 0), stop=(kc == 7)),
                                reads=[("ringb", slot), ("ab", 8 + kc)], writes=[("ps", bb)])
                        ta, tb = tmp[(dc % 2) * 2], tmp[(dc % 2) * 2 + 1]
                        ka, kb_ = ("tm", (dc % 2) * 2), ("tm", (dc % 2) * 2 + 1)
                        s.op("dve", lambda h, ta=ta, ba=ba, gs_=gs_: h.tensor_tensor(
                            ta[:, :], ps[ba][:, :], sg[gs_][:, 0, :], ALU.mult),
                            reads=[("ps", ba), ("sg", gs_)], writes=[ka])
                        s.op("dve", lambda h, tb=tb, bb=bb, gs_=gs_: h.tensor_tensor(
                            tb[:, :], ps[bb][:, :], sg[gs_][:, 1, :], ALU.mult),
                            reads=[("ps", bb), ("sgb", gs_)], writes=[kb_])
                        s.op("pool", lambda h, ta=ta, tb=tb, dc=dc: h.tensor_tensor(
                            mh[:, dc, :], ta[:, :], tb[:, :], ALU.add),
                            reads=[ka, kb_], writes=[("mh", dc)])
                for nb in range(4):
                    slot = wload(lambda h, r, nb=nb: h.dma_start(
                        out=r[:, :].rearrange("p (k n) -> p k n", k=16),
                        in_=W["WB_out"][:, nb * 512:(nb + 1) * 512].rearrange("(k p) n -> p k n", p=128)), None, [])
                    wv = ring[slot][:, :].rearrange("p (k n) -> p k n", k=16)
                    for sub in range(4):
                        bank = 4 + (nb * 4 + sub) % 2
                        for dc in range(16):
                            s.op("pe", lambda h, dc=dc, sub=sub, bank=bank, wv=wv: h.matmul(
                                ps[bank][:, :], mh[:, dc, sub * 128:(sub + 1) * 128], wv[:, dc, :],
                                start=(dc == 0), stop=(dc == 15)),
                                reads=[("ring", slot), ("mh", dc)], writes=[("ps", bank)])
                        s.op("dve", lambda h, sub=sub, nb=nb, bank=bank: h.tensor_tensor(
                            xt[:, sub, nb * 512:(nb + 1) * 512], ps[bank][:, :], xt[:, sub, nb * 512:(nb + 1) * 512],
                            ALU.add),
                            reads=[("ps", bank), ("x", sub)], writes=[("x", sub)])
                def nrm(sub):
                    xs = sub % 2
                    self._norm_T(xt[:, sub, :], ("x", sub), gf, "gf", hb[xs], ("hb3", xs), st[xs], ("st3", xs),
                                 ident, psT, mh, sub, part="norm")

                def trn(sub):
                    xs = sub % 2
                    self._norm_T(xt[:, sub, :], ("x", sub), gf, "gf", hb[xs], ("hb3", xs), st[xs], ("st3", xs),
                                 ident, psT, mh, sub, part="tr")
                nrm(0)
                nrm(1)
                trn(0)
                nrm(2)
                trn(1)
                nrm(3)
                trn(2)
                trn(3)
                for fb in range(DFF // 256):
                    slot = wload(lambda h, r, fb=fb: h.dma_start(
                        out=r[:, 0:4096].rearrange("p (k n) -> p k n", k=16),
                        in_=W["WB_gate"][:, fb * 256:(fb + 1) * 256].rearrange("(k p) n -> p k n", p=128)), None, [])
                    s.op("sp", lambda h, slot=slot, fb=fb: h.dma_start(
                        out=ring[slot][:, 4096:8192].rearrange("p (k n) -> p k n", k=16),
                        in_=W["WB_up"][:, fb * 256:(fb + 1) * 256].rearrange("(k p) n -> p k n", p=128)),
                        writes=[("ringb", slot)], reads=[("ring", slot)], dma=True)
                    wg = ring[slot][:, 0:4096].rearrange("p (k n) -> p k n", k=16)
                    wu = ring[slot][:, 4096:8192].rearrange("p (k n) -> p k n", k=16)
                    for fq in range(2):
                        fc = fb * 2 + fq
                        bg = (fc % 2) * 2
                        bu = bg + 1
                        for dc in range(16):
                            s.op("pe", lambda h, dc=dc, fq=fq, bg=bg, wg=wg: h.matmul(
                                ps[bg][:, :], wg[:, dc, fq * 128:(fq + 1) * 128], mh[:, dc, :],
                                start=(dc == 0), stop=(dc == 15)),
                                reads=[("ring", slot), ("mh", dc)] + [("mhT", q) for q in range(4)], writes=[("ps", bg)])
                        for dc in range(16):
                            s.op("pe", lambda h, dc=dc, fq=fq, bu=bu, wu=wu: h.matmul(
                                ps[bu][:, :], wu[:, dc, fq * 128:(fq + 1) * 128], mh[:, dc, :],
                                start=(dc == 0), stop=(dc == 15)),
                                reads=[("ringb", slot), ("mh", dc)] + [("mhT", q) for q in range(4)], writes=[("ps", bu)])
                        tg = tmp[fc % 2]
                        s.op("act", lambda h, tg=tg, bg=bg: h.activation(tg[:, :], ps[bg][:, :], AF.Silu),
                             reads=[("ps", bg)], writes=[("tm", fc % 2)])
                        s.op("dve", lambda h, tg=tg, bu=bu, fc=fc: h.tensor_tensor(
                            act[:, fc, :], ps[bu][:, :], tg[:, :], ALU.mult),
                            reads=[("ps", bu), ("tm", fc % 2)], writes=[("act", fc)])
                pieces = [(0, 16), (16, 16), (32, 12)]
                for nb in range(4):
                    for pi, (f0, nf) in enumerate(pieces):
                        slot = wload(lambda h, r, nb=nb, f0=f0, nf=nf: h.dma_start(
                            out=r[:, 0:nf * 512].rearrange("p (k n) -> p k n", k=nf),
                            in_=W["WB_down"][f0 * 128:(f0 + nf) * 128, nb * 512:(nb + 1) * 512].rearrange(
                                "(k p) n -> p k n", p=128)), None, [])
                        wv = ring[slot][:, 0:nf * 512].rearrange("p (k n) -> p k n", k=nf)
                        for sub in range(4):
                            bank = 4 + sub
                            for k in range(nf):
                                fc = f0 + k
                                s.op("pe", lambda h, fc=fc, k=k, sub=sub, bank=bank, wv=wv: h.matmul(
                                    ps[bank][:, :], act[:, fc, sub * 128:(sub + 1) * 128], wv[:, k, :],
                                    start=(fc == 0), stop=(fc == NF - 1)),
                                    reads=[("ring", slot), ("act", fc)], writes=[("ps", bank)])
                    for sub in range(4):
                        bank = 4 + sub
                        s.op("dve", lambda h, sub=sub, nb=nb, bank=bank: h.tensor_tensor(
                            xt[:, sub, nb * 512:(nb + 1) * 512], ps[bank][:, :], xt[:, sub, nb * 512:(nb + 1) * 512],
                            ALU.add),
                            reads=[("ps", bank), ("x", sub)], writes=[("x", sub)])
                if tt + 1 < self.nt:
                    ntsl = slice((tt + 1) * T, (tt + 2) * T)
                    ab_load(ntsl)
                    sgload(0, ntsl)
                    sgload(1, ntsl)
                for sub in range(4):
                    xs = nhb % 2
                    nhb += 1
                    stt, stk = st[xs], ("st3", xs)
                    xa = xt[:, sub, :]
                    s.op("act", lambda h, xs=xs, xa=xa, stt=stt: h.activation(hb[xs][:, :], xa, AF.Square,
                                                                             accum_out=stt[:, 0:1]),
                         reads=[("x", sub)], writes=[("hb3", xs), stk])
                    s.op("act", lambda h, stt=stt: h.activation(stt[:, 1:2], stt[:, 0:1], AF.Sqrt,
                                                                bias=EPS, scale=1.0 / D),
                         reads=[stk], writes=[(stk, 1)])
                    s.op("dve", lambda h, stt=stt: h.reciprocal(stt[:, 2:3], stt[:, 1:2]),
                         reads=[(stk, 1)], writes=[(stk, 2)])
                    s.op("dve", lambda h, xa=xa, stt=stt: h.scalar_tensor_tensor(
                        xa, xa, stt[:, 2:3], gl[:, :], ALU.mult, ALU.mult),
                        reads=[("x", sub), (stk, 2), "gl"], writes=[("x", sub)])
                    t0 = tt * T + sub * 128
                    o = s.op("pool", lambda h, xa=xa, t0=t0: h.dma_start(out=self.y[t0:t0 + 128, :], in_=xa),
                             reads=[("x", sub)], writes=[("y", t0)], dma=True)
                    self.out_dmas.append(o)
            s.barrier()

    def _norm_T(self, xa, xkey, g, gkey, hbt, hbk, stt, stk, ident, psT, dstT, sub, part="both"):
        s = self.s
        if part in ("both", "norm"):
            s.op("act", lambda h: h.activation(hbt[:, :], xa, AF.Square, accum_out=stt[:, 0:1]),
                 reads=[xkey], writes=[hbk, stk])
            s.op("act", lambda h: h.activation(stt[:, 1:2], stt[:, 0:1], AF.Sqrt, bias=EPS, scale=1.0 / D),
                 reads=[stk], writes=[(stk, 1)])
            s.op("dve", lambda h: h.reciprocal(stt[:, 2:3], stt[:, 1:2]), reads=[(stk, 1)], writes=[(stk, 2)])
            s.op("dve", lambda h: h.scalar_tensor_tensor(hbt[:, :], xa, stt[:, 2:3], g[:, :], ALU.mult, ALU.mult),
                 reads=[xkey, (stk, 2), gkey], writes=[hbk])
        if part == "norm":
            return
        for half in range(2):
            for q in range(8):
                dc = half * 8 + q
                s.op("pe", lambda h, dc=dc, half=half, q=q: h.transpose(
                    psT[half][:, q * 128:(q + 1) * 128], hbt[:, dc * 128:(dc + 1) * 128], ident[:, :]),
                    reads=[hbk, "ident"], writes=[("ps", 6 + half)])
            dstv = dstT[:, half * 8:(half + 1) * 8, sub * 128:(sub + 1) * 128]
            srcv = psT[half].rearrange("p (q t) -> p q t", q=8)
            if half == 0:
                fn = lambda h, dstv=dstv, srcv=srcv: h.copy(dstv, srcv)
                eng = "act"
            else:
                fn = lambda h, dstv=dstv, srcv=srcv: h.tensor_copy(dstv, srcv)
                eng = "dve"
            s.op(eng, fn, reads=[("ps", 6 + half)], writes=[("mhT", sub)] + [("mh", d_) for d_ in range(half * 8, half * 8 + 8)])

def core_inputs(inputs, core):
    if core < 4:
        xc = np.ascontiguousarray(inputs["x_prompt"][core])
    else:
        i = core - 4
        xc = np.ascontiguousarray(inputs["x_sample"][2 * i:2 * i + 2].reshape(NTOK, D))
    m = {"x": xc.astype(np.float32, copy=False)}
    for name, shp in WEIGHT_SPECS:
        m[name] = np.ascontiguousarray(np.asarray(inputs[name], dtype=np.float32).reshape(shp))
    m.update(make_constants(core < 4))
    return m


_NC_CACHE = {}


def kernel(**inputs):
    inputs = {k: np.asarray(v) for k, v in inputs.items()}
    if "nc" not in _NC_CACHE:
        _NC_CACHE["nc"] = Builder().build(phases="0BFHCD")
    nc = _NC_CACHE["nc"]
    in_maps = [core_inputs(inputs, c) for c in range(NCORES)]
    res = run_bass_kernel_spmd(nc, in_maps, core_ids=list(range(NCORES)))
    ys = [r["y"] for r in res.results]
    y_prompt = np.stack(ys[:4], axis=0).astype(np.float32)
    y_sample = np.concatenate([y.reshape(2, 4096, D) for y in ys[4:]], axis=0).astype(np.float32)
    return (y_prompt, y_sample)
```
